# Optimizing a Trainium2 kernel written in Bass

```python
import math
import jax, jax.numpy as jnp
from jax import lax
import numpy as np

D_MODEL = 2048
BATCH = 8
SEQ = 4096
DEPTH = 4

CHUNK = 64
Q_BLOCK = 128
MLA_HEADS = 16
QK_NOPE_DIM = 128
QK_ROPE_DIM = 64
V_HEAD_DIM = 128
Q_LORA_RANK = 512
KV_LORA_RANK = 256
ROPE_THETA = 10000.0
SSM_WIDTH = D_MODEL // 2
SSM_GROUP = 16
SSM_GROUPS = SSM_WIDTH // SSM_GROUP
SSM_STATE = 64
DT_MIN = 1e-3
DT_MAX = 1e-1
FFN_HIDDEN = -(-8 * D_MODEL // (3 * 256)) * 256
OFF_KV = Q_LORA_RANK
OFF_SSM = OFF_KV + KV_LORA_RANK + QK_ROPE_DIM
OFF_GATE = OFF_SSM + SSM_WIDTH
IN_WIDTH = OFF_GATE + 2 * D_MODEL
EPS = 1e-6

kernel_name = "chunk_causal_mla_s5_hybrid"


def rms_norm(x, w):
    xf = x.astype(jnp.float32)
    y = xf * lax.rsqrt(jnp.mean(xf * xf, axis=-1, keepdims=True) + EPS)
    return (y * w.astype(jnp.float32)).astype(x.dtype)


def rope_tables(positions):
    inv_freq = ROPE_THETA ** (-jnp.arange(0, QK_ROPE_DIM, 2, dtype=jnp.float32) / QK_ROPE_DIM)
    ang = positions.astype(jnp.float32)[..., None] * inv_freq
    return jnp.cos(ang), jnp.sin(ang)


def apply_rope(x, cos, sin):
    half = x.shape[-1] // 2
    x1 = x[..., :half].astype(jnp.float32)
    x2 = x[..., half:].astype(jnp.float32)
    return jnp.concatenate([x1 * cos - x2 * sin, x1 * sin + x2 * cos], axis=-1).astype(x.dtype)


def mla(cq_raw, ckv_raw, q_norm_w, kv_norm_w, w_uq, w_ukv, w_o, cos, sin):
    B, S, _ = cq_raw.shape
    H = MLA_HEADS
    c_q = rms_norm(cq_raw, q_norm_w)
    q = jnp.einsum('bsr,re->bse', c_q, w_uq).reshape(B, S, H, QK_NOPE_DIM + QK_ROPE_DIM)
    q_nope = q[..., :QK_NOPE_DIM]
    q_pe = apply_rope(q[..., QK_NOPE_DIM:], cos[:, :, None, :], sin[:, :, None, :])
    c_kv = rms_norm(ckv_raw[..., :KV_LORA_RANK], kv_norm_w)
    k_pe = apply_rope(ckv_raw[..., KV_LORA_RANK:], cos, sin)
    kv = jnp.einsum('bsr,re->bse', c_kv, w_ukv).reshape(B, S, H, QK_NOPE_DIM + V_HEAD_DIM)
    k_nope = kv[..., :QK_NOPE_DIM]
    v = kv[..., QK_NOPE_DIM:]
    scale = (QK_NOPE_DIM + QK_ROPE_DIM) ** -0.5
    outs = []
    for q0 in range(0, S, Q_BLOCK):
        q1 = min(q0 + Q_BLOCK, S)
        kv_len = min(S, ((q1 - 1) // CHUNK + 1) * CHUNK)
        s = (jnp.einsum('bqhd,bkhd->bhqk', q_nope[:, q0:q1], k_nope[:, :kv_len],
                        preferred_element_type=jnp.float32)
             + jnp.einsum('bqhd,bkd->bhqk', q_pe[:, q0:q1], k_pe[:, :kv_len],
                          preferred_element_type=jnp.float32)) * scale
        q_chunk = jnp.arange(q0, q1) // CHUNK
        k_chunk = jnp.arange(kv_len) // CHUNK
        mask = k_chunk[None, :] <= q_chunk[:, None]
        s = jnp.where(mask, s, jnp.finfo(jnp.float32).min)
        p = jax.nn.softmax(s, axis=-1).astype(v.dtype)
        outs.append(jnp.einsum('bhqk,bkhd->bqhd', p, v[:, :kv_len]))
    o = jnp.concatenate(outs, axis=1).reshape(B, S, H * V_HEAD_DIM)
    return jnp.einsum('bse,ed->bsd', o, w_o)


def s5(u, a_re, a_im, log_dt, b_re, b_im, c_re, c_im, d_skip, w_glu, b_glu):
    B, S, _ = u.shape
    G, N, P = SSM_GROUPS, SSM_STATE, SSM_GROUP
    f32 = jnp.float32
    uf = u.astype(f32).reshape(B, S, G, P)
    a_re = a_re.astype(f32)
    a_im = a_im.astype(f32)
    dt = jnp.exp(log_dt.astype(f32))[:, None]
    mag = jnp.exp(a_re * dt)
    abar_re = mag * jnp.cos(a_im * dt)
    abar_im = mag * jnp.sin(a_im * dt)
    den = a_re * a_re + a_im * a_im
    nr = abar_re - 1.0
    f_re = (nr * a_re + abar_im * a_im) / den
    f_im = (abar_im * a_re - nr * a_im) / den
    br = b_re.astype(f32)
    bi = b_im.astype(f32)
    bb_re = f_re[..., None] * br - f_im[..., None] * bi
    bb_im = f_re[..., None] * bi + f_im[..., None] * br
    bu_re = jnp.einsum('bsgp,gnp->bsgn', uf, bb_re)
    bu_im = jnp.einsum('bsgp,gnp->bsgn', uf, bb_im)
    ar = jnp.broadcast_to(abar_re[None, None], (1, S, G, N))
    ai = jnp.broadcast_to(abar_im[None, None], (1, S, G, N))

    def combine(e1, e2):
        a1r, a1i, b1r, b1i = e1
        a2r, a2i, b2r, b2i = e2
        return (a2r * a1r - a2i * a1i,
                a2r * a1i + a2i * a1r,
                a2r * b1r - a2i * b1i + b2r,
                a2r * b1i + a2i * b1r + b2i)

    _, _, xr, xi = lax.associative_scan(combine, (ar, ai, bu_re, bu_im), axis=1)
    y = (jnp.einsum('bsgn,gpn->bsgp', xr, c_re.astype(f32))
         - jnp.einsum('bsgn,gpn->bsgp', xi, c_im.astype(f32))
         + d_skip.astype(f32).reshape(G, P) * uf).reshape(B, S, SSM_WIDTH)
    z = jax.nn.gelu(y).astype(u.dtype)
    zz = jnp.einsum('bsw,we->bse', z, w_glu) + b_glu
    return zz[..., :D_MODEL] * jax.nn.sigmoid(zz[..., D_MODEL:])


def setup_inputs(seed: int = 0) -> dict:
    key = jax.random.key(seed)
    ks = jax.random.split(key, 32)
    L, D = DEPTH, D_MODEL
    G, N, P = SSM_GROUPS, SSM_STATE, SSM_GROUP
    f32 = jnp.float32

    def nrm(k, shape, scale):
        return jax.random.normal(k, shape, f32) * scale

    def gain(k, n):
        return 1.0 + 0.05 * jax.random.normal(k, (L, n), f32)

    x = jax.random.normal(ks[0], (BATCH, SEQ, D), f32)
    offsets = jax.random.randint(ks[1], (BATCH, 1), 0, 4096, dtype=jnp.int32)
    positions = offsets + jnp.arange(SEQ, dtype=jnp.int32)[None, :]
    n_idx = jnp.arange(N, dtype=f32)
    return {
        "x": x,
        "positions": positions,
        "pre_mix_norm": gain(ks[2], D),
        "w_in": nrm(ks[3], (L, D, IN_WIDTH), D ** -0.5),
        "b_gate": nrm(ks[4], (L, 2 * D), 0.01),
        "q_norm": gain(ks[5], Q_LORA_RANK),
        "kv_norm": gain(ks[6], KV_LORA_RANK),
        "w_uq": nrm(ks[7], (L, Q_LORA_RANK, MLA_HEADS * (QK_NOPE_DIM + QK_ROPE_DIM)), Q_LORA_RANK ** -0.5),
        "w_ukv": nrm(ks[8], (L, KV_LORA_RANK, MLA_HEADS * (QK_NOPE_DIM + V_HEAD_DIM)), KV_LORA_RANK ** -0.5),
        "w_o_mla": nrm(ks[9], (L, MLA_HEADS * V_HEAD_DIM, D), (MLA_HEADS * V_HEAD_DIM) ** -0.5),
        "ssm_a_re": -0.5 + nrm(ks[10], (L, G, N), 0.01),
        "ssm_a_im": math.pi * n_idx + nrm(ks[11], (L, G, N), 0.01),
        "ssm_log_dt": jax.random.uniform(ks[12], (L, G), f32, math.log(DT_MIN), math.log(DT_MAX)),
        "ssm_b_re": nrm(ks[13], (L, G, N, P), (2 * P) ** -0.5),
        "ssm_b_im": nrm(ks[14], (L, G, N, P), (2 * P) ** -0.5),
        "ssm_c_re": nrm(ks[15], (L, G, P, N), (2 * N) ** -0.5),
        "ssm_c_im": nrm(ks[16], (L, G, P, N), (2 * N) ** -0.5),
        "ssm_d": nrm(ks[17], (L, SSM_WIDTH), 1.0),
        "w_glu": nrm(ks[18], (L, SSM_WIDTH, 2 * D), SSM_WIDTH ** -0.5),
        "b_glu": nrm(ks[19], (L, 2 * D), 0.01),
        "w_out": nrm(ks[20], (L, D, D), D ** -0.5),
        "post_mix_norm": gain(ks[21], D),
        "pre_ffn_norm": gain(ks[22], D),
        "w_ffn_gate": nrm(ks[23], (L, D, FFN_HIDDEN), D ** -0.5),
        "w_ffn_up": nrm(ks[24], (L, D, FFN_HIDDEN), D ** -0.5),
        "w_ffn_down": nrm(ks[25], (L, FFN_HIDDEN, D), FFN_HIDDEN ** -0.5),
        "post_ffn_norm": gain(ks[26], D),
    }


def reference(x, positions, pre_mix_norm, w_in, b_gate, q_norm, kv_norm, w_uq, w_ukv, w_o_mla,
              ssm_a_re, ssm_a_im, ssm_log_dt, ssm_b_re, ssm_b_im, ssm_c_re, ssm_c_im, ssm_d,
              w_glu, b_glu, w_out, post_mix_norm, pre_ffn_norm, w_ffn_gate, w_ffn_up,
              w_ffn_down, post_ffn_norm):
    cos, sin = rope_tables(positions)
    for l in range(DEPTH):
        h = rms_norm(x, pre_mix_norm[l])
        proj = jnp.einsum('bsd,de->bse', h, w_in[l])
        cq_raw = proj[..., :OFF_KV]
        ckv_raw = proj[..., OFF_KV:OFF_SSM]
        u = proj[..., OFF_SSM:OFF_GATE]
        gates = jax.nn.sigmoid((proj[..., OFF_GATE:] + b_gate[l]).astype(jnp.float32))
        a_out = mla(cq_raw, ckv_raw, q_norm[l], kv_norm[l], w_uq[l], w_ukv[l], w_o_mla[l], cos, sin)
        s_out = s5(u, ssm_a_re[l], ssm_a_im[l], ssm_log_dt[l], ssm_b_re[l], ssm_b_im[l],
                   ssm_c_re[l], ssm_c_im[l], ssm_d[l], w_glu[l], b_glu[l])
        merged = (gates[..., :D_MODEL] * a_out.astype(jnp.float32)
                  + gates[..., D_MODEL:] * s_out.astype(jnp.float32)).astype(x.dtype)
        mix = jnp.einsum('bsd,de->bse', merged, w_out[l])
        x = x + rms_norm(mix, post_mix_norm[l])
        h = rms_norm(x, pre_ffn_norm[l])
        f = jnp.einsum('bsf,fd->bsd',
                       jax.nn.silu(jnp.einsum('bsd,df->bsf', h, w_ffn_gate[l]))
                       * jnp.einsum('bsd,df->bsf', h, w_ffn_up[l]),
                       w_ffn_down[l])
        x = x + rms_norm(f, post_ffn_norm[l])
    return x
```

```python
import math
import numpy as np
import concourse.bass as bass
import concourse.mybir as mybir
from concourse.bass_utils import run_bass_kernel_spmd

F32 = mybir.dt.float32
BF16 = mybir.dt.bfloat16
I32 = mybir.dt.int32
AF = mybir.ActivationFunctionType
ALU = mybir.AluOpType

D = 2048
S_LEN = 4096
L_DEPTH = 4
TT = 1024
NT = S_LEN // TT
NH = 16
FF = 5632
IN_W = 5952
OFF_KV = 512
OFF_PE = 768
OFF_SSM = 832
OFF_GATE = 1856
EPS = 1e-6
SCALE = 192.0 ** -0.5
MAGIC = 12582912.0
TWO_PI = 2.0 * math.pi
C1 = 6.28125
C2 = TWO_PI - C1

V_PRE, V_QN, V_KVN, V_BG, V_BGLU, V_POSTMIX, V_PREF, V_POSTF, V_SD = 0, 16, 20, 22, 54, 86, 102, 118, 134
NVEC = 142
C_ID, C_PSW, C_MASK, C_SEL, C_ONES, C_SM = 0, 128, 256, 384, 384 + 2048, 384 + 2048 + 128
C_SEL2 = C_SM + 8
NCONST = C_SEL2 + 2048
P_AR, P_AI, P_LDT, P_BRE, P_BIM, P_CRE, P_CIM = 0, 64, 128, 192, 192 + 1024, 192 + 2048, 192 + 3072
NS5P = 192 + 4096


class Buf:
    __slots__ = ("name", "w", "r")

    def __init__(self, name=""):
        self.name = name
        self.w = {}
        self.r = {}


class Sched:
    ENGS = ("pe", "act", "dve", "pool", "sp")
    ROT = 30000
    RING = 20

    def __init__(self, nc):
        self.nc = nc
        self.ops = {e: [] for e in self.ENGS}
        self.known = {e: {} for e in self.ENGS}
        self.csem = {}
        self.ccnt = {}
        self._csem_idx = {}
        for e in self.ENGS:
            self._new_csem(e)
        self.ring = {}
        self.ring_val = {}
        self.ring_i = {}
        for q in ("sp", "pool"):
            self.ring[q] = [nc.alloc_semaphore(f"dq_{q}_{i}") for i in range(self.RING)]
            self.ring_val[q] = [0] * self.RING
            self.ring_i[q] = 0
        self.n_instr = {e: 0 for e in self.ENGS}

    def _new_csem(self, e):
        idx = self._csem_idx.get(e, 0)
        self._csem_idx[e] = idx + 1
        self.csem[e] = self.nc.alloc_semaphore(f"c_{e}_{idx}")
        self.ccnt[e] = 0

    def _collect(self, eng, reads, writes):
        deps = {}

        def add(t):
            key = id(t[1])
            if key not in deps or deps[key][2] < t[2]:
                deps[key] = t

        for b in reads:
            for t in b.w.values():
                add(t)
        for b in writes:
            for t in b.w.values():
                add(t)
            for t in b.r.values():
                if t[0] == eng and eng in ("pe", "act", "dve"):
                    continue
                add(t)
        waits = []
        kn = self.known[eng]
        for key, t in deps.items():
            if t[0] == eng and eng == "pe":
                continue
            if kn.get(key, 0) >= t[2]:
                continue
            kn[key] = t[2]
            waits.append((t[1], t[2]))
        return waits

    def _register(self, ticket, reads, writes):
        key = id(ticket[1])
        for b in writes:
            b.w = {key: ticket}
            b.r = {}
        for b in reads:
            b.r[key] = ticket

    def op(self, eng, fn, reads=(), writes=()):
        waits = self._collect(eng, reads, writes)
        if self.ccnt[eng] >= self.ROT:
            self._new_csem(eng)
        self.ccnt[eng] += 1
        sem = self.csem[eng]
        ticket = (eng, sem, self.ccnt[eng])
        self.ops[eng].append((waits, fn, sem, 1))
        self._register(ticket, reads, writes)
        return ticket

    def dma(self, q, out_ap, in_ap, reads=(), writes=(), **kw):
        i = self.ring_i[q]
        self.ring_i[q] = (i + 1) % self.RING
        sem = self.ring[q][i]
        prev = self.ring_val[q][i]
        waits = self._collect(q, reads, writes)
        kn = self.known[q]
        if prev > 0 and kn.get(id(sem), 0) < prev:
            kn[id(sem)] = prev
            waits.append((sem, prev))
        val = prev + 16
        self.ring_val[q][i] = val
        ticket = (q + "_dma", sem, val)

        def fn(e, out_ap=out_ap, in_ap=in_ap, kw=kw):
            return e.dma_start(out=out_ap, in_=in_ap, **kw)

        self.ops[q].append((waits, fn, sem, 16))
        self._register(ticket, reads, writes)
        return ticket

    def barrier(self):
        tickets = []
        for q in ("sp", "pool"):
            for i in range(self.RING):
                if self.ring_val[q][i] > 0:
                    tickets.append((q + "_dma", self.ring[q][i], self.ring_val[q][i]))
        for e in self.ENGS:
            if self.ccnt[e] > 0:
                tickets.append((e, self.csem[e], self.ccnt[e]))
        for e in self.ENGS:
            kn = self.known[e]
            waits = []
            for t in tickets:
                if t[0] == e and e == "pe":
                    continue
                key = id(t[1])
                if kn.get(key, 0) >= t[2]:
                    continue
                kn[key] = t[2]
                waits.append((t[1], t[2]))
            if waits:
                self.ops[e].append((waits, None, None, 0))

    def emit(self):
        nc = self.nc
        engmap = {"pe": "tensor", "act": "scalar", "dve": "vector", "pool": "gpsimd", "sp": "sync"}
        with nc.Block() as block:
            for e in self.ENGS:
                ops = self.ops[e]
                if not ops:
                    continue

                def body(eng, ops=ops, e=e):
                    n = 0
                    for (waits, fn, sem, inc) in ops:
                        for (s, v) in waits:
                            eng.wait_ge(s, v)
                            n += 1
                        if fn is not None:
                            ins = fn(eng)
                            ins.then_inc(sem, inc)
                            n += 1
                    self.n_instr[e] = n

                getattr(block, engmap[e])(body)


class SBAlloc:
    BASE = 16512
    CAP = 196608

    def __init__(self, nc):
        self.nc = nc
        self.top = self.BASE
        self.n = 0

    def mark(self):
        return self.top

    def reset(self, m):
        self.top = m

    def alloc(self, shape, dtype):
        nb = 4 if dtype in (F32, I32) else 2
        n = 1
        for s in shape[1:]:
            n *= s
        off = (self.top + 63) // 64 * 64
        self.top = off + n * nb
        assert self.top <= self.CAP, f"SBUF overflow {self.top}"
        self.n += 1
        self.last_off = off
        return self.nc.alloc_sbuf_tensor_at(f"sb{self.n}", list(shape), dtype, offset=off)

    def at(self, shape, dtype, off):
        self.n += 1
        return self.nc.alloc_sbuf_tensor_at(f"sb{self.n}", list(shape), dtype, offset=off)


class Rot:
    def __init__(self, sb, n, shape, dtype, name="r"):
        self.items = [(sb.alloc(shape, dtype), Buf(f"{name}{i}")) for i in range(n)]
        self.i = 0

    def next(self):
        it = self.items[self.i]
        self.i = (self.i + 1) % len(self.items)
        return it


def build(n_layers=L_DEPTH, dump=False, phases="ABCDE"):
    nc = bass.Bass("TRN2", target_bir_lowering=False)
    S = Sched(nc)
    sb = SBAlloc(nc)

    def din(name, shape, dt=F32):
        return nc.dram_tensor(name, list(shape), dt, kind="ExternalInput").ap()

    def dscr(name, shape, dt):
        kind = "ExternalOutput" if dump else "Internal"
        return nc.dram_tensor(name, list(shape), dt, kind=kind).ap()

    xT_in = din("xT", [D, S_LEN])
    pos_in = din("pos", [S_LEN], I32)
    w_in = din("w_in", [n_layers, D, IN_W])
    w_uq = din("w_uq", [n_layers, 512, 3072])
    w_ukv = din("w_ukv", [n_layers, 256, 4096])
    w_o = din("w_o_mla", [n_layers, D, D])
    w_glu = din("w_glu", [n_layers, 1024, 4096])
    w_out = din("w_out", [n_layers, D, D])
    w_fg = din("w_ffn_gate", [n_layers, D, FF])
    w_fu = din("w_ffn_up", [n_layers, D, FF])
    w_fd = din("w_ffn_down", [n_layers, FF, D])
    vecs_in = din("vecs", [n_layers, 128, NVEC])
    s5p_in = din("s5p", [n_layers, 128, NS5P])
    consts_in = din("consts", [128, NCONST])
    outT = nc.dram_tensor("outT", [D, S_LEN], F32, kind="ExternalOutput").ap()

    XT = dscr("XT", [D, S_LEN], F32)
    YT = dscr("YT", [D, S_LEN], F32)
    QT = dscr("QT", [NH, 192, S_LEN], BF16)
    KT = dscr("KT", [NH, 128, S_LEN], BF16)
    KPE = dscr("KPE", [64, S_LEN], BF16)
    VS = dscr("VS", [S_LEN, D], BF16)
    UT = dscr("UT", [1024, S_LEN], BF16)
    GT = dscr("GT", [4096, S_LEN], BF16)
    OT = dscr("OT", [D, S_LEN], BF16)
    ZT = dscr("ZT", [1024, S_LEN], BF16)
    CS = dscr("CS", [2, 64, S_LEN], F32)

    ps = [nc.alloc_psum_tensor(f"ps{i}", [128, 512], F32) for i in range(8)]
    pb = [Buf(f"pb{i}") for i in range(8)]
    bank_i = [0]

    def bank(n=4):
        b = bank_i[0] % n
        bank_i[0] += 1
        return b

    def mm(out_ap, pairs, reads, writes, start=True, stop=True):
        def fn(e):
            n = len(pairs)
            ins = None
            for i, (lh, rh) in enumerate(pairs):
                ins = e.matmul(out_ap, lhsT=lh, rhs=rh, start=(start and i == 0), stop=(stop and i == n - 1))
            return ins
        return S.op("pe", fn, reads, writes)

    def act(out, in_, func, reads, writes, **kw):
        return S.op("act", lambda e: e.activation(out=out, in_=in_, func=func, **kw), reads, writes)

    def tt(out, a, b, op, reads, writes, eng="dve"):
        return S.op(eng, lambda e: e.tensor_tensor(out=out, in0=a, in1=b, op=op), reads, writes)

    def ts(out, a, s1, s2, op0, op1, reads, writes, eng="dve"):
        if op1 is None:
            return S.op(eng, lambda e: e.tensor_scalar(out=out, in0=a, scalar1=s1, scalar2=None, op0=op0), reads, writes)
        return S.op(eng, lambda e: e.tensor_scalar(out=out, in0=a, scalar1=s1, scalar2=s2, op0=op0, op1=op1), reads, writes)

    def stt(out, a, sc, b, op0, op1, reads, writes):
        return S.op("dve", lambda e: e.scalar_tensor_tensor(out=out, in0=a, scalar=sc, in1=b, op0=op0, op1=op1), reads, writes)

    def recip(out, in_, reads, writes):
        return S.op("dve", lambda e: e.reciprocal(out=out, in_=in_), reads, writes)

    def wload(wt, bwt, W2d, c0, ncols, kcn, col_dst=0):
        src = W2d[:, c0:c0 + ncols].rearrange("(kc p) n -> p kc n", p=128)
        S.dma("pool", wt[:, 0:kcn, col_dst:col_dst + ncols], src, writes=[bwt])

    cst = sb.alloc([128, C_SEL], F32)
    csm = sb.alloc([128, 8], F32)
    onesb = sb.alloc([128, 128], BF16)
    bcst = Buf("cst")
    S.dma("sp", cst[:], consts_in[:, 0:C_SEL], writes=[bcst])
    S.dma("sp", csm[:], consts_in[:, C_SM:C_SM + 8], writes=[bcst])
    S.dma("pool", onesb[:], consts_in[:, C_ONES:C_ONES + 128], writes=[bcst])

    ident = cst[:, C_ID:C_ID + 128]
    pswap = cst[:, C_PSW:C_PSW + 128]
    maskT = cst[:, C_MASK:C_MASK + 128]
    SELRE, SELIM, NSELRE, NSELIM, SGNROPE, INVF = 0, 1, 2, 3, 4, 5
    vecs = sb.alloc([128, NVEC], F32)
    bvec = Buf("vecs")
    persist_mark = sb.mark()

    def rope_tables():
        m = sb.mark()
        posi = sb.alloc([64, S_LEN], I32)
        ang = sb.alloc([64, S_LEN], F32)
        t1 = sb.alloc([64, S_LEN], F32)
        t2 = sb.alloc([64, S_LEN], F32)
        bp, ba, b1, b2 = Buf(), Buf(), Buf(), Buf()
        S.dma("sp", posi[:], pos_in.partition_broadcast(64), writes=[bp])
        S.op("dve", lambda e: e.tensor_copy(out=ang[:], in_=posi[:]), [bp], [ba])
        ts(ang[:], ang[:], csm[0:64, INVF:INVF + 1], None, ALU.mult, None, [ba, bcst], [ba])

        def reduce_sin(shift, dst_idx, signed):
            if shift != 0.0:
                ts(t1[:], ang[:], shift, None, ALU.add, None, [ba], [b1])
                srcang, bsrc = t1, b1
            else:
                srcang, bsrc = ang, ba
            ts(t2[:], srcang[:], 1.0 / TWO_PI, MAGIC, ALU.mult, ALU.add, [bsrc], [b2])
            ts(t2[:], t2[:], MAGIC, None, ALU.subtract, None, [b2], [b2])
            if shift == 0.0:
                stt(t1[:], t2[:], -C1, ang[:], ALU.mult, ALU.add, [b2, ba], [b1])
            else:
                stt(t1[:], t2[:], -C1, t1[:], ALU.mult, ALU.add, [b2, b1], [b1])
            stt(t1[:], t2[:], -C2, t1[:], ALU.mult, ALU.add, [b2, b1], [b1])
            ts(t1[:], t1[:], math.pi, -math.pi, ALU.min, ALU.max, [b1], [b1])
            act(t2[:], t1[:], AF.Sin, [b1], [b2])
            if signed:
                ts(t2[:], t2[:], csm[0:64, SGNROPE:SGNROPE + 1], None, ALU.mult, None, [b2, bcst], [b2])
            S.dma("sp", CS[dst_idx], t2[:], reads=[b2])

        reduce_sin(math.pi / 2.0, 0, False)
        reduce_sin(0.0, 1, True)
        S.barrier()
        sb.reset(m)

    def rstd_from(banks, out_tile, bout, nfeat, extra_scale=1.0):
        es2 = extra_scale * extra_scale
        for sub, bk in enumerate(banks):
            act(out_tile[:, sub * 512:(sub + 1) * 512], ps[bk][:, :], AF.Sqrt, [pb[bk]], [bout],
                scale=1.0 / (nfeat * es2), bias=EPS / es2)
        recip(out_tile[:], out_tile[:], [bout], [bout])

    def make_hT(src, t0, vcol, hT, bh, xrot, sqrot, rstd, brstd):
        for kc in range(16):
            xc, bx = xrot.next()
            S.dma("sp", xc[:], src[kc * 128:(kc + 1) * 128, t0:t0 + TT], writes=[bx])
            for sub in range(2):
                sq, bq = sqrot.next()
                act(sq[:], xc[:, sub * 512:(sub + 1) * 512], AF.Square, [bx], [bq])
                mm(ps[6 + sub][:, :], [(onesb[:], sq[:])], [bq, bcst], [pb[6 + sub]], start=(kc == 0), stop=(kc == 15))
        rstd_from([6, 7], rstd, brstd, D)
        for kc in range(16):
            xc, bx = xrot.next()
            S.dma("sp", xc[:], src[kc * 128:(kc + 1) * 128, t0:t0 + TT], writes=[bx])
            stt(hT[:, kc, :], xc[:], vecs[:, vcol + kc:vcol + kc + 1], rstd[:], ALU.mult, ALU.mult,
                [bx, brstd, bvec], [bh[kc]])

    def postnorm_steps(t0, ysrc, byt, xsrc, dst, vcol, rstd, brstd, yrot, xrot):
        return [(lambda n=n: postnorm_one(n, t0, ysrc, byt, xsrc, dst, vcol, rstd, brstd, yrot, xrot)) for n in range(16)]

    def postnorm_one(n, t0, ysrc, byt, xsrc, dst, vcol, rstd, brstd, yrot, xrot):
        if True:
            yc, by = yrot.next()
            xc, bx = xrot.next()
            S.dma("sp", yc[:], ysrc[n * 128:(n + 1) * 128, t0:t0 + TT], reads=[byt[n]], writes=[by])
            S.dma("sp", xc[:], xsrc[n * 128:(n + 1) * 128, t0:t0 + TT], writes=[bx])
            stt(yc[:], yc[:], vecs[:, vcol + n:vcol + n + 1], rstd[:], ALU.mult, ALU.mult, [by, brstd, bvec], [by])
            tt(yc[:], yc[:], xc[:], ALU.add, [by, bx], [by])
            S.dma("sp", dst[n * 128:(n + 1) * 128, t0:t0 + TT], yc[:], reads=[by])

    def phase_A(l):
        m = sb.mark()
        src = xT_in if l == 0 else XT
        wuq = sb.alloc([128, 4, 3072], BF16)
        wuqr = sb.alloc([128, 4, 16, 64], BF16)
        wukv = sb.alloc([128, 2, 4096], BF16)
        bw = Buf("wres")
        for c in range(3):
            wload(wuq, bw, w_uq[l], c * 1024, 1024, 4, col_dst=c * 1024)
        for c in range(4):
            wload(wukv, bw, w_ukv[l], c * 1024, 1024, 2, col_dst=c * 1024)
        uq4 = w_uq[l].rearrange("(kc p) (h e) -> kc p h e", p=128, e=192)
        for kc in range(4):
            S.dma("pool", wuqr[:, kc, :, 0:32], uq4[kc][:, :, 160:192], writes=[bw])
            S.dma("pool", wuqr[:, kc, :, 32:64], uq4[kc][:, :, 128:160], writes=[bw])
        hT = sb.alloc([128, 16, TT], BF16)
        bh = [Buf(f"h{k}") for k in range(16)]
        xrot = Rot(sb, 3, [128, TT], F32, "x")
        sqrot = Rot(sb, 3, [128, 512], BF16, "sq")
        rstd = sb.alloc([128, TT], F32)
        brstd = Buf("rstd")
        rstdq = sb.alloc([128, TT], F32)
        brq = Buf("rstdq")
        rstdkv = sb.alloc([128, TT], F32)
        brkv = Buf("rstdkv")
        rkt = sb.alloc([128, 8], F32)
        brkt = Buf("rkt")
        cqw = sb.alloc([128, 4, TT], BF16)
        bcq = Buf("cqw")
        ckvw = sb.alloc([128, 2, TT], BF16)
        bckv = Buf("ckvw")
        sqkv = sb.alloc([128, 2, TT], BF16)
        bsqkv = Buf("sqkv")
        cst_ = sb.alloc([64, 2, TT], F32)
        bcs = Buf("cs")
        wrot = Rot(sb, 3, [128, 16, 128], BF16, "w")
        strot = Rot(sb, 4, [128, 512], BF16, "st")
        tmrot = Rot(sb, 4, [64, 512], F32, "tm")

        def proj(c0, M, evac, rot_cols=None):
            wt, bwt = wrot.next()
            if rot_cols is None:
                wload(wt, bwt, w_in[l], c0, M, 16)
            else:
                wload(wt, bwt, w_in[l], rot_cols[0], 32, 16, col_dst=0)
                wload(wt, bwt, w_in[l], rot_cols[1], 32, 16, col_dst=32)
            for sub in range(2):
                b = bank()
                mm(ps[b][0:M, :], [(wt[:, kc, 0:M], hT[:, kc, sub * 512:(sub + 1) * 512]) for kc in range(16)],
                   [bwt] + bh, [pb[b]])
                evac(b, sub)

        for tti in range(NT):
            t0 = tti * TT
            S.dma("sp", cst_[:, 0, :], CS[0][:, t0:t0 + TT], writes=[bcs])
            S.dma("sp", cst_[:, 1, :], CS[1][:, t0:t0 + TT], writes=[bcs])
            make_hT(src, t0, V_PRE, hT, bh, xrot, sqrot, rstd, brstd)

            for j in range(4):
                def ev(b, sub, j=j):
                    sq, bq = sqrot.next()
                    act(sq[:], ps[b][:, :], AF.Square, [pb[b]], [bq])
                    act(cqw[:, j, sub * 512:(sub + 1) * 512], ps[b][:, :], AF.Copy, [pb[b], bvec], [bcq],
                        scale=vecs[:, V_QN + j:V_QN + j + 1])
                    mm(ps[6 + sub][:, :], [(onesb[:], sq[:])], [bq, bcst], [pb[6 + sub]], start=(j == 0), stop=(j == 3))
                proj(j * 128, 128, ev)
            rstd_from([6, 7], rstdq, brq, 512, extra_scale=SCALE)
            for j in range(2):
                def ev(b, sub, j=j):
                    act(sqkv[:, j, sub * 512:(sub + 1) * 512], ps[b][:, :], AF.Square, [pb[b]], [bsqkv])
                    act(ckvw[:, j, sub * 512:(sub + 1) * 512], ps[b][:, :], AF.Copy, [pb[b], bvec], [bckv],
                        scale=vecs[:, V_KVN + j:V_KVN + j + 1])
                    mm(ps[4 + sub][:, :], [(onesb[:], sqkv[:, j, sub * 512:(sub + 1) * 512])], [bsqkv, bcst], [pb[4 + sub]],
                       start=(j == 0), stop=(j == 1))
                proj(OFF_KV + j * 128, 128, ev)
            rstd_from([4, 5], rstdkv, brkv, 256)
            b = bank()
            for blk in range(8):
                mm(ps[b][:, blk:blk + 1], [(sqkv[:, j, blk * 128:(blk + 1) * 128], onesb[:, 0:1]) for j in range(2)],
                   [bsqkv, bcst], [pb[b]])
            act(rkt[:], ps[b][:, 0:8], AF.Sqrt, [pb[b]], [brkt], scale=1.0 / 256.0, bias=EPS)
            recip(rkt[:], rkt[:], [brkt], [brkt])
            pe_hold = {}

            def ev_pe(b, sub):
                ta, bta = tmrot.next()
                tt(ta[:], ps[b][0:64, :], cst_[:, 0, sub * 512:(sub + 1) * 512], ALU.mult, [pb[b], bcs], [bta])
                pe_hold[sub] = (ta, bta)

            def ev_rot(b, sub):
                ta, bta = pe_hold[sub]
                tb, btb = tmrot.next()
                tt(tb[:], ps[b][0:64, :], cst_[:, 1, sub * 512:(sub + 1) * 512], ALU.mult, [pb[b], bcs], [btb])
                st, bst = strot.next()
                tt(st[0:64, :], ta[:], tb[:], ALU.add, [bta, btb], [bst])
                S.dma("sp", KPE[:, t0 + sub * 512:t0 + (sub + 1) * 512], st[0:64, :], reads=[bst])
            proj(OFF_PE, 64, ev_pe)
            proj(OFF_PE, 64, ev_rot, rot_cols=(OFF_PE + 32, OFF_PE))
            for j in range(8):
                def ev(b, sub, j=j):
                    st, bst = strot.next()
                    act(st[:], ps[b][:, :], AF.Copy, [pb[b]], [bst])
                    S.dma("sp", UT[j * 128:(j + 1) * 128, t0 + sub * 512:t0 + (sub + 1) * 512], st[:], reads=[bst])
                proj(OFF_SSM + j * 128, 128, ev)
            for j in range(32):
                def ev(b, sub, j=j):
                    st, bst = strot.next()
                    act(st[:], ps[b][:, :], AF.Sigmoid, [pb[b], bvec], [bst], bias=vecs[:, V_BG + j:V_BG + j + 1])
                    S.dma("sp", GT[j * 128:(j + 1) * 128, t0 + sub * 512:t0 + (sub + 1) * 512], st[:], reads=[bst])
                proj(OFF_GATE + j * 128, 128, ev)
            for h in range(NH):
                for sub in range(2):
                    cs_ = slice(sub * 512, (sub + 1) * 512)
                    tok = slice(t0 + sub * 512, t0 + (sub + 1) * 512)
                    b = bank()
                    mm(ps[b][:, :], [(wuq[:, kc, h * 192:h * 192 + 128], cqw[:, kc, cs_]) for kc in range(4)], [bw, bcq], [pb[b]])
                    st, bst = strot.next()
                    tt(st[:], ps[b][:, :], rstdq[:, cs_], ALU.mult, [pb[b], brq], [bst])
                    S.dma("sp", QT[h, 0:128, tok], st[:], reads=[bst])
                    b1 = bank()
                    mm(ps[b1][0:64, :], [(wuq[:, kc, h * 192 + 128:h * 192 + 192], cqw[:, kc, cs_]) for kc in range(4)], [bw, bcq], [pb[b1]])
                    b2 = bank()
                    mm(ps[b2][0:64, :], [(wuqr[:, kc, h, :], cqw[:, kc, cs_]) for kc in range(4)], [bw, bcq], [pb[b2]])
                    ta, bta = tmrot.next()
                    tb, btb = tmrot.next()
                    tt(ta[:], ps[b1][0:64, :], cst_[:, 0, cs_], ALU.mult, [pb[b1], bcs], [bta])
                    tt(tb[:], ps[b2][0:64, :], cst_[:, 1, cs_], ALU.mult, [pb[b2], bcs], [btb])
                    tt(ta[:], ta[:], tb[:], ALU.add, [bta, btb], [bta])
                    st2, bst2 = strot.next()
                    tt(st2[0:64, :], ta[:], rstdq[0:64, cs_], ALU.mult, [bta, brq], [bst2])
                    S.dma("sp", QT[h, 128:192, tok], st2[0:64, :], reads=[bst2])
            for h in range(NH):
                for sub in range(2):
                    cs_ = slice(sub * 512, (sub + 1) * 512)
                    tok = slice(t0 + sub * 512, t0 + (sub + 1) * 512)
                    b = bank()
                    mm(ps[b][:, :], [(wukv[:, kc, h * 256:h * 256 + 128], ckvw[:, kc, cs_]) for kc in range(2)], [bw, bckv], [pb[b]])
                    st, bst = strot.next()
                    tt(st[:], ps[b][:, :], rstdkv[:, cs_], ALU.mult, [pb[b], brkv], [bst])
                    S.dma("sp", KT[h, :, tok], st[:], reads=[bst])
            wv = wukv[:, :, :].rearrange("p k (h e) -> p k h e", e=256)
            for blk in range(8):
                for cg in range(4):
                    b = bank()
                    mm(ps[b][:, :].rearrange("p (h e) -> p h e", e=128),
                       [(ckvw[:, kc, blk * 128:(blk + 1) * 128], wv[:, kc, 4 * cg:4 * cg + 4, 128:256]) for kc in range(2)],
                       [bw, bckv], [pb[b]])
                    st, bst = strot.next()
                    act(st[:], ps[b][:, :], AF.Copy, [pb[b], brkt], [bst], scale=rkt[:, blk:blk + 1])
                    S.dma("sp", VS[t0 + blk * 128:t0 + (blk + 1) * 128, cg * 512:(cg + 1) * 512], st[:], reads=[bst])
        S.barrier()
        sb.reset(m)

    def phase_B(l):
        m = sb.mark()
        kpe = sb.alloc([64, S_LEN], BF16)
        bkpe = Buf("kpe")
        S.dma("sp", kpe[:], KPE[:, :], writes=[bkpe])
        hb = []
        for i in range(2):
            hb.append(dict(
                qn=sb.alloc([128, S_LEN], BF16), qp=sb.alloc([64, S_LEN], BF16), kn=sb.alloc([128, S_LEN], BF16),
                v=sb.alloc([128, 32, 128], BF16), o=sb.alloc([128, S_LEN], BF16),
                bin=Buf(f"hin{i}"), bo=Buf(f"ho{i}")))
        ptrot = Rot(sb, 4, [128, 512], BF16, "pt")
        rcrot = Rot(sb, 2, [128, 512], F32, "rc")
        st_i = [0]
        for h in range(NH):
            B = hb[h % 2]
            S.dma("sp", B["qn"][:], QT[h, 0:128, :], writes=[B["bin"]])
            S.dma("sp", B["qp"][:], QT[h, 128:192, :], writes=[B["bin"]])
            S.dma("sp", B["kn"][:], KT[h, :, :], writes=[B["bin"]])
            S.dma("sp", B["v"][:], VS[:, h * 128:(h + 1) * 128].rearrange("(b p) e -> p b e", p=128), writes=[B["bin"]])
            pairs = []
            for qt in range(8):
                nkb = 4 * qt + 4
                for kb in range(nkb):
                    pairs.append((qt, kb, nkb))
            held = {}

            def emit_qk(i):
                qt, kb, nkb = pairs[i]
                d = kb - 4 * qt
                c0 = 0 if d < 0 else d * 128
                qs = slice(qt * 512 + c0, (qt + 1) * 512)
                ks = slice(kb * 128, (kb + 1) * 128)
                sbk = st_i[0] % 2
                st_i[0] += 1
                mm(ps[sbk][:, c0:512], [(B["kn"][:, ks], B["qn"][:, qs]), (kpe[:, ks], B["qp"][:, qs])],
                   [B["bin"], bkpe], [pb[sbk]])
                pt, bpt = ptrot.next()
                act(pt[:, c0:512], ps[sbk][:, c0:512], AF.Exp, [pb[sbk]], [bpt])
                if d >= 0:
                    S.op("pool", lambda e, pt=pt, c0=c0: e.memset(pt[64:128, c0:c0 + 64], 0.0), [], [bpt])
                held[i] = (pt, bpt, c0)

            def emit_pv(i):
                qt, kb, nkb = pairs[i]
                pt, bpt, c0 = held.pop(i)
                ob = 2 + (qt % 2)
                lb = 4 + (qt % 2)
                mm(ps[ob][:, c0:512], [(B["v"][:, kb, :], pt[:, c0:512])], [B["bin"], bpt], [pb[ob]],
                   start=(kb == 0), stop=(kb == nkb - 1))
                mm(ps[lb][:, c0:512], [(onesb[:], pt[:, c0:512])], [bcst, bpt], [pb[lb]],
                   start=(kb == 0), stop=(kb == nkb - 1))
                if kb == nkb - 1:
                    rc, brc = rcrot.next()
                    recip(rc[:], ps[lb][:, :], [pb[lb]], [brc])
                    tt(B["o"][:, qt * 512:(qt + 1) * 512], ps[ob][:, :], rc[:], ALU.mult, [pb[ob], brc], [B["bo"]])

            for i in range(len(pairs) + 1):
                if i < len(pairs):
                    emit_qk(i)
                if i >= 1:
                    emit_pv(i - 1)
            S.dma("sp", OT[h * 128:(h + 1) * 128, :], B["o"][:], reads=[B["bo"]])
        S.barrier()
        sb.reset(m)

    def phase_C(l):
        m = sb.mark()
        selb = sb.alloc([128, 2048], BF16)
        selb2 = sb.alloc([128, 2048], BF16)
        bsel = Buf("sel")
        for (t_, c_) in ((selb, C_SEL), (selb2, C_SEL2)):
            S.dma("pool", t_[:, 0:1024], consts_in[:, c_:c_ + 1024], writes=[bsel])
            S.dma("pool", t_[:, 1024:2048], consts_in[:, c_ + 1024:c_ + 2048], writes=[bsel])

        def selrows(q4):
            if q4 < 3:
                return selb, slice(32 * q4, 32 * q4 + 32)
            return selb2, slice(64, 128)
        prm = sb.alloc([128, 192], F32)
        bre = sb.alloc([128, 64, 16], F32)
        bim = sb.alloc([128, 64, 16], F32)
        cre = sb.alloc([128, 64, 16], F32)
        cim = sb.alloc([128, 64, 16], F32)
        bprm = Buf("prm")
        S.dma("sp", prm[:], s5p_in[l][:, 0:192], writes=[bprm])
        for t_, off in ((bre, P_BRE), (bim, P_BIM), (cre, P_CRE), (cim, P_CIM)):
            S.dma("sp", t_[:].rearrange("p g q -> p (g q)"), s5p_in[l][:, off:off + 1024], writes=[bprm])
        NW = 40
        wk = sb.alloc([128, NW, 64], F32)
        bwk = Buf("wk")
        W = lambda i: wk[:, i, :]
        R = [bwk, bprm, bcst]
        ar, ai, ldt = prm[:, P_AR:P_AR + 64], prm[:, P_AI:P_AI + 64], prm[:, P_LDT:P_LDT + 64]
        DT, ARD, AID, MAG, T1, T2, SINP, COSP, ABR, ABI, DEN, NR, FRE, FIM, IR, II = range(16)
        act(W(DT), ldt, AF.Exp, R, [bwk])
        tt(W(ARD), ar, W(DT), ALU.mult, R, [bwk])
        tt(W(AID), ai, W(DT), ALU.mult, R, [bwk])
        act(W(MAG), W(ARD), AF.Exp, R, [bwk])

        def red_sin(dst, shift):
            ts(W(T1), W(AID), shift, None, ALU.add, None, R, [bwk])
            ts(W(T2), W(T1), 1.0 / TWO_PI, MAGIC, ALU.mult, ALU.add, R, [bwk])
            ts(W(T2), W(T2), MAGIC, None, ALU.subtract, None, R, [bwk])
            stt(W(T1), W(T2), -C1, W(T1), ALU.mult, ALU.add, R, [bwk])
            stt(W(T1), W(T2), -C2, W(T1), ALU.mult, ALU.add, R, [bwk])
            ts(W(T1), W(T1), math.pi, -math.pi, ALU.min, ALU.max, R, [bwk])
            act(W(dst), W(T1), AF.Sin, R, [bwk])
        red_sin(SINP, 0.0)
        red_sin(COSP, math.pi / 2.0)
        tt(W(ABR), W(MAG), W(COSP), ALU.mult, R, [bwk])
        tt(W(ABI), W(MAG), W(SINP), ALU.mult, R, [bwk])
        tt(W(T1), ar, ar, ALU.mult, R, [bwk])
        tt(W(T2), ai, ai, ALU.mult, R, [bwk])
        tt(W(DEN), W(T1), W(T2), ALU.add, R, [bwk])
        recip(W(DEN), W(DEN), R, [bwk])
        ts(W(NR), W(ABR), -1.0, None, ALU.add, None, R, [bwk])
        tt(W(T1), W(NR), ar, ALU.mult, R, [bwk])
        tt(W(T2), W(ABI), ai, ALU.mult, R, [bwk])
        tt(W(T1), W(T1), W(T2), ALU.add, R, [bwk])
        tt(W(FRE), W(T1), W(DEN), ALU.mult, R, [bwk])
        tt(W(T1), W(ABI), ar, ALU.mult, R, [bwk])
        tt(W(T2), W(NR), ai, ALU.mult, R, [bwk])
        tt(W(T1), W(T1), W(T2), ALU.subtract, R, [bwk])
        tt(W(FIM), W(T1), W(DEN), ALU.mult, R, [bwk])
        tt(W(T1), W(ABR), W(ABR), ALU.mult, R, [bwk])
        tt(W(T2), W(ABI), W(ABI), ALU.mult, R, [bwk])
        tt(W(T1), W(T1), W(T2), ALU.add, R, [bwk])
        recip(W(T1), W(T1), R, [bwk])
        tt(W(IR), W(ABR), W(T1), ALU.mult, R, [bwk])
        tt(W(T2), W(ABI), W(T1), ALU.mult, R, [bwk])
        ts(W(II), W(T2), -1.0, None, ALU.mult, None, R, [bwk])
        pw = sb.alloc([128, 16, 2, 64], F32)
        bpw = Buf("pw")
        RP = [bwk, bpw, bcst]
        PR = lambda m_: pw[:, m_ + 7, 0, :]
        PI = lambda m_: pw[:, m_ + 7, 1, :]
        S.op("dve", lambda e: e.memset(PR(0), 1.0), RP, [bpw])
        S.op("dve", lambda e: e.memset(PI(0), 0.0), RP, [bpw])

        def cmul(dr, di, xr, xi, yr, yi):
            tt(W(T1), xr, yr, ALU.mult, RP, [bwk])
            tt(W(T2), xi, yi, ALU.mult, RP, [bwk])
            tt(W(16), xr, yi, ALU.mult, RP, [bwk])
            tt(W(17), xi, yr, ALU.mult, RP, [bwk])
            tt(dr, W(T1), W(T2), ALU.subtract, RP, [bpw])
            tt(di, W(16), W(17), ALU.add, RP, [bpw])
        for m_ in range(1, 9):
            cmul(PR(m_), PI(m_), PR(m_ - 1), PI(m_ - 1), W(ABR), W(ABI))
        for m_ in range(-1, -8, -1):
            cmul(PR(m_), PI(m_), PR(m_ + 1), PI(m_ + 1), W(IR), W(II))
        dp = sb.alloc([128, 9, 2, 64], F32)
        S.op("dve", lambda e: e.tensor_copy(out=dp[:, 0, 0, :], in_=PR(8)), RP, [bpw])
        S.op("dve", lambda e: e.tensor_copy(out=dp[:, 0, 1, :], in_=PI(8)), RP, [bpw])
        for k in range(1, 9):
            cmul(dp[:, k, 0, :], dp[:, k, 1, :], dp[:, k - 1, 0, :], dp[:, k - 1, 1, :], dp[:, k - 1, 0, :], dp[:, k - 1, 1, :])
        dps = sb.alloc([128, 9, 64], F32)
        sg = sb.alloc([128, 1], F32)
        tt(sg[:], csm[:, SELRE:SELRE + 1], csm[:, SELIM:SELIM + 1], ALU.subtract, RP, [bpw])
        for k in range(9):
            ts(dps[:, k, :], dp[:, k, 1, :], sg[:, 0:1], None, ALU.mult, None, RP, [bpw])
        E1 = sb.alloc([128, 9, 64], F32)
        E2 = sb.alloc([128, 9, 64], F32)
        for m_ in range(9):
            ts(W(T1), PR(m_), csm[:, SELRE:SELRE + 1], None, ALU.mult, None, RP, [bwk])
            stt(E1[:, m_, :], PI(m_), csm[:, NSELIM:NSELIM + 1], W(T1), ALU.mult, ALU.add, RP, [bpw])
            ts(W(T1), PI(m_), csm[:, NSELRE:NSELRE + 1], None, ALU.mult, None, RP, [bwk])
            stt(E2[:, m_, :], PR(m_), csm[:, NSELIM:NSELIM + 1], W(T1), ALU.mult, ALU.add, RP, [bpw])
        B1 = sb.alloc([128, 64, 16], F32)
        B2 = sb.alloc([128, 64, 16], F32)
        m_alias = sb.mark()
        bbr = sb.alloc([128, 64, 16], F32)
        bbi = sb.alloc([128, 64, 16], F32)
        tq = sb.alloc([128, 64, 16], F32)
        bbb = Buf("bb")
        RB = [bwk, bpw, bprm, bbb, bcst]
        fre_b = W(FRE).unsqueeze(2).broadcast_to([128, 64, 16])
        fim_b = W(FIM).unsqueeze(2).broadcast_to([128, 64, 16])
        tt(bbr[:], bre[:], fre_b, ALU.mult, RB, [bbb])
        tt(tq[:], bim[:], fim_b, ALU.mult, RB, [bbb])
        tt(bbr[:], bbr[:], tq[:], ALU.subtract, RB, [bbb])
        tt(bbi[:], bim[:], fre_b, ALU.mult, RB, [bbb])
        tt(tq[:], bre[:], fim_b, ALU.mult, RB, [bbb])
        tt(bbi[:], bbi[:], tq[:], ALU.add, RB, [bbb])
        f2 = lambda t_: t_[:].rearrange("p g q -> p (g q)")
        ts(f2(tq), f2(bbr), csm[:, SELRE:SELRE + 1], None, ALU.mult, None, RB, [bbb])
        stt(f2(B1), f2(bbi), csm[:, SELIM:SELIM + 1], f2(tq), ALU.mult, ALU.add, RB, [bbb])
        ts(f2(tq), f2(bbi), csm[:, NSELRE:NSELRE + 1], None, ALU.mult, None, RB, [bbb])
        stt(f2(B2), f2(bbr), csm[:, SELIM:SELIM + 1], f2(tq), ALU.mult, ALU.add, RB, [bbb])

        sb.reset(m_alias)
        LM = sb.alloc([128, 8, 8, 16], F32)
        WTt = sb.alloc([128, 8, 8, 16], F32)
        RR = sb.alloc([128, 8, 8, 16], F32)
        GGb = sb.alloc([128, 8, 8, 16], BF16)
        tq2 = sb.alloc([128, 8, 16], F32)
        tq3 = sb.alloc([128, 8, 16], F32)
        bbig = Buf("big")
        Tg = sb.alloc([128, 8, 128], BF16)
        Wg = sb.alloc([128, 8, 128], BF16)
        Mk = sb.alloc([128, 8, 9, 128], BF16)
        tmpM = sb.alloc([128, 128], F32)
        bmat = Buf("mat")
        uT = sb.alloc([128, S_LEN], BF16)
        buT = Buf("uT")
        yT = sb.alloc([128, S_LEN], F32)
        byT = Buf("yT")
        zT = sb.alloc([128, S_LEN], BF16)
        bzT = Buf("zT")
        gt1 = sb.alloc([128, S_LEN // 2], F32)
        bg1 = Buf("g1")
        Uf = sb.alloc([128, 8, 512], BF16)
        Xs = sb.alloc([128, 8, 512], BF16)
        Yf = sb.alloc([128, 8, 512], BF16)
        bUf = [Buf(f"uf{g}") for g in range(8)]
        bXs = [Buf(f"xs{g}") for g in range(8)]
        bYf = [Buf(f"yf{g}") for g in range(8)]
        RG = [bwk, bpw, bbb, bprm, bcst, bbig]
        for bt in range(8):
            g0 = bt * 8
            S.dma("sp", uT[:], UT[bt * 128:(bt + 1) * 128, :], writes=[buT])

            def bc(ap2):
                return ap2.unsqueeze(2).broadcast_to([128, 8, 16])
            for j in range(8):
                for (dst, m_, X1, X2, pr_, pi_) in (
                    (LM, -j, B1, B2, PR, PI), (WTt, 7 - j, B1, B2, PR, PI)):
                    tt(tq2[:], X1[:, g0:g0 + 8, :], bc(pr_(m_)[:, g0:g0 + 8]), ALU.mult, RG, [bbig])
                    tt(dst[:, :, j, :], X2[:, g0:g0 + 8, :], bc(pi_(m_)[:, g0:g0 + 8]), ALU.mult, RG, [bbig])
                    tt(dst[:, :, j, :], dst[:, :, j, :], tq2[:], ALU.add, RG, [bbig])
                for (dst, m_) in ((RR, j), (GGb, j + 1)):
                    tt(tq2[:], cre[:, g0:g0 + 8, :], bc(E1[:, m_, g0:g0 + 8]), ALU.mult, RG, [bbig])
                    tt(tq3[:], cim[:, g0:g0 + 8, :], bc(E2[:, m_, g0:g0 + 8]), ALU.mult, RG, [bbig])
                    tt(dst[:, :, j, :], tq3[:], tq2[:], ALU.add, RG, [bbig])
            for g in range(8):
                gg = g0 + g
                b = bank()
                mm(ps[b][:, 0:128], [(LM[:, g, :, :].rearrange("p j q -> p (j q)"), RR[:, g, :, :].rearrange("p i q -> p (i q)"))],
                   [bbig], [pb[b]])
                tt(Tg[:, g, :], ps[b][:, 0:128], maskT, ALU.mult, [pb[b], bcst], [bmat])
                b = bank()
                S.op("pe", lambda e, b=b, g=g: e.transpose(ps[b][:, 0:128], WTt[:, g, :, :].rearrange("p j q -> p (j q)"), ident),
                     [bbig, bcst], [pb[b]])
                act(Wg[:, g, :], ps[b][:, 0:128], AF.Copy, [pb[b]], [bmat])
                for k in range(9):
                    ts(tmpM[:], ident, dp[:, k, 0, gg:gg + 1], None, ALU.mult, None, [bpw, bcst, bmat], [bmat])
                    stt(Mk[:, g, k, :], pswap, dps[:, k, gg:gg + 1], tmpM[:], ALU.mult, ALU.add, [bpw, bcst, bmat], [bmat])
            for g in range(8):
                q4, half = (g // 2), (g % 2)
                selx, rows = selrows(q4)
                b = 4 + bank()
                mm(ps[b][:, :], [(selx[rows, (half * 8 + j) * 128:(half * 8 + j + 1) * 128], uT[rows, j::8]) for j in range(8)],
                   [bsel, buT], [pb[b]])
                act(Uf[:, g, :], ps[b][:, :], AF.Copy, [pb[b]], [bUf[g]])
                b = 4 + bank()
                mm(ps[b][:, :], [(Wg[:, g, :], Uf[:, g, :])], [bmat, bUf[g]], [pb[b]])
                act(Xs[:, g, :], ps[b][:, :], AF.Copy, [pb[b]], [bXs[g]])
            for gq in range(2):
                for k in range(9):
                    sh = 1 << k
                    for g4 in range(4):
                        g = gq * 4 + g4
                        mm(ps[g4][:, sh:512], [(Mk[:, g, k, :], Xs[:, g, 0:512 - sh])], [bmat, bXs[g]], [pb[g4]])
                    for g4 in range(4):
                        g = gq * 4 + g4
                        tt(Xs[:, g, sh:512], Xs[:, g, sh:512], ps[g4][:, sh:512], ALU.add, [pb[g4], bXs[g]], [bXs[g]])
            for g in range(8):
                b = 4 + bank()
                mm(ps[b][:, :], [(Tg[:, g, :], Uf[:, g, :])], [bmat, bUf[g]], [pb[b]], start=True, stop=False)
                mm(ps[b][:, 1:512], [(GGb[:, g, :, :].rearrange("p i q -> p (i q)"), Xs[:, g, 0:511])], [bbig, bXs[g]], [pb[b]],
                   start=False, stop=True)
                act(Yf[:, g, :], ps[b][:, :], AF.Copy, [pb[b]], [bYf[g]])
            for i in range(8):
                q4, half = (i // 2), (i % 2)
                selx, rows = selrows(q4)
                b = 4 + bank()
                mm(ps[b][:, :], [(selx[rows, (half * 8 + g) * 128:(half * 8 + g + 1) * 128], Yf[rows, g, :]) for g in range(8)],
                   [bsel] + bYf, [pb[b]])
                stt(yT[:, i::8], uT[:, i::8], vecs[:, V_SD + bt:V_SD + bt + 1], ps[b][:, :], ALU.mult, ALU.add,
                    [pb[b], buT, bvec], [byT])
            for hf in range(2):
                hs = slice(hf * (S_LEN // 2), (hf + 1) * (S_LEN // 2))
                act(gt1[:], yT[:, hs], AF.Square, [byT], [bg1])
                ts(gt1[:], gt1[:], 0.044715, 1.0, ALU.mult, ALU.add, [bg1], [bg1])
                tt(gt1[:], gt1[:], yT[:, hs], ALU.mult, [bg1, byT], [bg1])
                act(gt1[:], gt1[:], AF.Sigmoid, [bg1], [bg1], scale=1.5957691216057308)
                tt(zT[:, hs], gt1[:], yT[:, hs], ALU.mult, [bg1, byT], [bzT])
            S.dma("sp", ZT[bt * 128:(bt + 1) * 128, :], zT[:], reads=[bzT])
        S.barrier()
        sb.reset(m)

    def out_tail(l, t0, rhsT, brhs, W2d, kcn, wrot, ystage_rot, sqrot, byt):
        for n in range(16):
            wt, bwt = wrot.next()
            alias = getattr(wrot, "alias", {}).get(id(bwt), [])
            src_ = W2d[:, n * 128:n * 128 + 128].rearrange("(kc p) n -> p kc n", p=128)
            S.dma("pool", wt[:, 0:kcn, 0:128], src_, writes=[bwt] + list(alias))
            ys, bys = ystage_rot.next()
            for sub in range(2):
                b = bank()
                mm(ps[b][:, :], [(wt[:, kc, 0:128], rhsT[:, kc, sub * 512:(sub + 1) * 512]) for kc in range(kcn)],
                   [bwt] + brhs + list(alias), [pb[b]])
                act(ys[:, sub * 512:(sub + 1) * 512], ps[b][:, :], AF.Copy, [pb[b]], [bys])
                sq, bq = sqrot.next()
                act(sq[:], ps[b][:, :], AF.Square, [pb[b]], [bq])
                mm(ps[6 + sub][:, :], [(onesb[:], sq[:])], [bq, bcst], [pb[6 + sub]], start=(n == 0), stop=(n == 15))
            S.dma("sp", YT[n * 128:(n + 1) * 128, t0:t0 + TT], ys[:], reads=[bys], writes=[byt[n]])

    def phase_D(l):
        m = sb.mark()
        xsrc = xT_in if l == 0 else XT
        OTt = sb.alloc([128, 16, TT], BF16)
        ZTt = sb.alloc([128, 8, TT], BF16)
        bin_ = Buf("din")
        mg = sb.alloc([128, 16, TT], BF16)
        bmg = [Buf(f"mg{n}") for n in range(16)]
        wrot = Rot(sb, 3, [128, 16, 128], BF16, "w")
        w8rot = Rot(sb, 4, [128, 8, 128], BF16, "w8")
        grot = Rot(sb, 4, [128, TT], BF16, "g")
        trot = Rot(sb, 6, [128, 512], F32, "t")
        yrot = Rot(sb, 2, [128, TT], F32, "y")
        xrot = Rot(sb, 2, [128, TT], F32, "x")
        sqrot = Rot(sb, 3, [128, 512], BF16, "sq")
        rstd = sb.alloc([128, TT], F32)
        brstd = Buf("rstd")
        byt = [Buf(f"yt{n}") for n in range(16)]
        pending = []
        for tti in range(NT):
            t0 = tti * TT
            S.dma("sp", OTt[:], OT[:, t0:t0 + TT].rearrange("(kc p) t -> p kc t", p=128), writes=[bin_])
            S.dma("sp", ZTt[:], ZT[:, t0:t0 + TT].rearrange("(kc p) t -> p kc t", p=128), writes=[bin_])
            for n in range(16):
                wo, bwo = wrot.next()
                wload(wo, bwo, w_o[l], n * 128, 128, 16)
                w1, bw1 = w8rot.next()
                wload(w1, bw1, w_glu[l], n * 128, 128, 8)
                w2, bw2 = w8rot.next()
                wload(w2, bw2, w_glu[l], D + n * 128, 128, 8)
                ga, bga = grot.next()
                gb, bgb = grot.next()
                S.dma("sp", ga[:], GT[n * 128:(n + 1) * 128, t0:t0 + TT], writes=[bga])
                S.dma("sp", gb[:], GT[D + n * 128:D + (n + 1) * 128, t0:t0 + TT], writes=[bgb])
                for sub in range(2):
                    cs_ = slice(sub * 512, (sub + 1) * 512)
                    base = 3 * (bank_i[0] % 2)
                    bank_i[0] += 1
                    ba_, b1_, b2_ = base, base + 1, base + 2
                    mm(ps[ba_][:, :], [(wo[:, kc, :], OTt[:, kc, cs_]) for kc in range(16)], [bwo, bin_], [pb[ba_]])
                    mm(ps[b1_][:, :], [(w1[:, kc, :], ZTt[:, kc, cs_]) for kc in range(8)], [bw1, bin_], [pb[b1_]])
                    mm(ps[b2_][:, :], [(w2[:, kc, :], ZTt[:, kc, cs_]) for kc in range(8)], [bw2, bin_], [pb[b2_]])
                    sg_, bsg = trot.next()
                    act(sg_[:], ps[b2_][:, :], AF.Sigmoid, [pb[b2_], bvec], [bsg], bias=vecs[:, V_BGLU + 16 + n:V_BGLU + 17 + n])
                    so, bso = trot.next()
                    stt(so[:], ps[b1_][:, :], vecs[:, V_BGLU + n:V_BGLU + n + 1], sg_[:], ALU.add, ALU.mult, [pb[b1_], bsg, bvec], [bso])
                    ta, bta = trot.next()
                    tt(ta[:], ps[ba_][:, :], ga[:, cs_], ALU.mult, [pb[ba_], bga], [bta])
                    tt(so[:], so[:], gb[:, cs_], ALU.mult, [bso, bgb], [bso])
                    tt(mg[:, n, cs_], ta[:], so[:], ALU.add, [bta, bso], [bmg[n]])
                if pending:
                    pending.pop(0)()
            while pending:
                pending.pop(0)()
            out_tail(l, t0, mg, bmg, w_out[l], 16, wrot, yrot, sqrot, byt)
            rstd_from([6, 7], rstd, brstd, D)
            pending = postnorm_steps(t0, YT, byt, xsrc, XT, V_POSTMIX, rstd, brstd, yrot, xrot)
        while pending:
            pending.pop(0)()
        S.barrier()
        sb.reset(m)

    def phase_E(l, last):
        m = sb.mark()
        dst = outT if last else XT
        hT = sb.alloc([128, 16, TT], BF16)
        hT_off = sb.last_off
        bh = [Buf(f"h{k}") for k in range(16)]
        actT = sb.alloc([128, 44, TT], BF16)
        bact = [Buf(f"a{f}") for f in range(44)]
        wrot = Rot(sb, 4, [128, 16, 128], BF16, "w")
        wdrot = Rot.__new__(Rot)
        wdrot.items = [(sb.at([128, 44, 128], BF16, hT_off + i * 12288), Buf(f"wd{i}")) for i in range(2)]
        wdrot.i = 0
        wdrot.alias = {id(wdrot.items[i][1]): bh[6 * i:6 * i + 6] for i in range(2)}
        trot = Rot(sb, 3, [128, 512], F32, "t")
        yrot = Rot(sb, 2, [128, TT], F32, "y")
        xrot = Rot(sb, 2, [128, TT], F32, "x")
        sqrot = Rot(sb, 3, [128, 512], BF16, "sq")
        rstd = sb.alloc([128, TT], F32)
        brstd = Buf("rstd")
        byt = [Buf(f"yt{n}") for n in range(16)]
        rstd2 = sb.alloc([128, TT], F32)
        brstd2 = Buf("rstd2")
        pending = []
        for tti in range(NT):
            t0 = tti * TT
            make_hT(XT, t0, V_PREF, hT, bh, xrot, sqrot, rstd, brstd)
            for f in range(44):
                wg_, bwg = wrot.next()
                wload(wg_, bwg, w_fg[l], f * 128, 128, 16)
                wu_, bwu = wrot.next()
                wload(wu_, bwu, w_fu[l], f * 128, 128, 16)
                for sub in range(2):
                    cs_ = slice(sub * 512, (sub + 1) * 512)
                    base = 2 * (bank_i[0] % 3)
                    bank_i[0] += 1
                    bg_, bu_ = base, base + 1
                    mm(ps[bg_][:, :], [(wg_[:, kc, :], hT[:, kc, cs_]) for kc in range(16)], [bwg] + bh, [pb[bg_]])
                    mm(ps[bu_][:, :], [(wu_[:, kc, :], hT[:, kc, cs_]) for kc in range(16)], [bwu] + bh, [pb[bu_]])
                    sg_, bsg = trot.next()
                    act(sg_[:], ps[bg_][:, :], AF.Silu, [pb[bg_]], [bsg])
                    tt(actT[:, f, cs_], ps[bu_][:, :], sg_[:], ALU.mult, [pb[bu_], bsg], [bact[f]])
                if pending:
                    pending.pop(0)()
            while pending:
                pending.pop(0)()
            out_tail(l, t0, actT, bact, w_fd[l], 44, wdrot, yrot, sqrot, byt)
            rstd_from([6, 7], rstd2, brstd2, D)
            pending = postnorm_steps(t0, YT, byt, XT, dst, V_POSTF, rstd2, brstd2, yrot, xrot)
        while pending:
            pending.pop(0)()
        S.barrier()
        sb.reset(m)

    rope_tables()
    for l in range(n_layers):
        S.dma("sp", vecs[:], vecs_in[l], writes=[bvec])
        if "A" in phases:
            phase_A(l)
        if "B" in phases:
            phase_B(l)
        if "C" in phases:
            phase_C(l)
        if "D" in phases:
            phase_D(l)
        if "E" in phases:
            phase_E(l, last=(l == n_layers - 1))
        S.barrier()
    S.barrier()
    S.emit()
    return nc, S


def _chunkcols(v):
    v = np.asarray(v, dtype=np.float32)
    return np.ascontiguousarray(v.reshape(-1, 128).T)


def host_consts():
    c = np.zeros((128, NCONST), np.float32)
    r = np.arange(128)
    c[r, C_ID + r] = 1.0
    c[r, C_PSW + (r + 64) % 128] = 1.0
    jj = r // 16
    c[:, C_MASK:C_MASK + 128] = (jj[None, :] >= jj[:, None]).astype(np.float32)
    for half in range(2):
        for j in range(8):
            blk = (half * 8 + j) * 128
            for rr in range(128):
                hp, qp = (rr % 32) // 16, rr % 16
                if hp == half:
                    c[rr, C_SEL + blk + 16 * j + qp] = 1.0
    c[:, C_ONES:C_ONES + 128] = 1.0
    c[96:128, C_SEL2:C_SEL2 + 2048] = c[96:128, C_SEL:C_SEL + 2048]
    sm = C_SM
    c[:64, sm + 0] = 1.0
    c[64:, sm + 1] = 1.0
    c[:64, sm + 2] = -1.0
    c[64:, sm + 3] = -1.0
    sg = np.where((r % 64) < 32, -1.0, 1.0)
    c[:, sm + 4] = sg
    inv = (10000.0 ** (-(np.arange(0, 64, 2, dtype=np.float32)) / 64.0)).astype(np.float32)
    c[:, sm + 5] = inv[r % 32]
    return c


def host_prepare(inp):
    Ld = L_DEPTH
    vecs = np.zeros((Ld, 128, NVEC), np.float32)
    s5p = np.zeros((Ld, 128, NS5P), np.float32)
    for l in range(Ld):
        vecs[l, :, V_PRE:V_PRE + 16] = _chunkcols(inp["pre_mix_norm"][l])
        vecs[l, :, V_QN:V_QN + 4] = _chunkcols(inp["q_norm"][l])
        vecs[l, :, V_KVN:V_KVN + 2] = _chunkcols(inp["kv_norm"][l])
        vecs[l, :, V_BG:V_BG + 32] = _chunkcols(inp["b_gate"][l])
        vecs[l, :, V_BGLU:V_BGLU + 32] = _chunkcols(inp["b_glu"][l])
        vecs[l, :, V_POSTMIX:V_POSTMIX + 16] = _chunkcols(inp["post_mix_norm"][l])
        vecs[l, :, V_PREF:V_PREF + 16] = _chunkcols(inp["pre_ffn_norm"][l])
        vecs[l, :, V_POSTF:V_POSTF + 16] = _chunkcols(inp["post_ffn_norm"][l])
        vecs[l, :, V_SD:V_SD + 8] = _chunkcols(inp["ssm_d"][l])
        arT = np.asarray(inp["ssm_a_re"][l], np.float32).T
        aiT = np.asarray(inp["ssm_a_im"][l], np.float32).T
        s5p[l, :, P_AR:P_AR + 64] = np.concatenate([arT, arT], 0)
        s5p[l, :, P_AI:P_AI + 64] = np.concatenate([aiT, aiT], 0)
        s5p[l, :, P_LDT:P_LDT + 64] = np.broadcast_to(np.asarray(inp["ssm_log_dt"][l], np.float32)[None, :], (128, 64))
        for key, off, perm in (("ssm_b_re", P_BRE, (1, 0, 2)), ("ssm_b_im", P_BIM, (1, 0, 2)),
                               ("ssm_c_re", P_CRE, (2, 0, 1)), ("ssm_c_im", P_CIM, (2, 0, 1))):
            a = np.transpose(np.asarray(inp[key][l], np.float32), perm).reshape(64, 1024)
            s5p[l, :, off:off + 1024] = np.concatenate([a, a], 0)
    return vecs, s5p


_CACHE = {}
LAYERS_PER_LAUNCH = 1


def kernel(**inputs):
    inp = {k: np.asarray(v) for k, v in inputs.items()}
    x = inp["x"]
    B = x.shape[0]
    vecs, s5p = host_prepare(inp)
    consts = host_consts()
    npl = LAYERS_PER_LAUNCH
    if npl not in _CACHE:
        _CACHE[npl] = build(n_layers=npl)[0]
    nc = _CACHE[npl]
    wnames = ["w_in", "w_uq", "w_ukv", "w_o_mla", "w_glu", "w_out", "w_ffn_gate", "w_ffn_up", "w_ffn_down"]
    xT = [np.ascontiguousarray(x[b].T, dtype=np.float32) for b in range(B)]
    for l0 in range(0, L_DEPTH, npl):
        shared = {k: np.ascontiguousarray(inp[k][l0:l0 + npl], dtype=np.float32) for k in wnames}
        shared["vecs"] = np.ascontiguousarray(vecs[l0:l0 + npl])
        shared["s5p"] = np.ascontiguousarray(s5p[l0:l0 + npl])
        shared["consts"] = consts
        in_maps = []
        for b in range(B):
            mp = dict(shared)
            mp["xT"] = xT[b]
            mp["pos"] = np.ascontiguousarray(inp["positions"][b], dtype=np.int32)
            in_maps.append(mp)
        res = run_bass_kernel_spmd(nc, in_maps, core_ids=list(range(B)))
        xT = [np.ascontiguousarray(res.results[b]["outT"], dtype=np.float32) for b in range(B)]
    out = np.stack([np.ascontiguousarray(xT[b].T) for b in range(B)], axis=0)
    return out.astype(np.float32)
```

```python
import math
import numpy as np
import concourse.bass as bass
import concourse.mybir as mybir
from concourse.bass_utils import run_bass_kernel_spmd

F32 = mybir.dt.float32
BF16 = mybir.dt.bfloat16
I32 = mybir.dt.int32
AF = mybir.ActivationFunctionType
ALU = mybir.AluOpType

D = 2048
S_LEN = 4096
L_DEPTH = 4
TT = 1024
NT = S_LEN // TT
NH = 16
FF = 5632
IN_W = 5952
OFF_KV = 512
OFF_PE = 768
OFF_SSM = 832
OFF_GATE = 1856
EPS = 1e-6
SCALE = 192.0 ** -0.5
MAGIC = 12582912.0
TWO_PI = 2.0 * math.pi
C1 = 6.28125
C2 = TWO_PI - C1

V_PRE, V_QN, V_KVN, V_BG, V_BGLU, V_POSTMIX, V_PREF, V_POSTF, V_SD = 0, 16, 20, 22, 54, 86, 102, 118, 134
NVEC = 142
C_ID, C_PSW, C_MASK, C_SEL, C_ONES, C_SM = 0, 128, 256, 384, 384 + 2048, 384 + 2048 + 128
C_SEL2 = C_SM + 8
NCONST = C_SEL2 + 2048
P_AR, P_AI, P_LDT, P_BRE, P_BIM, P_CRE, P_CIM = 0, 64, 128, 192, 192 + 1024, 192 + 2048, 192 + 3072
NS5P = 192 + 4096


class Buf:
    __slots__ = ("name", "w", "r")

    def __init__(self, name=""):
        self.name = name
        self.w = {}
        self.r = {}


class Sched:
    ENGS = ("pe", "act", "dve", "pool", "sp")
    ROT = 30000
    RING = 20

    def __init__(self, nc):
        self.nc = nc
        self.ops = {e: [] for e in self.ENGS}
        self.known = {e: {} for e in self.ENGS}
        self.csem = {}
        self.ccnt = {}
        self._csem_idx = {}
        for e in self.ENGS:
            self._new_csem(e)
        self.ring = {}
        self.ring_val = {}
        self.ring_i = {}
        for q in ("sp", "pool"):
            self.ring[q] = [nc.alloc_semaphore(f"dq_{q}_{i}") for i in range(self.RING)]
            self.ring_val[q] = [0] * self.RING
            self.ring_i[q] = 0
        self.n_instr = {e: 0 for e in self.ENGS}

    def _new_csem(self, e):
        idx = self._csem_idx.get(e, 0)
        self._csem_idx[e] = idx + 1
        self.csem[e] = self.nc.alloc_semaphore(f"c_{e}_{idx}")
        self.ccnt[e] = 0

    def _collect(self, eng, reads, writes):
        deps = {}

        def add(t):
            key = id(t[1])
            if key not in deps or deps[key][2] < t[2]:
                deps[key] = t

        for b in reads:
            for t in b.w.values():
                add(t)
        for b in writes:
            for t in b.w.values():
                add(t)
            for t in b.r.values():
                if t[0] == eng and eng in ("pe", "act", "dve"):
                    continue
                add(t)
        waits = []
        kn = self.known[eng]
        for key, t in deps.items():
            if t[0] == eng and eng == "pe":
                continue
            if kn.get(key, 0) >= t[2]:
                continue
            kn[key] = t[2]
            waits.append((t[1], t[2]))
        return waits

    def _register(self, ticket, reads, writes):
        key = id(ticket[1])
        for b in writes:
            b.w = {key: ticket}
            b.r = {}
        for b in reads:
            b.r[key] = ticket

    def op(self, eng, fn, reads=(), writes=()):
        waits = self._collect(eng, reads, writes)
        if self.ccnt[eng] >= self.ROT:
            self._new_csem(eng)
        self.ccnt[eng] += 1
        sem = self.csem[eng]
        ticket = (eng, sem, self.ccnt[eng])
        self.ops[eng].append((waits, fn, sem, 1))
        self._register(ticket, reads, writes)
        return ticket

    def dma(self, q, out_ap, in_ap, reads=(), writes=(), **kw):
        i = self.ring_i[q]
        self.ring_i[q] = (i + 1) % self.RING
        sem = self.ring[q][i]
        prev = self.ring_val[q][i]
        waits = self._collect(q, reads, writes)
        kn = self.known[q]
        if prev > 0 and kn.get(id(sem), 0) < prev:
            kn[id(sem)] = prev
            waits.append((sem, prev))
        val = prev + 16
        self.ring_val[q][i] = val
        ticket = (q + "_dma", sem, val)

        def fn(e, out_ap=out_ap, in_ap=in_ap, kw=kw):
            return e.dma_start(out=out_ap, in_=in_ap, **kw)

        self.ops[q].append((waits, fn, sem, 16))
        self._register(ticket, reads, writes)
        return ticket

    def barrier(self):
        tickets = []
        for q in ("sp", "pool"):
            for i in range(self.RING):
                if self.ring_val[q][i] > 0:
                    tickets.append((q + "_dma", self.ring[q][i], self.ring_val[q][i]))
        for e in self.ENGS:
            if self.ccnt[e] > 0:
                tickets.append((e, self.csem[e], self.ccnt[e]))
        for e in self.ENGS:
            kn = self.known[e]
            waits = []
            for t in tickets:
                if t[0] == e and e == "pe":
                    continue
                key = id(t[1])
                if kn.get(key, 0) >= t[2]:
                    continue
                kn[key] = t[2]
                waits.append((t[1], t[2]))
            if waits:
                self.ops[e].append((waits, None, None, 0))

    def emit(self):
        nc = self.nc
        engmap = {"pe": "tensor", "act": "scalar", "dve": "vector", "pool": "gpsimd", "sp": "sync"}
        with nc.Block() as block:
            for e in self.ENGS:
                ops = self.ops[e]
                if not ops:
                    continue

                def body(eng, ops=ops, e=e):
                    n = 0
                    for (waits, fn, sem, inc) in ops:
                        expl = waits if fn is None else waits[:-1]
                        for (s, v) in expl:
                            eng.wait_ge(s, v)
                            n += 1
                        if fn is not None:
                            r = fn(eng)
                            first, last = r if isinstance(r, tuple) else (r, r)
                            if waits:
                                first._wait_ge(waits[-1][0], waits[-1][1])
                            last.then_inc(sem, inc)
                            n += 1
                    self.n_instr[e] = n

                getattr(block, engmap[e])(body)


class SBAlloc:
    BASE = 16512
    CAP = 196608

    def __init__(self, nc):
        self.nc = nc
        self.top = self.BASE
        self.n = 0

    def mark(self):
        return self.top

    def reset(self, m):
        self.top = m

    def alloc(self, shape, dtype):
        nb = 4 if dtype in (F32, I32) else 2
        n = 1
        for s in shape[1:]:
            n *= s
        off = (self.top + 63) // 64 * 64
        self.top = off + n * nb
        assert self.top <= self.CAP, f"SBUF overflow {self.top}"
        self.n += 1
        self.last_off = off
        return self.nc.alloc_sbuf_tensor_at(f"sb{self.n}", list(shape), dtype, offset=off)

    def at(self, shape, dtype, off):
        self.n += 1
        return self.nc.alloc_sbuf_tensor_at(f"sb{self.n}", list(shape), dtype, offset=off)


class Rot:
    def __init__(self, sb, n, shape, dtype, name="r"):
        self.items = [(sb.alloc(shape, dtype), Buf(f"{name}{i}")) for i in range(n)]
        self.i = 0

    def next(self):
        it = self.items[self.i]
        self.i = (self.i + 1) % len(self.items)
        return it


def build(n_layers=L_DEPTH, dump=False, phases="ABCDE"):
    nc = bass.Bass("TRN2", target_bir_lowering=False)
    S = Sched(nc)
    sb = SBAlloc(nc)

    def din(name, shape, dt=F32):
        return nc.dram_tensor(name, list(shape), dt, kind="ExternalInput").ap()

    def dscr(name, shape, dt):
        kind = "ExternalOutput" if dump else "Internal"
        return nc.dram_tensor(name, list(shape), dt, kind=kind).ap()

    xT_in = din("xT", [D, S_LEN])
    pos_in = din("pos", [S_LEN], I32)
    w_in = din("w_in", [n_layers, D, IN_W])
    w_uq = din("w_uq", [n_layers, 512, 3072])
    w_ukv = din("w_ukv", [n_layers, 256, 4096])
    w_o = din("w_o_mla", [n_layers, D, D])
    w_glu = din("w_glu", [n_layers, 1024, 4096])
    w_out = din("w_out", [n_layers, D, D])
    w_fg = din("w_ffn_gate", [n_layers, D, FF])
    w_fu = din("w_ffn_up", [n_layers, D, FF])
    w_fd = din("w_ffn_down", [n_layers, FF, D])
    vecs_in = din("vecs", [n_layers, 128, NVEC])
    s5p_in = din("s5p", [n_layers, 128, NS5P])
    consts_in = din("consts", [128, NCONST])
    outT = nc.dram_tensor("outT", [D, S_LEN], F32, kind="ExternalOutput").ap()

    XT = dscr("XT", [D, S_LEN], F32)
    YT = dscr("YT", [D, S_LEN], F32)
    QT = dscr("QT", [NH, 192, S_LEN], BF16)
    KT = dscr("KT", [NH, 128, S_LEN], BF16)
    KPE = dscr("KPE", [64, S_LEN], BF16)
    VS = dscr("VS", [S_LEN, D], BF16)
    UT = dscr("UT", [1024, S_LEN], BF16)
    GT = dscr("GT", [4096, S_LEN], BF16)
    OT = dscr("OT", [D, S_LEN], BF16)
    ZT = dscr("ZT", [1024, S_LEN], BF16)
    CS = dscr("CS", [2, 64, S_LEN], F32)

    ps = [nc.alloc_psum_tensor(f"ps{i}", [128, 512], F32) for i in range(8)]
    pb = [Buf(f"pb{i}") for i in range(8)]
    bank_i = [0]

    def bank(n=4):
        b = bank_i[0] % n
        bank_i[0] += 1
        return b

    def mm(out_ap, pairs, reads, writes, start=True, stop=True):
        def fn(e):
            n = len(pairs)
            ins = None
            first = None
            for i, (lh, rh) in enumerate(pairs):
                ins = e.matmul(out_ap, lhsT=lh, rhs=rh, start=(start and i == 0), stop=(stop and i == n - 1))
                if first is None:
                    first = ins
            return (first, ins)
        return S.op("pe", fn, reads, writes)

    def act(out, in_, func, reads, writes, **kw):
        return S.op("act", lambda e: e.activation(out=out, in_=in_, func=func, **kw), reads, writes)

    def tt(out, a, b, op, reads, writes, eng="dve"):
        return S.op(eng, lambda e: e.tensor_tensor(out=out, in0=a, in1=b, op=op), reads, writes)

    def ts(out, a, s1, s2, op0, op1, reads, writes, eng="dve"):
        if op1 is None:
            return S.op(eng, lambda e: e.tensor_scalar(out=out, in0=a, scalar1=s1, scalar2=None, op0=op0), reads, writes)
        return S.op(eng, lambda e: e.tensor_scalar(out=out, in0=a, scalar1=s1, scalar2=s2, op0=op0, op1=op1), reads, writes)

    def stt(out, a, sc, b, op0, op1, reads, writes):
        return S.op("dve", lambda e: e.scalar_tensor_tensor(out=out, in0=a, scalar=sc, in1=b, op0=op0, op1=op1), reads, writes)

    def recip(out, in_, reads, writes):
        return S.op("dve", lambda e: e.reciprocal(out=out, in_=in_), reads, writes)

    def wload(wt, bwt, W2d, c0, ncols, kcn, col_dst=0):
        src = W2d[:, c0:c0 + ncols].rearrange("(kc p) n -> p kc n", p=128)
        S.dma("pool", wt[:, 0:kcn, col_dst:col_dst + ncols], src, writes=[bwt])

    cst = sb.alloc([128, C_SEL], F32)
    csm = sb.alloc([128, 8], F32)
    onesb = sb.alloc([128, 128], BF16)
    bcst = Buf("cst")
    S.dma("sp", cst[:], consts_in[:, 0:C_SEL], writes=[bcst])
    S.dma("sp", csm[:], consts_in[:, C_SM:C_SM + 8], writes=[bcst])
    S.dma("pool", onesb[:], consts_in[:, C_ONES:C_ONES + 128], writes=[bcst])

    ident = cst[:, C_ID:C_ID + 128]
    pswap = cst[:, C_PSW:C_PSW + 128]
    maskT = cst[:, C_MASK:C_MASK + 128]
    SELRE, SELIM, NSELRE, NSELIM, SGNROPE, INVF = 0, 1, 2, 3, 4, 5
    vecs = sb.alloc([128, NVEC], F32)
    bvec = Buf("vecs")
    persist_mark = sb.mark()

    def rope_tables():
        m = sb.mark()
        posi = sb.alloc([64, S_LEN], I32)
        ang = sb.alloc([64, S_LEN], F32)
        t1 = sb.alloc([64, S_LEN], F32)
        t2 = sb.alloc([64, S_LEN], F32)
        bp, ba, b1, b2 = Buf(), Buf(), Buf(), Buf()
        S.dma("sp", posi[:], pos_in.partition_broadcast(64), writes=[bp])
        S.op("dve", lambda e: e.tensor_copy(out=ang[:], in_=posi[:]), [bp], [ba])
        ts(ang[:], ang[:], csm[0:64, INVF:INVF + 1], None, ALU.mult, None, [ba, bcst], [ba])

        def reduce_sin(shift, dst_idx, signed):
            if shift != 0.0:
                ts(t1[:], ang[:], shift, None, ALU.add, None, [ba], [b1])
                srcang, bsrc = t1, b1
            else:
                srcang, bsrc = ang, ba
            ts(t2[:], srcang[:], 1.0 / TWO_PI, MAGIC, ALU.mult, ALU.add, [bsrc], [b2])
            ts(t2[:], t2[:], MAGIC, None, ALU.subtract, None, [b2], [b2])
            if shift == 0.0:
                stt(t1[:], t2[:], -C1, ang[:], ALU.mult, ALU.add, [b2, ba], [b1])
            else:
                stt(t1[:], t2[:], -C1, t1[:], ALU.mult, ALU.add, [b2, b1], [b1])
            stt(t1[:], t2[:], -C2, t1[:], ALU.mult, ALU.add, [b2, b1], [b1])
            ts(t1[:], t1[:], math.pi, -math.pi, ALU.min, ALU.max, [b1], [b1])
            act(t2[:], t1[:], AF.Sin, [b1], [b2])
            if signed:
                ts(t2[:], t2[:], csm[0:64, SGNROPE:SGNROPE + 1], None, ALU.mult, None, [b2, bcst], [b2])
            S.dma("sp", CS[dst_idx], t2[:], reads=[b2])

        reduce_sin(math.pi / 2.0, 0, False)
        reduce_sin(0.0, 1, True)
        S.barrier()
        sb.reset(m)

    def rstd_from(banks, out_tile, bout, nfeat, extra_scale=1.0):
        es2 = extra_scale * extra_scale
        for sub, bk in enumerate(banks):
            act(out_tile[:, sub * 512:(sub + 1) * 512], ps[bk][:, :], AF.Sqrt, [pb[bk]], [bout],
                scale=1.0 / (nfeat * es2), bias=EPS / es2)
        recip(out_tile[:], out_tile[:], [bout], [bout])

    def make_hT(src, t0, vcol, hT, bh, xrot, sqrot, rstd, brstd):
        for kc in range(16):
            xc, bx = xrot.next()
            S.dma("sp", xc[:], src[kc * 128:(kc + 1) * 128, t0:t0 + TT], writes=[bx])
            for sub in range(2):
                sq, bq = sqrot.next()
                act(sq[:], xc[:, sub * 512:(sub + 1) * 512], AF.Square, [bx], [bq])
                mm(ps[6 + sub][:, :], [(onesb[:], sq[:])], [bq, bcst], [pb[6 + sub]], start=(kc == 0), stop=(kc == 15))
        rstd_from([6, 7], rstd, brstd, D)
        for kc in range(16):
            xc, bx = xrot.next()
            S.dma("sp", xc[:], src[kc * 128:(kc + 1) * 128, t0:t0 + TT], writes=[bx])
            stt(hT[:, kc, :], xc[:], vecs[:, vcol + kc:vcol + kc + 1], rstd[:], ALU.mult, ALU.mult,
                [bx, brstd, bvec], [bh[kc]])

    def postnorm_steps(t0, ysrc, byt, xsrc, dst, vcol, rstd, brstd, yrot, xrot):
        return [(lambda n=n: postnorm_one(n, t0, ysrc, byt, xsrc, dst, vcol, rstd, brstd, yrot, xrot)) for n in range(16)]

    def postnorm_one(n, t0, ysrc, byt, xsrc, dst, vcol, rstd, brstd, yrot, xrot):
        if True:
            yc, by = yrot.next()
            xc, bx = xrot.next()
            S.dma("sp", yc[:], ysrc[n * 128:(n + 1) * 128, t0:t0 + TT], reads=[byt[n]], writes=[by])
            S.dma("sp", xc[:], xsrc[n * 128:(n + 1) * 128, t0:t0 + TT], writes=[bx])
            stt(yc[:], yc[:], vecs[:, vcol + n:vcol + n + 1], rstd[:], ALU.mult, ALU.mult, [by, brstd, bvec], [by])
            tt(yc[:], yc[:], xc[:], ALU.add, [by, bx], [by])
            S.dma("sp", dst[n * 128:(n + 1) * 128, t0:t0 + TT], yc[:], reads=[by])

    def phase_A(l):
        m = sb.mark()
        src = xT_in if l == 0 else XT
        wuq = sb.alloc([128, 4, 3072], BF16)
        wuqr = sb.alloc([128, 4, 16, 64], BF16)
        wukv = sb.alloc([128, 2, 4096], BF16)
        bw = Buf("wres")
        for c in range(3):
            wload(wuq, bw, w_uq[l], c * 1024, 1024, 4, col_dst=c * 1024)
        for c in range(4):
            wload(wukv, bw, w_ukv[l], c * 1024, 1024, 2, col_dst=c * 1024)
        uq4 = w_uq[l].rearrange("(kc p) (h e) -> kc p h e", p=128, e=192)
        for kc in range(4):
            S.dma("pool", wuqr[:, kc, :, 0:32], uq4[kc][:, :, 160:192], writes=[bw])
            S.dma("pool", wuqr[:, kc, :, 32:64], uq4[kc][:, :, 128:160], writes=[bw])
        hT = sb.alloc([128, 16, TT], BF16)
        bh = [Buf(f"h{k}") for k in range(16)]
        xrot = Rot(sb, 3, [128, TT], F32, "x")
        sqrot = Rot(sb, 3, [128, 512], BF16, "sq")
        rstd = sb.alloc([128, TT], F32)
        brstd = Buf("rstd")
        rstdq = sb.alloc([128, TT], F32)
        brq = Buf("rstdq")
        rstdkv = sb.alloc([128, TT], F32)
        brkv = Buf("rstdkv")
        rkt = sb.alloc([128, 8], F32)
        brkt = Buf("rkt")
        cqw = sb.alloc([128, 4, TT], BF16)
        bcq = Buf("cqw")
        ckvw = sb.alloc([128, 2, TT], BF16)
        bckv = Buf("ckvw")
        sqkv = sb.alloc([128, 2, TT], BF16)
        bsqkv = Buf("sqkv")
        cst_ = sb.alloc([64, 2, TT], F32)
        bcs = Buf("cs")
        wrot = Rot(sb, 3, [128, 16, 128], BF16, "w")
        strot = Rot(sb, 4, [128, 512], BF16, "st")
        tmrot = Rot(sb, 4, [64, 512], F32, "tm")

        def proj(c0, M, evac, rot_cols=None):
            wt, bwt = wrot.next()
            if rot_cols is None:
                wload(wt, bwt, w_in[l], c0, M, 16)
            else:
                wload(wt, bwt, w_in[l], rot_cols[0], 32, 16, col_dst=0)
                wload(wt, bwt, w_in[l], rot_cols[1], 32, 16, col_dst=32)
            for sub in range(2):
                b = bank()
                mm(ps[b][0:M, :], [(wt[:, kc, 0:M], hT[:, kc, sub * 512:(sub + 1) * 512]) for kc in range(16)],
                   [bwt] + bh, [pb[b]])
                evac(b, sub)

        make_hT(src, 0, V_PRE, hT, bh, xrot, sqrot, rstd, brstd)
        for tti in range(NT):
            t0 = tti * TT
            S.dma("sp", cst_[:, 0, :], CS[0][:, t0:t0 + TT], writes=[bcs])
            S.dma("sp", cst_[:, 1, :], CS[1][:, t0:t0 + TT], writes=[bcs])

            for j in range(4):
                def ev(b, sub, j=j):
                    sq, bq = sqrot.next()
                    act(sq[:], ps[b][:, :], AF.Square, [pb[b]], [bq])
                    act(cqw[:, j, sub * 512:(sub + 1) * 512], ps[b][:, :], AF.Copy, [pb[b], bvec], [bcq],
                        scale=vecs[:, V_QN + j:V_QN + j + 1])
                    mm(ps[6 + sub][:, :], [(onesb[:], sq[:])], [bq, bcst], [pb[6 + sub]], start=(j == 0), stop=(j == 3))
                proj(j * 128, 128, ev)
            rstd_from([6, 7], rstdq, brq, 512, extra_scale=SCALE)
            for j in range(2):
                def ev(b, sub, j=j):
                    act(sqkv[:, j, sub * 512:(sub + 1) * 512], ps[b][:, :], AF.Square, [pb[b]], [bsqkv])
                    act(ckvw[:, j, sub * 512:(sub + 1) * 512], ps[b][:, :], AF.Copy, [pb[b], bvec], [bckv],
                        scale=vecs[:, V_KVN + j:V_KVN + j + 1])
                    mm(ps[4 + sub][:, :], [(onesb[:], sqkv[:, j, sub * 512:(sub + 1) * 512])], [bsqkv, bcst], [pb[4 + sub]],
                       start=(j == 0), stop=(j == 1))
                proj(OFF_KV + j * 128, 128, ev)
            rstd_from([4, 5], rstdkv, brkv, 256)
            b = bank()
            for blk in range(8):
                mm(ps[b][:, blk:blk + 1], [(sqkv[:, j, blk * 128:(blk + 1) * 128], onesb[:, 0:1]) for j in range(2)],
                   [bsqkv, bcst], [pb[b]])
            act(rkt[:], ps[b][:, 0:8], AF.Sqrt, [pb[b]], [brkt], scale=1.0 / 256.0, bias=EPS)
            recip(rkt[:], rkt[:], [brkt], [brkt])
            pe_hold = {}

            def ev_pe(b, sub):
                ta, bta = tmrot.next()
                tt(ta[:], ps[b][0:64, :], cst_[:, 0, sub * 512:(sub + 1) * 512], ALU.mult, [pb[b], bcs], [bta])
                pe_hold[sub] = (ta, bta)

            def ev_rot(b, sub):
                ta, bta = pe_hold[sub]
                tb, btb = tmrot.next()
                tt(tb[:], ps[b][0:64, :], cst_[:, 1, sub * 512:(sub + 1) * 512], ALU.mult, [pb[b], bcs], [btb])
                st, bst = strot.next()
                tt(st[0:64, :], ta[:], tb[:], ALU.add, [bta, btb], [bst])
                S.dma("sp", KPE[:, t0 + sub * 512:t0 + (sub + 1) * 512], st[0:64, :], reads=[bst])
            proj(OFF_PE, 64, ev_pe)
            proj(OFF_PE, 64, ev_rot, rot_cols=(OFF_PE + 32, OFF_PE))
            for j in range(8):
                def ev(b, sub, j=j):
                    st, bst = strot.next()
                    act(st[:], ps[b][:, :], AF.Copy, [pb[b]], [bst])
                    S.dma("sp", UT[j * 128:(j + 1) * 128, t0 + sub * 512:t0 + (sub + 1) * 512], st[:], reads=[bst])
                proj(OFF_SSM + j * 128, 128, ev)
            for j in range(32):
                def ev(b, sub, j=j):
                    st, bst = strot.next()
                    act(st[:], ps[b][:, :], AF.Sigmoid, [pb[b], bvec], [bst], bias=vecs[:, V_BG + j:V_BG + j + 1])
                    S.dma("sp", GT[j * 128:(j + 1) * 128, t0 + sub * 512:t0 + (sub + 1) * 512], st[:], reads=[bst])
                proj(OFF_GATE + j * 128, 128, ev)
            if tti + 1 < NT:
                make_hT(src, t0 + TT, V_PRE, hT, bh, xrot, sqrot, rstd, brstd)
            for h in range(NH):
                for sub in range(2):
                    cs_ = slice(sub * 512, (sub + 1) * 512)
                    tok = slice(t0 + sub * 512, t0 + (sub + 1) * 512)
                    b = bank()
                    mm(ps[b][:, :], [(wuq[:, kc, h * 192:h * 192 + 128], cqw[:, kc, cs_]) for kc in range(4)], [bw, bcq], [pb[b]])
                    st, bst = strot.next()
                    tt(st[:], ps[b][:, :], rstdq[:, cs_], ALU.mult, [pb[b], brq], [bst])
                    S.dma("sp", QT[h, 0:128, tok], st[:], reads=[bst])
                    b1 = bank()
                    mm(ps[b1][0:64, :], [(wuq[:, kc, h * 192 + 128:h * 192 + 192], cqw[:, kc, cs_]) for kc in range(4)], [bw, bcq], [pb[b1]])
                    b2 = bank()
                    mm(ps[b2][0:64, :], [(wuqr[:, kc, h, :], cqw[:, kc, cs_]) for kc in range(4)], [bw, bcq], [pb[b2]])
                    ta, bta = tmrot.next()
                    tb, btb = tmrot.next()
                    tt(ta[:], ps[b1][0:64, :], cst_[:, 0, cs_], ALU.mult, [pb[b1], bcs], [bta])
                    tt(tb[:], ps[b2][0:64, :], cst_[:, 1, cs_], ALU.mult, [pb[b2], bcs], [btb])
                    tt(ta[:], ta[:], tb[:], ALU.add, [bta, btb], [bta])
                    st2, bst2 = strot.next()
                    tt(st2[0:64, :], ta[:], rstdq[0:64, cs_], ALU.mult, [bta, brq], [bst2])
                    S.dma("sp", QT[h, 128:192, tok], st2[0:64, :], reads=[bst2])
            for h in range(NH):
                for sub in range(2):
                    cs_ = slice(sub * 512, (sub + 1) * 512)
                    tok = slice(t0 + sub * 512, t0 + (sub + 1) * 512)
                    b = bank()
                    mm(ps[b][:, :], [(wukv[:, kc, h * 256:h * 256 + 128], ckvw[:, kc, cs_]) for kc in range(2)], [bw, bckv], [pb[b]])
                    st, bst = strot.next()
                    tt(st[:], ps[b][:, :], rstdkv[:, cs_], ALU.mult, [pb[b], brkv], [bst])
                    S.dma("sp", KT[h, :, tok], st[:], reads=[bst])
            wv = wukv[:, :, :].rearrange("p k (h e) -> p k h e", e=256)
            for blk in range(8):
                for cg in range(4):
                    b = bank()
                    mm(ps[b][:, :].rearrange("p (h e) -> p h e", e=128),
                       [(ckvw[:, kc, blk * 128:(blk + 1) * 128], wv[:, kc, 4 * cg:4 * cg + 4, 128:256]) for kc in range(2)],
                       [bw, bckv], [pb[b]])
                    st, bst = strot.next()
                    act(st[:], ps[b][:, :], AF.Copy, [pb[b], brkt], [bst], scale=rkt[:, blk:blk + 1])
                    S.dma("sp", VS[t0 + blk * 128:t0 + (blk + 1) * 128, cg * 512:(cg + 1) * 512], st[:], reads=[bst])
        S.barrier()
        sb.reset(m)

    def phase_B(l):
        m = sb.mark()
        kpe = sb.alloc([64, S_LEN], BF16)
        bkpe = Buf("kpe")
        S.dma("sp", kpe[:], KPE[:, :], writes=[bkpe])
        hb = []
        for i in range(2):
            hb.append(dict(
                qn=sb.alloc([128, S_LEN], BF16), qp=sb.alloc([64, S_LEN], BF16), kn=sb.alloc([128, S_LEN], BF16),
                v=sb.alloc([128, 32, 128], BF16), o=sb.alloc([128, S_LEN], BF16),
                bin=Buf(f"hin{i}"), bo=Buf(f"ho{i}")))
        ptrot = Rot(sb, 4, [128, 512], BF16, "pt")
        rcrot = Rot(sb, 2, [128, 512], F32, "rc")
        st_i = [0]
        for h in range(NH):
            B = hb[h % 2]
            S.dma("sp", B["qn"][:], QT[h, 0:128, :], writes=[B["bin"]])
            S.dma("sp", B["qp"][:], QT[h, 128:192, :], writes=[B["bin"]])
            S.dma("sp", B["kn"][:], KT[h, :, :], writes=[B["bin"]])
            S.dma("sp", B["v"][:], VS[:, h * 128:(h + 1) * 128].rearrange("(b p) e -> p b e", p=128), writes=[B["bin"]])
            pairs = []
            for qt in range(8):
                nkb = 4 * qt + 4
                for kb in range(nkb):
                    pairs.append((qt, kb, nkb))
            held = {}

            def emit_qk(i):
                qt, kb, nkb = pairs[i]
                d = kb - 4 * qt
                c0 = 0 if d < 0 else d * 128
                qs = slice(qt * 512 + c0, (qt + 1) * 512)
                ks = slice(kb * 128, (kb + 1) * 128)
                sbk = st_i[0] % 2
                st_i[0] += 1
                mm(ps[sbk][:, c0:512], [(B["kn"][:, ks], B["qn"][:, qs]), (kpe[:, ks], B["qp"][:, qs])],
                   [B["bin"], bkpe], [pb[sbk]])
                pt, bpt = ptrot.next()
                act(pt[:, c0:512], ps[sbk][:, c0:512], AF.Exp, [pb[sbk]], [bpt])
                if d >= 0:
                    S.op("pool", lambda e, pt=pt, c0=c0: e.memset(pt[64:128, c0:c0 + 64], 0.0), [], [bpt])
                held[i] = (pt, bpt, c0)

            def emit_pv(i):
                qt, kb, nkb = pairs[i]
                pt, bpt, c0 = held.pop(i)
                ob = 2 + (qt % 2)
                lb = 4 + (qt % 2)
                mm(ps[ob][:, c0:512], [(B["v"][:, kb, :], pt[:, c0:512])], [B["bin"], bpt], [pb[ob]],
                   start=(kb == 0), stop=(kb == nkb - 1))
                mm(ps[lb][:, c0:512], [(onesb[:], pt[:, c0:512])], [bcst, bpt], [pb[lb]],
                   start=(kb == 0), stop=(kb == nkb - 1))
                if kb == nkb - 1:
                    rc, brc = rcrot.next()
                    recip(rc[:], ps[lb][:, :], [pb[lb]], [brc])
                    tt(B["o"][:, qt * 512:(qt + 1) * 512], ps[ob][:, :], rc[:], ALU.mult, [pb[ob], brc], [B["bo"]])

            for i in range(len(pairs) + 1):
                if i < len(pairs):
                    emit_qk(i)
                if i >= 1:
                    emit_pv(i - 1)
            S.dma("sp", OT[h * 128:(h + 1) * 128, :], B["o"][:], reads=[B["bo"]])
        S.barrier()
        sb.reset(m)

    def phase_C(l):
        m = sb.mark()
        selb = sb.alloc([128, 2048], BF16)
        selb2 = sb.alloc([128, 2048], BF16)
        bsel = Buf("sel")
        for (t_, c_) in ((selb, C_SEL), (selb2, C_SEL2)):
            S.dma("pool", t_[:, 0:1024], consts_in[:, c_:c_ + 1024], writes=[bsel])
            S.dma("pool", t_[:, 1024:2048], consts_in[:, c_ + 1024:c_ + 2048], writes=[bsel])

        def selrows(q4):
            if q4 < 3:
                return selb, slice(32 * q4, 32 * q4 + 32)
            return selb2, slice(64, 128)
        prm = sb.alloc([128, 192], F32)
        bre = sb.alloc([128, 64, 16], F32)
        bim = sb.alloc([128, 64, 16], F32)
        cre = sb.alloc([128, 64, 16], F32)
        cim = sb.alloc([128, 64, 16], F32)
        bprm = Buf("prm")
        S.dma("sp", prm[:], s5p_in[l][:, 0:192], writes=[bprm])
        for t_, off in ((bre, P_BRE), (bim, P_BIM), (cre, P_CRE), (cim, P_CIM)):
            S.dma("sp", t_[:].rearrange("p g q -> p (g q)"), s5p_in[l][:, off:off + 1024], writes=[bprm])
        NW = 40
        wk = sb.alloc([128, NW, 64], F32)
        bwk = Buf("wk")
        W = lambda i: wk[:, i, :]
        R = [bwk, bprm, bcst]
        ar, ai, ldt = prm[:, P_AR:P_AR + 64], prm[:, P_AI:P_AI + 64], prm[:, P_LDT:P_LDT + 64]
        DT, ARD, AID, MAG, T1, T2, SINP, COSP, ABR, ABI, DEN, NR, FRE, FIM, IR, II = range(16)
        act(W(DT), ldt, AF.Exp, R, [bwk])
        tt(W(ARD), ar, W(DT), ALU.mult, R, [bwk])
        tt(W(AID), ai, W(DT), ALU.mult, R, [bwk])
        act(W(MAG), W(ARD), AF.Exp, R, [bwk])

        def red_sin(dst, shift):
            ts(W(T1), W(AID), shift, None, ALU.add, None, R, [bwk])
            ts(W(T2), W(T1), 1.0 / TWO_PI, MAGIC, ALU.mult, ALU.add, R, [bwk])
            ts(W(T2), W(T2), MAGIC, None, ALU.subtract, None, R, [bwk])
            stt(W(T1), W(T2), -C1, W(T1), ALU.mult, ALU.add, R, [bwk])
            stt(W(T1), W(T2), -C2, W(T1), ALU.mult, ALU.add, R, [bwk])
            ts(W(T1), W(T1), math.pi, -math.pi, ALU.min, ALU.max, R, [bwk])
            act(W(dst), W(T1), AF.Sin, R, [bwk])
        red_sin(SINP, 0.0)
        red_sin(COSP, math.pi / 2.0)
        tt(W(ABR), W(MAG), W(COSP), ALU.mult, R, [bwk])
        tt(W(ABI), W(MAG), W(SINP), ALU.mult, R, [bwk])
        tt(W(T1), ar, ar, ALU.mult, R, [bwk])
        tt(W(T2), ai, ai, ALU.mult, R, [bwk])
        tt(W(DEN), W(T1), W(T2), ALU.add, R, [bwk])
        recip(W(DEN), W(DEN), R, [bwk])
        ts(W(NR), W(ABR), -1.0, None, ALU.add, None, R, [bwk])
        tt(W(T1), W(NR), ar, ALU.mult, R, [bwk])
        tt(W(T2), W(ABI), ai, ALU.mult, R, [bwk])
        tt(W(T1), W(T1), W(T2), ALU.add, R, [bwk])
        tt(W(FRE), W(T1), W(DEN), ALU.mult, R, [bwk])
        tt(W(T1), W(ABI), ar, ALU.mult, R, [bwk])
        tt(W(T2), W(NR), ai, ALU.mult, R, [bwk])
        tt(W(T1), W(T1), W(T2), ALU.subtract, R, [bwk])
        tt(W(FIM), W(T1), W(DEN), ALU.mult, R, [bwk])
        tt(W(T1), W(ABR), W(ABR), ALU.mult, R, [bwk])
        tt(W(T2), W(ABI), W(ABI), ALU.mult, R, [bwk])
        tt(W(T1), W(T1), W(T2), ALU.add, R, [bwk])
        recip(W(T1), W(T1), R, [bwk])
        tt(W(IR), W(ABR), W(T1), ALU.mult, R, [bwk])
        tt(W(T2), W(ABI), W(T1), ALU.mult, R, [bwk])
        ts(W(II), W(T2), -1.0, None, ALU.mult, None, R, [bwk])
        pw = sb.alloc([128, 16, 2, 64], F32)
        bpw = Buf("pw")
        RP = [bwk, bpw, bcst]
        PR = lambda m_: pw[:, m_ + 7, 0, :]
        PI = lambda m_: pw[:, m_ + 7, 1, :]
        S.op("dve", lambda e: e.memset(PR(0), 1.0), RP, [bpw])
        S.op("dve", lambda e: e.memset(PI(0), 0.0), RP, [bpw])

        def cmul(dr, di, xr, xi, yr, yi):
            tt(W(T1), xr, yr, ALU.mult, RP, [bwk])
            tt(W(T2), xi, yi, ALU.mult, RP, [bwk])
            tt(W(16), xr, yi, ALU.mult, RP, [bwk])
            tt(W(17), xi, yr, ALU.mult, RP, [bwk])
            tt(dr, W(T1), W(T2), ALU.subtract, RP, [bpw])
            tt(di, W(16), W(17), ALU.add, RP, [bpw])
        for m_ in range(1, 9):
            cmul(PR(m_), PI(m_), PR(m_ - 1), PI(m_ - 1), W(ABR), W(ABI))
        for m_ in range(-1, -8, -1):
            cmul(PR(m_), PI(m_), PR(m_ + 1), PI(m_ + 1), W(IR), W(II))
        dp = sb.alloc([128, 9, 2, 64], F32)
        S.op("dve", lambda e: e.tensor_copy(out=dp[:, 0, 0, :], in_=PR(8)), RP, [bpw])
        S.op("dve", lambda e: e.tensor_copy(out=dp[:, 0, 1, :], in_=PI(8)), RP, [bpw])
        for k in range(1, 9):
            cmul(dp[:, k, 0, :], dp[:, k, 1, :], dp[:, k - 1, 0, :], dp[:, k - 1, 1, :], dp[:, k - 1, 0, :], dp[:, k - 1, 1, :])
        dps = sb.alloc([128, 9, 64], F32)
        sg = sb.alloc([128, 1], F32)
        tt(sg[:], csm[:, SELRE:SELRE + 1], csm[:, SELIM:SELIM + 1], ALU.subtract, RP, [bpw])
        for k in range(9):
            ts(dps[:, k, :], dp[:, k, 1, :], sg[:, 0:1], None, ALU.mult, None, RP, [bpw])
        E1 = sb.alloc([128, 9, 64], F32)
        E2 = sb.alloc([128, 9, 64], F32)
        for m_ in range(9):
            ts(W(T1), PR(m_), csm[:, SELRE:SELRE + 1], None, ALU.mult, None, RP, [bwk])
            stt(E1[:, m_, :], PI(m_), csm[:, NSELIM:NSELIM + 1], W(T1), ALU.mult, ALU.add, RP, [bpw])
            ts(W(T1), PI(m_), csm[:, NSELRE:NSELRE + 1], None, ALU.mult, None, RP, [bwk])
            stt(E2[:, m_, :], PR(m_), csm[:, NSELIM:NSELIM + 1], W(T1), ALU.mult, ALU.add, RP, [bpw])
        B1 = sb.alloc([128, 64, 16], F32)
        B2 = sb.alloc([128, 64, 16], F32)
        m_alias = sb.mark()
        bbr = sb.alloc([128, 64, 16], F32)
        bbi = sb.alloc([128, 64, 16], F32)
        tq = sb.alloc([128, 64, 16], F32)
        bbb = Buf("bb")
        RB = [bwk, bpw, bprm, bbb, bcst]
        fre_b = W(FRE).unsqueeze(2).broadcast_to([128, 64, 16])
        fim_b = W(FIM).unsqueeze(2).broadcast_to([128, 64, 16])
        tt(bbr[:], bre[:], fre_b, ALU.mult, RB, [bbb])
        tt(tq[:], bim[:], fim_b, ALU.mult, RB, [bbb])
        tt(bbr[:], bbr[:], tq[:], ALU.subtract, RB, [bbb])
        tt(bbi[:], bim[:], fre_b, ALU.mult, RB, [bbb])
        tt(tq[:], bre[:], fim_b, ALU.mult, RB, [bbb])
        tt(bbi[:], bbi[:], tq[:], ALU.add, RB, [bbb])
        f2 = lambda t_: t_[:].rearrange("p g q -> p (g q)")
        ts(f2(tq), f2(bbr), csm[:, SELRE:SELRE + 1], None, ALU.mult, None, RB, [bbb])
        stt(f2(B1), f2(bbi), csm[:, SELIM:SELIM + 1], f2(tq), ALU.mult, ALU.add, RB, [bbb])
        ts(f2(tq), f2(bbi), csm[:, NSELRE:NSELRE + 1], None, ALU.mult, None, RB, [bbb])
        stt(f2(B2), f2(bbr), csm[:, SELIM:SELIM + 1], f2(tq), ALU.mult, ALU.add, RB, [bbb])

        sb.reset(m_alias)
        LM = sb.alloc([128, 8, 8, 16], F32)
        WTt = sb.alloc([128, 8, 8, 16], F32)
        RR = sb.alloc([128, 8, 8, 16], F32)
        GGb = sb.alloc([128, 8, 8, 16], BF16)
        tq2 = sb.alloc([128, 8, 16], F32)
        tq3 = sb.alloc([128, 8, 16], F32)
        bbig = Buf("big")
        Tg = sb.alloc([128, 8, 128], BF16)
        Wg = sb.alloc([128, 8, 128], BF16)
        Mk = sb.alloc([128, 8, 9, 128], BF16)
        tmpM = sb.alloc([128, 128], F32)
        bmat = Buf("mat")
        uT = sb.alloc([128, S_LEN], BF16)
        buT = Buf("uT")
        yT = sb.alloc([128, S_LEN], F32)
        byT = Buf("yT")
        zT = sb.alloc([128, S_LEN], BF16)
        bzT = Buf("zT")
        gt1 = sb.alloc([128, S_LEN // 2], F32)
        bg1 = Buf("g1")
        Uf = sb.alloc([128, 8, 512], BF16)
        Xs = sb.alloc([128, 8, 512], BF16)
        Yf = sb.alloc([128, 8, 512], BF16)
        bUf = [Buf(f"uf{g}") for g in range(8)]
        bXs = [Buf(f"xs{g}") for g in range(8)]
        bYf = [Buf(f"yf{g}") for g in range(8)]
        RG = [bwk, bpw, bbb, bprm, bcst, bbig]
        for bt in range(8):
            g0 = bt * 8
            S.dma("sp", uT[:], UT[bt * 128:(bt + 1) * 128, :], writes=[buT])

            def bc(ap2):
                return ap2.unsqueeze(2).broadcast_to([128, 8, 16])
            for j in range(8):
                for (dst, m_, X1, X2, pr_, pi_) in (
                    (LM, -j, B1, B2, PR, PI), (WTt, 7 - j, B1, B2, PR, PI)):
                    tt(tq2[:], X1[:, g0:g0 + 8, :], bc(pr_(m_)[:, g0:g0 + 8]), ALU.mult, RG, [bbig])
                    tt(dst[:, :, j, :], X2[:, g0:g0 + 8, :], bc(pi_(m_)[:, g0:g0 + 8]), ALU.mult, RG, [bbig])
                    tt(dst[:, :, j, :], dst[:, :, j, :], tq2[:], ALU.add, RG, [bbig])
                for (dst, m_) in ((RR, j), (GGb, j + 1)):
                    tt(tq2[:], cre[:, g0:g0 + 8, :], bc(E1[:, m_, g0:g0 + 8]), ALU.mult, RG, [bbig])
                    tt(tq3[:], cim[:, g0:g0 + 8, :], bc(E2[:, m_, g0:g0 + 8]), ALU.mult, RG, [bbig])
                    tt(dst[:, :, j, :], tq3[:], tq2[:], ALU.add, RG, [bbig])
            for g in range(8):
                gg = g0 + g
                b = bank()
                mm(ps[b][:, 0:128], [(LM[:, g, :, :].rearrange("p j q -> p (j q)"), RR[:, g, :, :].rearrange("p i q -> p (i q)"))],
                   [bbig], [pb[b]])
                tt(Tg[:, g, :], ps[b][:, 0:128], maskT, ALU.mult, [pb[b], bcst], [bmat])
                b = bank()
                S.op("pe", lambda e, b=b, g=g: e.transpose(ps[b][:, 0:128], WTt[:, g, :, :].rearrange("p j q -> p (j q)"), ident),
                     [bbig, bcst], [pb[b]])
                act(Wg[:, g, :], ps[b][:, 0:128], AF.Copy, [pb[b]], [bmat])
                for k in range(9):
                    ts(tmpM[:], ident, dp[:, k, 0, gg:gg + 1], None, ALU.mult, None, [bpw, bcst, bmat], [bmat])
                    stt(Mk[:, g, k, :], pswap, dps[:, k, gg:gg + 1], tmpM[:], ALU.mult, ALU.add, [bpw, bcst, bmat], [bmat])
            for g in range(8):
                q4, half = (g // 2), (g % 2)
                selx, rows = selrows(q4)
                b = 4 + bank()
                mm(ps[b][:, :], [(selx[rows, (half * 8 + j) * 128:(half * 8 + j + 1) * 128], uT[rows, j::8]) for j in range(8)],
                   [bsel, buT], [pb[b]])
                act(Uf[:, g, :], ps[b][:, :], AF.Copy, [pb[b]], [bUf[g]])
                b = 4 + bank()
                mm(ps[b][:, :], [(Wg[:, g, :], Uf[:, g, :])], [bmat, bUf[g]], [pb[b]])
                act(Xs[:, g, :], ps[b][:, :], AF.Copy, [pb[b]], [bXs[g]])
            for gq in range(2):
                for k in range(9):
                    sh = 1 << k
                    for g4 in range(4):
                        g = gq * 4 + g4
                        mm(ps[g4][:, sh:512], [(Mk[:, g, k, :], Xs[:, g, 0:512 - sh])], [bmat, bXs[g]], [pb[g4]])
                    for g4 in range(4):
                        g = gq * 4 + g4
                        tt(Xs[:, g, sh:512], Xs[:, g, sh:512], ps[g4][:, sh:512], ALU.add, [pb[g4], bXs[g]], [bXs[g]])
            for g in range(8):
                b = 4 + bank()
                mm(ps[b][:, :], [(Tg[:, g, :], Uf[:, g, :])], [bmat, bUf[g]], [pb[b]], start=True, stop=False)
                mm(ps[b][:, 1:512], [(GGb[:, g, :, :].rearrange("p i q -> p (i q)"), Xs[:, g, 0:511])], [bbig, bXs[g]], [pb[b]],
                   start=False, stop=True)
                act(Yf[:, g, :], ps[b][:, :], AF.Copy, [pb[b]], [bYf[g]])
            for i in range(8):
                q4, half = (i // 2), (i % 2)
                selx, rows = selrows(q4)
                b = 4 + bank()
                mm(ps[b][:, :], [(selx[rows, (half * 8 + g) * 128:(half * 8 + g + 1) * 128], Yf[rows, g, :]) for g in range(8)],
                   [bsel] + bYf, [pb[b]])
                stt(yT[:, i::8], uT[:, i::8], vecs[:, V_SD + bt:V_SD + bt + 1], ps[b][:, :], ALU.mult, ALU.add,
                    [pb[b], buT, bvec], [byT])
            for hf in range(2):
                hs = slice(hf * (S_LEN // 2), (hf + 1) * (S_LEN // 2))
                act(gt1[:], yT[:, hs], AF.Square, [byT], [bg1])
                ts(gt1[:], gt1[:], 0.044715, 1.0, ALU.mult, ALU.add, [bg1], [bg1])
                tt(gt1[:], gt1[:], yT[:, hs], ALU.mult, [bg1, byT], [bg1])
                act(gt1[:], gt1[:], AF.Sigmoid, [bg1], [bg1], scale=1.5957691216057308)
                tt(zT[:, hs], gt1[:], yT[:, hs], ALU.mult, [bg1, byT], [bzT])
            S.dma("sp", ZT[bt * 128:(bt + 1) * 128, :], zT[:], reads=[bzT])
        S.barrier()
        sb.reset(m)

    def out_tail(l, t0, rhsT, brhs, W2d, kcn, wrot, ystage_rot, sqrot, byt):
        for n in range(16):
            wt, bwt = wrot.next()
            alias = getattr(wrot, "alias", {}).get(id(bwt), [])
            src_ = W2d[:, n * 128:n * 128 + 128].rearrange("(kc p) n -> p kc n", p=128)
            S.dma("pool", wt[:, 0:kcn, 0:128], src_, writes=[bwt] + list(alias))
            ys, bys = ystage_rot.next()
            for sub in range(2):
                b = bank()
                mm(ps[b][:, :], [(wt[:, kc, 0:128], rhsT[:, kc, sub * 512:(sub + 1) * 512]) for kc in range(kcn)],
                   [bwt] + brhs + list(alias), [pb[b]])
                act(ys[:, sub * 512:(sub + 1) * 512], ps[b][:, :], AF.Copy, [pb[b]], [bys])
                sq, bq = sqrot.next()
                act(sq[:], ps[b][:, :], AF.Square, [pb[b]], [bq])
                mm(ps[6 + sub][:, :], [(onesb[:], sq[:])], [bq, bcst], [pb[6 + sub]], start=(n == 0), stop=(n == 15))
            S.dma("sp", YT[n * 128:(n + 1) * 128, t0:t0 + TT], ys[:], reads=[bys], writes=[byt[n]])

    def phase_D(l):
        m = sb.mark()
        xsrc = xT_in if l == 0 else XT
        OTt = sb.alloc([128, 16, TT], BF16)
        ZTt = sb.alloc([128, 8, TT], BF16)
        bin_ = Buf("din")
        mg = sb.alloc([128, 16, TT], BF16)
        bmg = [Buf(f"mg{n}") for n in range(16)]
        wrot = Rot(sb, 3, [128, 16, 128], BF16, "w")
        w8rot = Rot(sb, 4, [128, 8, 128], BF16, "w8")
        grot = Rot(sb, 4, [128, TT], BF16, "g")
        trot = Rot(sb, 6, [128, 512], F32, "t")
        yrot = Rot(sb, 2, [128, TT], F32, "y")
        xrot = Rot(sb, 2, [128, TT], F32, "x")
        sqrot = Rot(sb, 3, [128, 512], BF16, "sq")
        rstd = sb.alloc([128, TT], F32)
        brstd = Buf("rstd")
        byt = [Buf(f"yt{n}") for n in range(16)]
        pending = []
        for tti in range(NT):
            t0 = tti * TT
            S.dma("sp", OTt[:], OT[:, t0:t0 + TT].rearrange("(kc p) t -> p kc t", p=128), writes=[bin_])
            S.dma("sp", ZTt[:], ZT[:, t0:t0 + TT].rearrange("(kc p) t -> p kc t", p=128), writes=[bin_])
            for n in range(16):
                wo, bwo = wrot.next()
                wload(wo, bwo, w_o[l], n * 128, 128, 16)
                w1, bw1 = w8rot.next()
                wload(w1, bw1, w_glu[l], n * 128, 128, 8)
                w2, bw2 = w8rot.next()
                wload(w2, bw2, w_glu[l], D + n * 128, 128, 8)
                ga, bga = grot.next()
                gb, bgb = grot.next()
                S.dma("sp", ga[:], GT[n * 128:(n + 1) * 128, t0:t0 + TT], writes=[bga])
                S.dma("sp", gb[:], GT[D + n * 128:D + (n + 1) * 128, t0:t0 + TT], writes=[bgb])
                for sub in range(2):
                    cs_ = slice(sub * 512, (sub + 1) * 512)
                    base = 3 * (bank_i[0] % 2)
                    bank_i[0] += 1
                    ba_, b1_, b2_ = base, base + 1, base + 2
                    mm(ps[ba_][:, :], [(wo[:, kc, :], OTt[:, kc, cs_]) for kc in range(16)], [bwo, bin_], [pb[ba_]])
                    mm(ps[b1_][:, :], [(w1[:, kc, :], ZTt[:, kc, cs_]) for kc in range(8)], [bw1, bin_], [pb[b1_]])
                    mm(ps[b2_][:, :], [(w2[:, kc, :], ZTt[:, kc, cs_]) for kc in range(8)], [bw2, bin_], [pb[b2_]])
                    sg_, bsg = trot.next()
                    act(sg_[:], ps[b2_][:, :], AF.Sigmoid, [pb[b2_], bvec], [bsg], bias=vecs[:, V_BGLU + 16 + n:V_BGLU + 17 + n])
                    so, bso = trot.next()
                    stt(so[:], ps[b1_][:, :], vecs[:, V_BGLU + n:V_BGLU + n + 1], sg_[:], ALU.add, ALU.mult, [pb[b1_], bsg, bvec], [bso])
                    ta, bta = trot.next()
                    tt(ta[:], ps[ba_][:, :], ga[:, cs_], ALU.mult, [pb[ba_], bga], [bta])
                    tt(so[:], so[:], gb[:, cs_], ALU.mult, [bso, bgb], [bso])
                    tt(mg[:, n, cs_], ta[:], so[:], ALU.add, [bta, bso], [bmg[n]])
                if pending:
                    pending.pop(0)()
            while pending:
                pending.pop(0)()
            out_tail(l, t0, mg, bmg, w_out[l], 16, wrot, yrot, sqrot, byt)
            rstd_from([6, 7], rstd, brstd, D)
            pending = postnorm_steps(t0, YT, byt, xsrc, XT, V_POSTMIX, rstd, brstd, yrot, xrot)
        while pending:
            pending.pop(0)()
        S.barrier()
        sb.reset(m)

    def phase_E(l, last):
        m = sb.mark()
        dst = outT if last else XT
        hT = sb.alloc([128, 16, TT], BF16)
        hT_off = sb.last_off
        bh = [Buf(f"h{k}") for k in range(16)]
        actT = sb.alloc([128, 44, TT], BF16)
        bact = [Buf(f"a{f}") for f in range(44)]
        wrot = Rot(sb, 4, [128, 16, 128], BF16, "w")
        wdrot = Rot.__new__(Rot)
        wdrot.items = [(sb.at([128, 44, 128], BF16, hT_off + i * 12288), Buf(f"wd{i}")) for i in range(2)]
        wdrot.i = 0
        wdrot.alias = {id(wdrot.items[i][1]): bh[6 * i:6 * i + 6] for i in range(2)}
        trot = Rot(sb, 3, [128, 512], F32, "t")
        yrot = Rot(sb, 2, [128, TT], F32, "y")
        xrot = Rot(sb, 2, [128, TT], F32, "x")
        sqrot = Rot(sb, 3, [128, 512], BF16, "sq")
        rstd = sb.alloc([128, TT], F32)
        brstd = Buf("rstd")
        byt = [Buf(f"yt{n}") for n in range(16)]
        rstd2 = sb.alloc([128, TT], F32)
        brstd2 = Buf("rstd2")
        pending = []
        for tti in range(NT):
            t0 = tti * TT
            make_hT(XT, t0, V_PREF, hT, bh, xrot, sqrot, rstd, brstd)
            for f in range(44):
                wg_, bwg = wrot.next()
                wload(wg_, bwg, w_fg[l], f * 128, 128, 16)
                wu_, bwu = wrot.next()
                wload(wu_, bwu, w_fu[l], f * 128, 128, 16)
                for sub in range(2):
                    cs_ = slice(sub * 512, (sub + 1) * 512)
                    base = 2 * (bank_i[0] % 3)
                    bank_i[0] += 1
                    bg_, bu_ = base, base + 1
                    mm(ps[bg_][:, :], [(wg_[:, kc, :], hT[:, kc, cs_]) for kc in range(16)], [bwg] + bh, [pb[bg_]])
                    mm(ps[bu_][:, :], [(wu_[:, kc, :], hT[:, kc, cs_]) for kc in range(16)], [bwu] + bh, [pb[bu_]])
                    sg_, bsg = trot.next()
                    act(sg_[:], ps[bg_][:, :], AF.Silu, [pb[bg_]], [bsg])
                    tt(actT[:, f, cs_], ps[bu_][:, :], sg_[:], ALU.mult, [pb[bu_], bsg], [bact[f]])
                if pending:
                    pending.pop(0)()
            while pending:
                pending.pop(0)()
            out_tail(l, t0, actT, bact, w_fd[l], 44, wdrot, yrot, sqrot, byt)
            rstd_from([6, 7], rstd2, brstd2, D)
            pending = postnorm_steps(t0, YT, byt, XT, dst, V_POSTF, rstd2, brstd2, yrot, xrot)
        while pending:
            pending.pop(0)()
        S.barrier()
        sb.reset(m)

    rope_tables()
    for l in range(n_layers):
        S.dma("sp", vecs[:], vecs_in[l], writes=[bvec])
        if "A" in phases:
            phase_A(l)
        if "B" in phases:
            phase_B(l)
        if "C" in phases:
            phase_C(l)
        if "D" in phases:
            phase_D(l)
        if "E" in phases:
            phase_E(l, last=(l == n_layers - 1))
        S.barrier()
    S.barrier()
    S.emit()
    return nc, S


def _chunkcols(v):
    v = np.asarray(v, dtype=np.float32)
    return np.ascontiguousarray(v.reshape(-1, 128).T)


def host_consts():
    c = np.zeros((128, NCONST), np.float32)
    r = np.arange(128)
    c[r, C_ID + r] = 1.0
    c[r, C_PSW + (r + 64) % 128] = 1.0
    jj = r // 16
    c[:, C_MASK:C_MASK + 128] = (jj[None, :] >= jj[:, None]).astype(np.float32)
    for half in range(2):
        for j in range(8):
            blk = (half * 8 + j) * 128
            for rr in range(128):
                hp, qp = (rr % 32) // 16, rr % 16
                if hp == half:
                    c[rr, C_SEL + blk + 16 * j + qp] = 1.0
    c[:, C_ONES:C_ONES + 128] = 1.0
    c[96:128, C_SEL2:C_SEL2 + 2048] = c[96:128, C_SEL:C_SEL + 2048]
    sm = C_SM
    c[:64, sm + 0] = 1.0
    c[64:, sm + 1] = 1.0
    c[:64, sm + 2] = -1.0
    c[64:, sm + 3] = -1.0
    sg = np.where((r % 64) < 32, -1.0, 1.0)
    c[:, sm + 4] = sg
    inv = (10000.0 ** (-(np.arange(0, 64, 2, dtype=np.float32)) / 64.0)).astype(np.float32)
    c[:, sm + 5] = inv[r % 32]
    return c


def host_prepare(inp):
    Ld = L_DEPTH
    vecs = np.zeros((Ld, 128, NVEC), np.float32)
    s5p = np.zeros((Ld, 128, NS5P), np.float32)
    for l in range(Ld):
        vecs[l, :, V_PRE:V_PRE + 16] = _chunkcols(inp["pre_mix_norm"][l])
        vecs[l, :, V_QN:V_QN + 4] = _chunkcols(inp["q_norm"][l])
        vecs[l, :, V_KVN:V_KVN + 2] = _chunkcols(inp["kv_norm"][l])
        vecs[l, :, V_BG:V_BG + 32] = _chunkcols(inp["b_gate"][l])
        vecs[l, :, V_BGLU:V_BGLU + 32] = _chunkcols(inp["b_glu"][l])
        vecs[l, :, V_POSTMIX:V_POSTMIX + 16] = _chunkcols(inp["post_mix_norm"][l])
        vecs[l, :, V_PREF:V_PREF + 16] = _chunkcols(inp["pre_ffn_norm"][l])
        vecs[l, :, V_POSTF:V_POSTF + 16] = _chunkcols(inp["post_ffn_norm"][l])
        vecs[l, :, V_SD:V_SD + 8] = _chunkcols(inp["ssm_d"][l])
        arT = np.asarray(inp["ssm_a_re"][l], np.float32).T
        aiT = np.asarray(inp["ssm_a_im"][l], np.float32).T
        s5p[l, :, P_AR:P_AR + 64] = np.concatenate([arT, arT], 0)
        s5p[l, :, P_AI:P_AI + 64] = np.concatenate([aiT, aiT], 0)
        s5p[l, :, P_LDT:P_LDT + 64] = np.broadcast_to(np.asarray(inp["ssm_log_dt"][l], np.float32)[None, :], (128, 64))
        for key, off, perm in (("ssm_b_re", P_BRE, (1, 0, 2)), ("ssm_b_im", P_BIM, (1, 0, 2)),
                               ("ssm_c_re", P_CRE, (2, 0, 1)), ("ssm_c_im", P_CIM, (2, 0, 1))):
            a = np.transpose(np.asarray(inp[key][l], np.float32), perm).reshape(64, 1024)
            s5p[l, :, off:off + 1024] = np.concatenate([a, a], 0)
    return vecs, s5p


_CACHE = {}
LAYERS_PER_LAUNCH = 4


def kernel(**inputs):
    inp = {k: np.asarray(v) for k, v in inputs.items()}
    x = inp["x"]
    B = x.shape[0]
    vecs, s5p = host_prepare(inp)
    consts = host_consts()
    npl = LAYERS_PER_LAUNCH
    if npl not in _CACHE:
        _CACHE[npl] = build(n_layers=npl)[0]
    nc = _CACHE[npl]
    wnames = ["w_in", "w_uq", "w_ukv", "w_o_mla", "w_glu", "w_out", "w_ffn_gate", "w_ffn_up", "w_ffn_down"]
    xT = [np.ascontiguousarray(x[b].T, dtype=np.float32) for b in range(B)]
    for l0 in range(0, L_DEPTH, npl):
        shared = {k: np.ascontiguousarray(inp[k][l0:l0 + npl], dtype=np.float32) for k in wnames}
        shared["vecs"] = np.ascontiguousarray(vecs[l0:l0 + npl])
        shared["s5p"] = np.ascontiguousarray(s5p[l0:l0 + npl])
        shared["consts"] = consts
        in_maps = []
        for b in range(B):
            mp = dict(shared)
            mp["xT"] = xT[b]
            mp["pos"] = np.ascontiguousarray(inp["positions"][b], dtype=np.int32)
            in_maps.append(mp)
        res = run_bass_kernel_spmd(nc, in_maps, core_ids=list(range(B)))
        xT = [np.ascontiguousarray(res.results[b]["outT"], dtype=np.float32) for b in range(B)]
    out = np.stack([np.ascontiguousarray(xT[b].T) for b in range(B)], axis=0)
    return out.astype(np.float32)
```

```python
import math
import numpy as np
import concourse.bass as bass
import concourse.mybir as mybir
from concourse.bass_utils import run_bass_kernel_spmd

F32 = mybir.dt.float32
BF16 = mybir.dt.bfloat16
I32 = mybir.dt.int32
AF = mybir.ActivationFunctionType
ALU = mybir.AluOpType

D = 2048
S_LEN = 4096
L_DEPTH = 4
TT = 1024
NT = S_LEN // TT
NH = 16
FF = 5632
IN_W = 5952
OFF_KV = 512
OFF_PE = 768
OFF_SSM = 832
OFF_GATE = 1856
EPS = 1e-6
SCALE = 192.0 ** -0.5
MAGIC = 12582912.0
TWO_PI = 2.0 * math.pi
C1 = 6.28125
C2 = TWO_PI - C1

V_PRE, V_QN, V_KVN, V_BG, V_BGLU, V_POSTMIX, V_PREF, V_POSTF, V_SD = 0, 16, 20, 22, 54, 86, 102, 118, 134
NVEC = 142
C_ID, C_PSW, C_MASK, C_SEL, C_ONES, C_SM = 0, 128, 256, 384, 384 + 2048, 384 + 2048 + 128
C_SEL2 = C_SM + 8
NCONST = C_SEL2 + 2048
P_AR, P_AI, P_LDT, P_BRE, P_BIM, P_CRE, P_CIM = 0, 64, 128, 192, 192 + 1024, 192 + 2048, 192 + 3072
NS5P = 192 + 4096


class Buf:
    __slots__ = ("name", "w", "r")

    def __init__(self, name=""):
        self.name = name
        self.w = {}
        self.r = {}


class Sched:
    ENGS = ("pe", "act", "dve", "pool", "sp")
    ROT = 30000
    RING = 20

    def __init__(self, nc):
        self.nc = nc
        self.ops = {e: [] for e in self.ENGS}
        self.known = {e: {} for e in self.ENGS}
        self.csem = {}
        self.ccnt = {}
        self._csem_idx = {}
        for e in self.ENGS:
            self._new_csem(e)
        self.ring = {}
        self.ring_val = {}
        self.ring_i = {}
        for q in ("sp", "pool"):
            self.ring[q] = [nc.alloc_semaphore(f"dq_{q}_{i}") for i in range(self.RING)]
            self.ring_val[q] = [0] * self.RING
            self.ring_i[q] = 0
        self.n_instr = {e: 0 for e in self.ENGS}

    def _new_csem(self, e):
        idx = self._csem_idx.get(e, 0)
        self._csem_idx[e] = idx + 1
        self.csem[e] = self.nc.alloc_semaphore(f"c_{e}_{idx}")
        self.ccnt[e] = 0

    def _collect(self, eng, reads, writes):
        deps = {}

        def add(t):
            key = id(t[1])
            if key not in deps or deps[key][2] < t[2]:
                deps[key] = t

        for b in reads:
            for t in b.w.values():
                add(t)
        for b in writes:
            for t in b.w.values():
                add(t)
            for t in b.r.values():
                if t[0] == eng and eng in ("pe", "act", "dve"):
                    continue
                add(t)
        waits = []
        kn = self.known[eng]
        for key, t in deps.items():
            if t[0] == eng and eng == "pe":
                continue
            if kn.get(key, 0) >= t[2]:
                continue
            kn[key] = t[2]
            waits.append((t[1], t[2]))
        return waits

    def _register(self, ticket, reads, writes):
        key = id(ticket[1])
        for b in writes:
            b.w = {key: ticket}
            b.r = {}
        for b in reads:
            b.r[key] = ticket

    def op(self, eng, fn, reads=(), writes=()):
        waits = self._collect(eng, reads, writes)
        if self.ccnt[eng] >= self.ROT:
            self._new_csem(eng)
        self.ccnt[eng] += 1
        sem = self.csem[eng]
        ticket = (eng, sem, self.ccnt[eng])
        self.ops[eng].append((waits, fn, sem, 1))
        self._register(ticket, reads, writes)
        return ticket

    def dma(self, q, out_ap, in_ap, reads=(), writes=(), **kw):
        i = self.ring_i[q]
        self.ring_i[q] = (i + 1) % self.RING
        sem = self.ring[q][i]
        prev = self.ring_val[q][i]
        waits = self._collect(q, reads, writes)
        kn = self.known[q]
        if prev > 0 and kn.get(id(sem), 0) < prev:
            kn[id(sem)] = prev
            waits.append((sem, prev))
        val = prev + 16
        self.ring_val[q][i] = val
        ticket = (q + "_dma", sem, val)

        def fn(e, out_ap=out_ap, in_ap=in_ap, kw=kw):
            return e.dma_start(out=out_ap, in_=in_ap, **kw)

        self.ops[q].append((waits, fn, sem, 16))
        self._register(ticket, reads, writes)
        return ticket

    def barrier(self):
        tickets = []
        for q in ("sp", "pool"):
            for i in range(self.RING):
                if self.ring_val[q][i] > 0:
                    tickets.append((q + "_dma", self.ring[q][i], self.ring_val[q][i]))
        for e in self.ENGS:
            if self.ccnt[e] > 0:
                tickets.append((e, self.csem[e], self.ccnt[e]))
        for e in self.ENGS:
            kn = self.known[e]
            waits = []
            for t in tickets:
                if t[0] == e and e == "pe":
                    continue
                key = id(t[1])
                if kn.get(key, 0) >= t[2]:
                    continue
                kn[key] = t[2]
                waits.append((t[1], t[2]))
            if waits:
                self.ops[e].append((waits, None, None, 0))

    def emit(self):
        nc = self.nc
        engmap = {"pe": "tensor", "act": "scalar", "dve": "vector", "pool": "gpsimd", "sp": "sync"}
        with nc.Block() as block:
            for e in self.ENGS:
                ops = self.ops[e]
                if not ops:
                    continue

                def body(eng, ops=ops, e=e):
                    n = 0
                    for (waits, fn, sem, inc) in ops:
                        expl = waits if fn is None else waits[:-1]
                        for (s, v) in expl:
                            eng.wait_ge(s, v)
                            n += 1
                        if fn is not None:
                            r = fn(eng)
                            first, last = r if isinstance(r, tuple) else (r, r)
                            if waits:
                                first._wait_ge(waits[-1][0], waits[-1][1])
                            last.then_inc(sem, inc)
                            n += 1
                    self.n_instr[e] = n

                getattr(block, engmap[e])(body)


class SBAlloc:
    BASE = 16512
    CAP = 196608

    def __init__(self, nc):
        self.nc = nc
        self.top = self.BASE
        self.n = 0

    def mark(self):
        return self.top

    def reset(self, m):
        self.top = m

    def alloc(self, shape, dtype):
        nb = 4 if dtype in (F32, I32) else 2
        n = 1
        for s in shape[1:]:
            n *= s
        off = (self.top + 63) // 64 * 64
        self.top = off + n * nb
        assert self.top <= self.CAP, f"SBUF overflow {self.top}"
        self.n += 1
        self.last_off = off
        return self.nc.alloc_sbuf_tensor_at(f"sb{self.n}", list(shape), dtype, offset=off)

    def at(self, shape, dtype, off):
        self.n += 1
        return self.nc.alloc_sbuf_tensor_at(f"sb{self.n}", list(shape), dtype, offset=off)


class Rot:
    def __init__(self, sb, n, shape, dtype, name="r"):
        self.items = [(sb.alloc(shape, dtype), Buf(f"{name}{i}")) for i in range(n)]
        self.i = 0

    def next(self):
        it = self.items[self.i]
        self.i = (self.i + 1) % len(self.items)
        return it


def build(n_layers=L_DEPTH, dump=False, phases="ABCDE"):
    nc = bass.Bass("TRN2", target_bir_lowering=False)
    S = Sched(nc)
    sb = SBAlloc(nc)

    def din(name, shape, dt=F32):
        return nc.dram_tensor(name, list(shape), dt, kind="ExternalInput").ap()

    def dscr(name, shape, dt):
        kind = "ExternalOutput" if dump else "Internal"
        return nc.dram_tensor(name, list(shape), dt, kind=kind).ap()

    xT_in = din("xT", [D, S_LEN])
    pos_in = din("pos", [S_LEN], I32)
    w_in = din("w_in", [n_layers, D, IN_W])
    w_uq = din("w_uq", [n_layers, 512, 3072])
    w_ukv = din("w_ukv", [n_layers, 256, 4096])
    w_o = din("w_o_mla", [n_layers, D, D])
    w_glu = din("w_glu", [n_layers, 1024, 4096])
    w_out = din("w_out", [n_layers, D, D])
    w_fg = din("w_ffn_gate", [n_layers, D, FF])
    w_fu = din("w_ffn_up", [n_layers, D, FF])
    w_fd = din("w_ffn_down", [n_layers, FF, D])
    vecs_in = din("vecs", [n_layers, 128, NVEC])
    s5p_in = din("s5p", [n_layers, 128, NS5P])
    consts_in = din("consts", [128, NCONST])
    outT = nc.dram_tensor("outT", [D, S_LEN], F32, kind="ExternalOutput").ap()

    XT = dscr("XT", [D, S_LEN], F32)
    YT = dscr("YT", [D, S_LEN], F32)
    QT = dscr("QT", [NH, 192, S_LEN], BF16)
    KT = dscr("KT", [NH, 128, S_LEN], BF16)
    KPE = dscr("KPE", [64, S_LEN], BF16)
    VS = dscr("VS", [S_LEN, D], BF16)
    UT = dscr("UT", [1024, S_LEN], BF16)
    GT = dscr("GT", [4096, S_LEN], BF16)
    OT = dscr("OT", [D, S_LEN], BF16)
    ZT = dscr("ZT", [1024, S_LEN], BF16)
    CS = dscr("CS", [2, 64, S_LEN], F32)

    ps = [nc.alloc_psum_tensor(f"ps{i}", [128, 512], F32) for i in range(8)]
    pb = [Buf(f"pb{i}") for i in range(8)]
    bank_i = [0]

    def bank(n=4):
        b = bank_i[0] % n
        bank_i[0] += 1
        return b

    def mm(out_ap, pairs, reads, writes, start=True, stop=True):
        def fn(e):
            n = len(pairs)
            ins = None
            first = None
            for i, (lh, rh) in enumerate(pairs):
                ins = e.matmul(out_ap, lhsT=lh, rhs=rh, start=(start and i == 0), stop=(stop and i == n - 1))
                if first is None:
                    first = ins
            return (first, ins)
        return S.op("pe", fn, reads, writes)

    def act(out, in_, func, reads, writes, **kw):
        return S.op("act", lambda e: e.activation(out=out, in_=in_, func=func, **kw), reads, writes)

    def tt(out, a, b, op, reads, writes, eng="dve"):
        return S.op(eng, lambda e: e.tensor_tensor(out=out, in0=a, in1=b, op=op), reads, writes)

    def ts(out, a, s1, s2, op0, op1, reads, writes, eng="dve"):
        if op1 is None:
            return S.op(eng, lambda e: e.tensor_scalar(out=out, in0=a, scalar1=s1, scalar2=None, op0=op0), reads, writes)
        return S.op(eng, lambda e: e.tensor_scalar(out=out, in0=a, scalar1=s1, scalar2=s2, op0=op0, op1=op1), reads, writes)

    def stt(out, a, sc, b, op0, op1, reads, writes):
        return S.op("dve", lambda e: e.scalar_tensor_tensor(out=out, in0=a, scalar=sc, in1=b, op0=op0, op1=op1), reads, writes)

    def recip(out, in_, reads, writes):
        return S.op("dve", lambda e: e.reciprocal(out=out, in_=in_), reads, writes)

    def wload(wt, bwt, W2d, c0, ncols, kcn, col_dst=0):
        src = W2d[:, c0:c0 + ncols].rearrange("(kc p) n -> p kc n", p=128)
        S.dma("pool", wt[:, 0:kcn, col_dst:col_dst + ncols], src, writes=[bwt])

    cst = sb.alloc([128, C_SEL], F32)
    csm = sb.alloc([128, 8], F32)
    onesb = sb.alloc([128, 128], BF16)
    bcst = Buf("cst")
    S.dma("sp", cst[:], consts_in[:, 0:C_SEL], writes=[bcst])
    S.dma("sp", csm[:], consts_in[:, C_SM:C_SM + 8], writes=[bcst])
    S.dma("pool", onesb[:], consts_in[:, C_ONES:C_ONES + 128], writes=[bcst])

    ident = cst[:, C_ID:C_ID + 128]
    pswap = cst[:, C_PSW:C_PSW + 128]
    maskT = cst[:, C_MASK:C_MASK + 128]
    SELRE, SELIM, NSELRE, NSELIM, SGNROPE, INVF = 0, 1, 2, 3, 4, 5
    vecs = sb.alloc([128, NVEC], F32)
    bvec = Buf("vecs")
    persist_mark = sb.mark()

    def rope_tables():
        m = sb.mark()
        posi = sb.alloc([64, S_LEN], I32)
        ang = sb.alloc([64, S_LEN], F32)
        t1 = sb.alloc([64, S_LEN], F32)
        t2 = sb.alloc([64, S_LEN], F32)
        bp, ba, b1, b2 = Buf(), Buf(), Buf(), Buf()
        S.dma("sp", posi[:], pos_in.partition_broadcast(64), writes=[bp])
        S.op("dve", lambda e: e.tensor_copy(out=ang[:], in_=posi[:]), [bp], [ba])
        ts(ang[:], ang[:], csm[0:64, INVF:INVF + 1], None, ALU.mult, None, [ba, bcst], [ba])

        def reduce_sin(shift, dst_idx, signed):
            if shift != 0.0:
                ts(t1[:], ang[:], shift, None, ALU.add, None, [ba], [b1])
                srcang, bsrc = t1, b1
            else:
                srcang, bsrc = ang, ba
            ts(t2[:], srcang[:], 1.0 / TWO_PI, MAGIC, ALU.mult, ALU.add, [bsrc], [b2])
            ts(t2[:], t2[:], MAGIC, None, ALU.subtract, None, [b2], [b2])
            if shift == 0.0:
                stt(t1[:], t2[:], -C1, ang[:], ALU.mult, ALU.add, [b2, ba], [b1])
            else:
                stt(t1[:], t2[:], -C1, t1[:], ALU.mult, ALU.add, [b2, b1], [b1])
            stt(t1[:], t2[:], -C2, t1[:], ALU.mult, ALU.add, [b2, b1], [b1])
            ts(t1[:], t1[:], math.pi, -math.pi, ALU.min, ALU.max, [b1], [b1])
            act(t2[:], t1[:], AF.Sin, [b1], [b2])
            if signed:
                ts(t2[:], t2[:], csm[0:64, SGNROPE:SGNROPE + 1], None, ALU.mult, None, [b2, bcst], [b2])
            S.dma("sp", CS[dst_idx], t2[:], reads=[b2])

        reduce_sin(math.pi / 2.0, 0, False)
        reduce_sin(0.0, 1, True)
        S.barrier()
        sb.reset(m)

    def rstd_from(banks, out_tile, bout, nfeat, extra_scale=1.0):
        es2 = extra_scale * extra_scale
        for sub, bk in enumerate(banks):
            act(out_tile[:, sub * 512:(sub + 1) * 512], ps[bk][:, :], AF.Sqrt, [pb[bk]], [bout],
                scale=1.0 / (nfeat * es2), bias=EPS / es2)
        recip(out_tile[:], out_tile[:], [bout], [bout])

    def make_hT(src, t0, vcol, hT, bh, xrot, sqrot, rstd, brstd):
        for kc in range(16):
            xc, bx = xrot.next()
            S.dma("sp", xc[:], src[kc * 128:(kc + 1) * 128, t0:t0 + TT], writes=[bx])
            for sub in range(2):
                sq, bq = sqrot.next()
                act(sq[:], xc[:, sub * 512:(sub + 1) * 512], AF.Square, [bx], [bq])
                mm(ps[6 + sub][:, :], [(onesb[:], sq[:])], [bq, bcst], [pb[6 + sub]], start=(kc == 0), stop=(kc == 15))
        rstd_from([6, 7], rstd, brstd, D)
        for kc in range(16):
            xc, bx = xrot.next()
            S.dma("sp", xc[:], src[kc * 128:(kc + 1) * 128, t0:t0 + TT], writes=[bx])
            stt(hT[:, kc, :], xc[:], vecs[:, vcol + kc:vcol + kc + 1], rstd[:], ALU.mult, ALU.mult,
                [bx, brstd, bvec], [bh[kc]])

    def postnorm_steps(t0, ysrc, byt, xsrc, dst, vcol, rstd, brstd, yrot, xrot):
        return [(lambda n=n: postnorm_one(n, t0, ysrc, byt, xsrc, dst, vcol, rstd, brstd, yrot, xrot)) for n in range(16)]

    def postnorm_one(n, t0, ysrc, byt, xsrc, dst, vcol, rstd, brstd, yrot, xrot):
        if True:
            yc, by = yrot.next()
            xc, bx = xrot.next()
            S.dma("sp", yc[:], ysrc[n * 128:(n + 1) * 128, t0:t0 + TT], reads=[byt[n]], writes=[by])
            S.dma("sp", xc[:], xsrc[n * 128:(n + 1) * 128, t0:t0 + TT], writes=[bx])
            stt(yc[:], yc[:], vecs[:, vcol + n:vcol + n + 1], rstd[:], ALU.mult, ALU.mult, [by, brstd, bvec], [by])
            tt(yc[:], yc[:], xc[:], ALU.add, [by, bx], [by])
            S.dma("sp", dst[n * 128:(n + 1) * 128, t0:t0 + TT], yc[:], reads=[by])

    def phase_A(l):
        m = sb.mark()
        src = xT_in if l == 0 else XT
        wuq = sb.alloc([128, 4, 3072], BF16)
        wuqr = sb.alloc([128, 4, 16, 64], BF16)
        wukv = sb.alloc([128, 2, 4096], BF16)
        bw = Buf("wres")
        for c in range(3):
            wload(wuq, bw, w_uq[l], c * 1024, 1024, 4, col_dst=c * 1024)
        for c in range(4):
            wload(wukv, bw, w_ukv[l], c * 1024, 1024, 2, col_dst=c * 1024)
        uq4 = w_uq[l].rearrange("(kc p) (h e) -> kc p h e", p=128, e=192)
        for kc in range(4):
            S.dma("pool", wuqr[:, kc, :, 0:32], uq4[kc][:, :, 160:192], writes=[bw])
            S.dma("pool", wuqr[:, kc, :, 32:64], uq4[kc][:, :, 128:160], writes=[bw])
        hT = sb.alloc([128, 16, TT], BF16)
        bh = [Buf(f"h{k}") for k in range(16)]
        xrot = Rot(sb, 3, [128, TT], F32, "x")
        sqrot = Rot(sb, 3, [128, 512], BF16, "sq")
        rstd = sb.alloc([128, TT], F32)
        brstd = Buf("rstd")
        rstdq = sb.alloc([128, TT], F32)
        brq = Buf("rstdq")
        rstdkv = sb.alloc([128, TT], F32)
        brkv = Buf("rstdkv")
        rkt = sb.alloc([128, 8], F32)
        brkt = Buf("rkt")
        cqw = sb.alloc([128, 4, TT], BF16)
        bcq = Buf("cqw")
        ckvw = sb.alloc([128, 2, TT], BF16)
        bckv = Buf("ckvw")
        sqkv = sb.alloc([128, 2, TT], BF16)
        bsqkv = Buf("sqkv")
        cst_ = sb.alloc([64, 2, TT], F32)
        bcs = Buf("cs")
        wrot = Rot(sb, 3, [128, 16, 128], BF16, "w")
        strot = Rot(sb, 4, [128, 512], BF16, "st")
        tmrot = Rot(sb, 4, [64, 512], F32, "tm")

        def proj(c0, M, evac, rot_cols=None):
            wt, bwt = wrot.next()
            if rot_cols is None:
                wload(wt, bwt, w_in[l], c0, M, 16)
            else:
                wload(wt, bwt, w_in[l], rot_cols[0], 32, 16, col_dst=0)
                wload(wt, bwt, w_in[l], rot_cols[1], 32, 16, col_dst=32)
            for sub in range(2):
                b = bank()
                mm(ps[b][0:M, :], [(wt[:, kc, 0:M], hT[:, kc, sub * 512:(sub + 1) * 512]) for kc in range(16)],
                   [bwt] + bh, [pb[b]])
                evac(b, sub)

        make_hT(src, 0, V_PRE, hT, bh, xrot, sqrot, rstd, brstd)
        for tti in range(NT):
            t0 = tti * TT
            S.dma("sp", cst_[:, 0, :], CS[0][:, t0:t0 + TT], writes=[bcs])
            S.dma("sp", cst_[:, 1, :], CS[1][:, t0:t0 + TT], writes=[bcs])

            for j in range(4):
                def ev(b, sub, j=j):
                    sq, bq = sqrot.next()
                    act(sq[:], ps[b][:, :], AF.Square, [pb[b]], [bq])
                    act(cqw[:, j, sub * 512:(sub + 1) * 512], ps[b][:, :], AF.Copy, [pb[b], bvec], [bcq],
                        scale=vecs[:, V_QN + j:V_QN + j + 1])
                    mm(ps[6 + sub][:, :], [(onesb[:], sq[:])], [bq, bcst], [pb[6 + sub]], start=(j == 0), stop=(j == 3))
                proj(j * 128, 128, ev)
            rstd_from([6, 7], rstdq, brq, 512, extra_scale=SCALE)
            for j in range(2):
                def ev(b, sub, j=j):
                    act(sqkv[:, j, sub * 512:(sub + 1) * 512], ps[b][:, :], AF.Square, [pb[b]], [bsqkv])
                    act(ckvw[:, j, sub * 512:(sub + 1) * 512], ps[b][:, :], AF.Copy, [pb[b], bvec], [bckv],
                        scale=vecs[:, V_KVN + j:V_KVN + j + 1])
                    mm(ps[4 + sub][:, :], [(onesb[:], sqkv[:, j, sub * 512:(sub + 1) * 512])], [bsqkv, bcst], [pb[4 + sub]],
                       start=(j == 0), stop=(j == 1))
                proj(OFF_KV + j * 128, 128, ev)
            rstd_from([4, 5], rstdkv, brkv, 256)
            b = bank()
            for blk in range(8):
                mm(ps[b][:, blk:blk + 1], [(sqkv[:, j, blk * 128:(blk + 1) * 128], onesb[:, 0:1]) for j in range(2)],
                   [bsqkv, bcst], [pb[b]])
            act(rkt[:], ps[b][:, 0:8], AF.Sqrt, [pb[b]], [brkt], scale=1.0 / 256.0, bias=EPS)
            recip(rkt[:], rkt[:], [brkt], [brkt])
            pe_hold = {}

            def ev_pe(b, sub):
                ta, bta = tmrot.next()
                tt(ta[:], ps[b][0:64, :], cst_[:, 0, sub * 512:(sub + 1) * 512], ALU.mult, [pb[b], bcs], [bta])
                pe_hold[sub] = (ta, bta)

            def ev_rot(b, sub):
                ta, bta = pe_hold[sub]
                tb, btb = tmrot.next()
                tt(tb[:], ps[b][0:64, :], cst_[:, 1, sub * 512:(sub + 1) * 512], ALU.mult, [pb[b], bcs], [btb])
                st, bst = strot.next()
                tt(st[0:64, :], ta[:], tb[:], ALU.add, [bta, btb], [bst])
                S.dma("sp", KPE[:, t0 + sub * 512:t0 + (sub + 1) * 512], st[0:64, :], reads=[bst])
            proj(OFF_PE, 64, ev_pe)
            proj(OFF_PE, 64, ev_rot, rot_cols=(OFF_PE + 32, OFF_PE))
            for j in range(8):
                def ev(b, sub, j=j):
                    st, bst = strot.next()
                    act(st[:], ps[b][:, :], AF.Copy, [pb[b]], [bst])
                    S.dma("sp", UT[j * 128:(j + 1) * 128, t0 + sub * 512:t0 + (sub + 1) * 512], st[:], reads=[bst])
                proj(OFF_SSM + j * 128, 128, ev)
            for j in range(32):
                def ev(b, sub, j=j):
                    st, bst = strot.next()
                    act(st[:], ps[b][:, :], AF.Sigmoid, [pb[b], bvec], [bst], bias=vecs[:, V_BG + j:V_BG + j + 1])
                    S.dma("sp", GT[j * 128:(j + 1) * 128, t0 + sub * 512:t0 + (sub + 1) * 512], st[:], reads=[bst])
                proj(OFF_GATE + j * 128, 128, ev)
            if tti + 1 < NT:
                make_hT(src, t0 + TT, V_PRE, hT, bh, xrot, sqrot, rstd, brstd)
            for h in range(NH):
                for sub in range(2):
                    cs_ = slice(sub * 512, (sub + 1) * 512)
                    tok = slice(t0 + sub * 512, t0 + (sub + 1) * 512)
                    b = bank()
                    mm(ps[b][:, :], [(wuq[:, kc, h * 192:h * 192 + 128], cqw[:, kc, cs_]) for kc in range(4)], [bw, bcq], [pb[b]])
                    st, bst = strot.next()
                    tt(st[:], ps[b][:, :], rstdq[:, cs_], ALU.mult, [pb[b], brq], [bst])
                    S.dma("sp", QT[h, 0:128, tok], st[:], reads=[bst])
                    b1 = bank()
                    mm(ps[b1][0:64, :], [(wuq[:, kc, h * 192 + 128:h * 192 + 192], cqw[:, kc, cs_]) for kc in range(4)], [bw, bcq], [pb[b1]])
                    b2 = bank()
                    mm(ps[b2][0:64, :], [(wuqr[:, kc, h, :], cqw[:, kc, cs_]) for kc in range(4)], [bw, bcq], [pb[b2]])
                    ta, bta = tmrot.next()
                    tb, btb = tmrot.next()
                    tt(ta[:], ps[b1][0:64, :], cst_[:, 0, cs_], ALU.mult, [pb[b1], bcs], [bta])
                    tt(tb[:], ps[b2][0:64, :], cst_[:, 1, cs_], ALU.mult, [pb[b2], bcs], [btb])
                    tt(ta[:], ta[:], tb[:], ALU.add, [bta, btb], [bta])
                    st2, bst2 = strot.next()
                    tt(st2[0:64, :], ta[:], rstdq[0:64, cs_], ALU.mult, [bta, brq], [bst2])
                    S.dma("sp", QT[h, 128:192, tok], st2[0:64, :], reads=[bst2])
            for h in range(NH):
                for sub in range(2):
                    cs_ = slice(sub * 512, (sub + 1) * 512)
                    tok = slice(t0 + sub * 512, t0 + (sub + 1) * 512)
                    b = bank()
                    mm(ps[b][:, :], [(wukv[:, kc, h * 256:h * 256 + 128], ckvw[:, kc, cs_]) for kc in range(2)], [bw, bckv], [pb[b]])
                    st, bst = strot.next()
                    tt(st[:], ps[b][:, :], rstdkv[:, cs_], ALU.mult, [pb[b], brkv], [bst])
                    S.dma("sp", KT[h, :, tok], st[:], reads=[bst])
            wv = wukv[:, :, :].rearrange("p k (h e) -> p k h e", e=256)
            for blk in range(8):
                for cg in range(4):
                    b = bank()
                    mm(ps[b][:, :].rearrange("p (h e) -> p h e", e=128),
                       [(ckvw[:, kc, blk * 128:(blk + 1) * 128], wv[:, kc, 4 * cg:4 * cg + 4, 128:256]) for kc in range(2)],
                       [bw, bckv], [pb[b]])
                    st, bst = strot.next()
                    act(st[:], ps[b][:, :], AF.Copy, [pb[b], brkt], [bst], scale=rkt[:, blk:blk + 1])
                    S.dma("sp", VS[t0 + blk * 128:t0 + (blk + 1) * 128, cg * 512:(cg + 1) * 512], st[:], reads=[bst])
        S.barrier()
        sb.reset(m)

    def phase_B(l):
        m = sb.mark()
        kpe = sb.alloc([64, S_LEN], BF16)
        bkpe = Buf("kpe")
        S.dma("sp", kpe[:], KPE[:, :], writes=[bkpe])
        hb = []
        for i in range(2):
            hb.append(dict(
                qn=sb.alloc([128, S_LEN], BF16), qp=sb.alloc([64, S_LEN], BF16), kn=sb.alloc([128, S_LEN], BF16),
                v=sb.alloc([128, 32, 128], BF16), o=sb.alloc([128, S_LEN], BF16),
                bin=Buf(f"hin{i}"), bo=Buf(f"ho{i}")))
        ptrot = Rot(sb, 6, [128, 512], BF16, "pt")
        rcrot = Rot(sb, 2, [128, 512], F32, "rc")
        st_i = [0]
        for h in range(NH):
            B = hb[h % 2]
            S.dma("sp", B["qn"][:], QT[h, 0:128, :], writes=[B["bin"]])
            S.dma("sp", B["qp"][:], QT[h, 128:192, :], writes=[B["bin"]])
            S.dma("sp", B["kn"][:], KT[h, :, :], writes=[B["bin"]])
            S.dma("sp", B["v"][:], VS[:, h * 128:(h + 1) * 128].rearrange("(b p) e -> p b e", p=128), writes=[B["bin"]])
            pairs = []
            for qt in range(8):
                nkb = 4 * qt + 4
                for kb in range(nkb):
                    pairs.append((qt, kb, nkb))
            held = {}

            def emit_qk(i):
                qt, kb, nkb = pairs[i]
                d = kb - 4 * qt
                c0 = 0 if d < 0 else d * 128
                qs = slice(qt * 512 + c0, (qt + 1) * 512)
                ks = slice(kb * 128, (kb + 1) * 128)
                sbk = (0, 1, 6, 7)[st_i[0] % 4]
                st_i[0] += 1
                mm(ps[sbk][:, c0:512], [(B["kn"][:, ks], B["qn"][:, qs]), (kpe[:, ks], B["qp"][:, qs])],
                   [B["bin"], bkpe], [pb[sbk]])
                pt, bpt = ptrot.next()
                act(pt[:, c0:512], ps[sbk][:, c0:512], AF.Exp, [pb[sbk]], [bpt])
                if d >= 0:
                    S.op("pool", lambda e, pt=pt, c0=c0: e.memset(pt[64:128, c0:c0 + 64], 0.0), [], [bpt])
                held[i] = (pt, bpt, c0)

            def emit_pv(i):
                qt, kb, nkb = pairs[i]
                pt, bpt, c0 = held.pop(i)
                ob = 2 + (qt % 2)
                lb = 4 + (qt % 2)
                mm(ps[ob][:, c0:512], [(B["v"][:, kb, :], pt[:, c0:512])], [B["bin"], bpt], [pb[ob]],
                   start=(kb == 0), stop=(kb == nkb - 1))
                mm(ps[lb][:, c0:512], [(onesb[:], pt[:, c0:512])], [bcst, bpt], [pb[lb]],
                   start=(kb == 0), stop=(kb == nkb - 1))
                if kb == nkb - 1:
                    rc, brc = rcrot.next()
                    recip(rc[:], ps[lb][:, :], [pb[lb]], [brc])
                    tt(B["o"][:, qt * 512:(qt + 1) * 512], ps[ob][:, :], rc[:], ALU.mult, [pb[ob], brc], [B["bo"]])

            PD = 2
            for i in range(len(pairs) + PD):
                if i < len(pairs):
                    emit_qk(i)
                if i >= PD:
                    emit_pv(i - PD)
            S.dma("sp", OT[h * 128:(h + 1) * 128, :], B["o"][:], reads=[B["bo"]])
        S.barrier()
        sb.reset(m)

    def phase_C(l):
        m = sb.mark()
        selb = sb.alloc([128, 2048], BF16)
        selb2 = sb.alloc([128, 2048], BF16)
        bsel = Buf("sel")
        for (t_, c_) in ((selb, C_SEL), (selb2, C_SEL2)):
            S.dma("pool", t_[:, 0:1024], consts_in[:, c_:c_ + 1024], writes=[bsel])
            S.dma("pool", t_[:, 1024:2048], consts_in[:, c_ + 1024:c_ + 2048], writes=[bsel])

        def selrows(q4):
            if q4 < 3:
                return selb, slice(32 * q4, 32 * q4 + 32)
            return selb2, slice(64, 128)
        prm = sb.alloc([128, 192], F32)
        cre = sb.alloc([128, 64, 16], F32)
        cim = sb.alloc([128, 64, 16], F32)
        bre = sb.alloc([128, 64, 16], F32)
        mk2_off = sb.last_off
        bim = sb.alloc([128, 64, 16], F32)
        bprm = Buf("prm")
        S.dma("sp", prm[:], s5p_in[l][:, 0:192], writes=[bprm])
        for t_, off in ((bre, P_BRE), (bim, P_BIM), (cre, P_CRE), (cim, P_CIM)):
            S.dma("sp", t_[:].rearrange("p g q -> p (g q)"), s5p_in[l][:, off:off + 1024], writes=[bprm])
        NW = 40
        wk = sb.alloc([128, NW, 64], F32)
        bwk = Buf("wk")
        W = lambda i: wk[:, i, :]
        R = [bwk, bprm, bcst]
        ar, ai, ldt = prm[:, P_AR:P_AR + 64], prm[:, P_AI:P_AI + 64], prm[:, P_LDT:P_LDT + 64]
        DT, ARD, AID, MAG, T1, T2, SINP, COSP, ABR, ABI, DEN, NR, FRE, FIM, IR, II = range(16)
        act(W(DT), ldt, AF.Exp, R, [bwk])
        tt(W(ARD), ar, W(DT), ALU.mult, R, [bwk])
        tt(W(AID), ai, W(DT), ALU.mult, R, [bwk])
        act(W(MAG), W(ARD), AF.Exp, R, [bwk])

        def red_sin(dst, shift):
            ts(W(T1), W(AID), shift, None, ALU.add, None, R, [bwk])
            ts(W(T2), W(T1), 1.0 / TWO_PI, MAGIC, ALU.mult, ALU.add, R, [bwk])
            ts(W(T2), W(T2), MAGIC, None, ALU.subtract, None, R, [bwk])
            stt(W(T1), W(T2), -C1, W(T1), ALU.mult, ALU.add, R, [bwk])
            stt(W(T1), W(T2), -C2, W(T1), ALU.mult, ALU.add, R, [bwk])
            ts(W(T1), W(T1), math.pi, -math.pi, ALU.min, ALU.max, R, [bwk])
            act(W(dst), W(T1), AF.Sin, R, [bwk])
        red_sin(SINP, 0.0)
        red_sin(COSP, math.pi / 2.0)
        tt(W(ABR), W(MAG), W(COSP), ALU.mult, R, [bwk])
        tt(W(ABI), W(MAG), W(SINP), ALU.mult, R, [bwk])
        tt(W(T1), ar, ar, ALU.mult, R, [bwk])
        tt(W(T2), ai, ai, ALU.mult, R, [bwk])
        tt(W(DEN), W(T1), W(T2), ALU.add, R, [bwk])
        recip(W(DEN), W(DEN), R, [bwk])
        ts(W(NR), W(ABR), -1.0, None, ALU.add, None, R, [bwk])
        tt(W(T1), W(NR), ar, ALU.mult, R, [bwk])
        tt(W(T2), W(ABI), ai, ALU.mult, R, [bwk])
        tt(W(T1), W(T1), W(T2), ALU.add, R, [bwk])
        tt(W(FRE), W(T1), W(DEN), ALU.mult, R, [bwk])
        tt(W(T1), W(ABI), ar, ALU.mult, R, [bwk])
        tt(W(T2), W(NR), ai, ALU.mult, R, [bwk])
        tt(W(T1), W(T1), W(T2), ALU.subtract, R, [bwk])
        tt(W(FIM), W(T1), W(DEN), ALU.mult, R, [bwk])
        tt(W(T1), W(ABR), W(ABR), ALU.mult, R, [bwk])
        tt(W(T2), W(ABI), W(ABI), ALU.mult, R, [bwk])
        tt(W(T1), W(T1), W(T2), ALU.add, R, [bwk])
        recip(W(T1), W(T1), R, [bwk])
        tt(W(IR), W(ABR), W(T1), ALU.mult, R, [bwk])
        tt(W(T2), W(ABI), W(T1), ALU.mult, R, [bwk])
        ts(W(II), W(T2), -1.0, None, ALU.mult, None, R, [bwk])
        pw = sb.alloc([128, 16, 2, 64], F32)
        bpw = Buf("pw")
        RP = [bwk, bpw, bcst]
        PR = lambda m_: pw[:, m_ + 7, 0, :]
        PI = lambda m_: pw[:, m_ + 7, 1, :]
        S.op("dve", lambda e: e.memset(PR(0), 1.0), RP, [bpw])
        S.op("dve", lambda e: e.memset(PI(0), 0.0), RP, [bpw])

        def cmul(dr, di, xr, xi, yr, yi):
            tt(W(T1), xr, yr, ALU.mult, RP, [bwk])
            tt(W(T2), xi, yi, ALU.mult, RP, [bwk])
            tt(W(16), xr, yi, ALU.mult, RP, [bwk])
            tt(W(17), xi, yr, ALU.mult, RP, [bwk])
            tt(dr, W(T1), W(T2), ALU.subtract, RP, [bpw])
            tt(di, W(16), W(17), ALU.add, RP, [bpw])
        for m_ in range(1, 9):
            cmul(PR(m_), PI(m_), PR(m_ - 1), PI(m_ - 1), W(ABR), W(ABI))
        for m_ in range(-1, -8, -1):
            cmul(PR(m_), PI(m_), PR(m_ + 1), PI(m_ + 1), W(IR), W(II))
        dp = sb.alloc([128, 9, 2, 64], F32)
        S.op("dve", lambda e: e.tensor_copy(out=dp[:, 0, 0, :], in_=PR(8)), RP, [bpw])
        S.op("dve", lambda e: e.tensor_copy(out=dp[:, 0, 1, :], in_=PI(8)), RP, [bpw])
        for k in range(1, 9):
            cmul(dp[:, k, 0, :], dp[:, k, 1, :], dp[:, k - 1, 0, :], dp[:, k - 1, 1, :], dp[:, k - 1, 0, :], dp[:, k - 1, 1, :])
        dps = sb.alloc([128, 9, 64], F32)
        sg = sb.alloc([128, 1], F32)
        tt(sg[:], csm[:, SELRE:SELRE + 1], csm[:, SELIM:SELIM + 1], ALU.subtract, RP, [bpw])
        for k in range(9):
            ts(dps[:, k, :], dp[:, k, 1, :], sg[:, 0:1], None, ALU.mult, None, RP, [bpw])
        E1 = sb.alloc([128, 9, 64], F32)
        E2 = sb.alloc([128, 9, 64], F32)
        for m_ in range(9):
            ts(W(T1), PR(m_), csm[:, SELRE:SELRE + 1], None, ALU.mult, None, RP, [bwk])
            stt(E1[:, m_, :], PI(m_), csm[:, NSELIM:NSELIM + 1], W(T1), ALU.mult, ALU.add, RP, [bpw])
            ts(W(T1), PI(m_), csm[:, NSELRE:NSELRE + 1], None, ALU.mult, None, RP, [bwk])
            stt(E2[:, m_, :], PR(m_), csm[:, NSELIM:NSELIM + 1], W(T1), ALU.mult, ALU.add, RP, [bpw])
        B1 = sb.alloc([128, 64, 16], F32)
        B2 = sb.alloc([128, 64, 16], F32)
        m_alias = sb.mark()
        bbr = sb.alloc([128, 64, 16], F32)
        bbi = sb.alloc([128, 64, 16], F32)
        tq = sb.alloc([128, 64, 16], F32)
        bbb = Buf("bb")
        RB = [bwk, bpw, bprm, bbb, bcst]
        fre_b = W(FRE).unsqueeze(2).broadcast_to([128, 64, 16])
        fim_b = W(FIM).unsqueeze(2).broadcast_to([128, 64, 16])
        tt(bbr[:], bre[:], fre_b, ALU.mult, RB, [bbb])
        tt(tq[:], bim[:], fim_b, ALU.mult, RB, [bbb])
        tt(bbr[:], bbr[:], tq[:], ALU.subtract, RB, [bbb])
        tt(bbi[:], bim[:], fre_b, ALU.mult, RB, [bbb])
        tt(tq[:], bre[:], fim_b, ALU.mult, RB, [bbb])
        tt(bbi[:], bbi[:], tq[:], ALU.add, RB, [bbb])
        f2 = lambda t_: t_[:].rearrange("p g q -> p (g q)")
        ts(f2(tq), f2(bbr), csm[:, SELRE:SELRE + 1], None, ALU.mult, None, RB, [bbb])
        stt(f2(B1), f2(bbi), csm[:, SELIM:SELIM + 1], f2(tq), ALU.mult, ALU.add, RB, [bbb])
        ts(f2(tq), f2(bbi), csm[:, NSELRE:NSELRE + 1], None, ALU.mult, None, RB, [bbb])
        stt(f2(B2), f2(bbr), csm[:, SELIM:SELIM + 1], f2(tq), ALU.mult, ALU.add, RB, [bbb])

        sb.reset(m_alias)
        LM = sb.alloc([128, 8, 8, 16], F32)
        WTt = sb.alloc([128, 8, 8, 16], F32)
        RR = sb.alloc([128, 8, 8, 16], F32)
        GGb = sb.alloc([128, 8, 8, 16], BF16)
        tq2 = sb.alloc([128, 8, 16], F32)
        tq3 = sb.alloc([128, 8, 16], F32)
        bbig = Buf("big")
        Tg = sb.alloc([128, 8, 128], BF16)
        Wg = sb.alloc([128, 8, 128], BF16)
        MkB = [sb.alloc([128, 8, 9, 128], BF16), sb.at([128, 8, 9, 128], BF16, mk2_off)]
        bmk = [Buf("mk0"), Buf("mk1")]
        tmrot2 = Rot(sb, 2, [128, 128], F32, "tmM")
        bmat = Buf("mat")

        def build_Mk(bt_, g, k):
            gg = bt_ * 8 + g
            tmpM, btm = tmrot2.next()
            ts(tmpM[:], ident, dp[:, k, 0, gg:gg + 1], None, ALU.mult, None, [bpw, bcst], [btm])
            stt(MkB[bt_ % 2][:, g, k, :], pswap, dps[:, k, gg:gg + 1], tmpM[:], ALU.mult, ALU.add, [bpw, bcst, btm], [bmk[bt_ % 2]])
        uT = sb.alloc([128, S_LEN], BF16)
        buT = Buf("uT")
        yT = sb.alloc([128, S_LEN], F32)
        byT = Buf("yT")
        zT = sb.alloc([128, S_LEN], BF16)
        bzT = Buf("zT")
        gt1 = sb.alloc([128, S_LEN // 2], F32)
        bg1 = Buf("g1")
        Uf = sb.alloc([128, 8, 512], BF16)
        Xs = sb.alloc([128, 8, 512], BF16)
        Yf = sb.alloc([128, 8, 512], BF16)
        bUf = [Buf(f"uf{g}") for g in range(8)]
        bXs = [Buf(f"xs{g}") for g in range(8)]
        bYf = [Buf(f"yf{g}") for g in range(8)]
        RG = [bwk, bpw, bbb, bprm, bcst, bbig]
        for bt in range(8):
            g0 = bt * 8
            S.dma("sp", uT[:], UT[bt * 128:(bt + 1) * 128, :], writes=[buT])
            if bt == 0:
                for g in range(8):
                    for k in range(9):
                        build_Mk(0, g, k)
            for g in range(8):
                q4, half = (g // 2), (g % 2)
                selx, rows = selrows(q4)
                b = 4 + bank()
                mm(ps[b][:, :], [(selx[rows, (half * 8 + j) * 128:(half * 8 + j + 1) * 128], uT[rows, j::8]) for j in range(8)],
                   [bsel, buT], [pb[b]])
                act(Uf[:, g, :], ps[b][:, :], AF.Copy, [pb[b]], [bUf[g]])

            def bc(ap2):
                return ap2.unsqueeze(2).broadcast_to([128, 8, 16])
            for j in range(8):
                for (dst, m_, X1, X2, pr_, pi_) in (
                    (LM, -j, B1, B2, PR, PI), (WTt, 7 - j, B1, B2, PR, PI)):
                    tt(tq2[:], X1[:, g0:g0 + 8, :], bc(pr_(m_)[:, g0:g0 + 8]), ALU.mult, RG, [bbig])
                    tt(dst[:, :, j, :], X2[:, g0:g0 + 8, :], bc(pi_(m_)[:, g0:g0 + 8]), ALU.mult, RG, [bbig])
                    tt(dst[:, :, j, :], dst[:, :, j, :], tq2[:], ALU.add, RG, [bbig])
                for (dst, m_) in ((RR, j), (GGb, j + 1)):
                    tt(tq2[:], cre[:, g0:g0 + 8, :], bc(E1[:, m_, g0:g0 + 8]), ALU.mult, RG, [bbig])
                    tt(tq3[:], cim[:, g0:g0 + 8, :], bc(E2[:, m_, g0:g0 + 8]), ALU.mult, RG, [bbig])
                    tt(dst[:, :, j, :], tq3[:], tq2[:], ALU.add, RG, [bbig])
            for g in range(8):
                gg = g0 + g
                b = bank()
                mm(ps[b][:, 0:128], [(LM[:, g, :, :].rearrange("p j q -> p (j q)"), RR[:, g, :, :].rearrange("p i q -> p (i q)"))],
                   [bbig], [pb[b]])
                tt(Tg[:, g, :], ps[b][:, 0:128], maskT, ALU.mult, [pb[b], bcst], [bmat])
                b = bank()
                S.op("pe", lambda e, b=b, g=g: e.transpose(ps[b][:, 0:128], WTt[:, g, :, :].rearrange("p j q -> p (j q)"), ident),
                     [bbig, bcst], [pb[b]])
                act(Wg[:, g, :], ps[b][:, 0:128], AF.Copy, [pb[b]], [bmat])
            for g in range(8):
                b = 4 + bank()
                mm(ps[b][:, :], [(Wg[:, g, :], Uf[:, g, :])], [bmat, bUf[g]], [pb[b]])
                act(Xs[:, g, :], ps[b][:, :], AF.Copy, [pb[b]], [bXs[g]])
            for k in range(9):
                sh = 1 << k
                for g in range(8):
                    mm(ps[g][:, sh:512], [(MkB[bt % 2][:, g, k, :], Xs[:, g, 0:512 - sh])], [bmk[bt % 2], bXs[g]], [pb[g]])
                for g in range(8):
                    tt(Xs[:, g, sh:512], Xs[:, g, sh:512], ps[g][:, sh:512], ALU.add, [pb[g], bXs[g]], [bXs[g]])
                if bt + 1 < 8:
                    for g in range(8):
                        build_Mk(bt + 1, g, k)
            for g in range(8):
                b = 4 + bank()
                mm(ps[b][:, :], [(Tg[:, g, :], Uf[:, g, :])], [bmat, bUf[g]], [pb[b]], start=True, stop=False)
                mm(ps[b][:, 1:512], [(GGb[:, g, :, :].rearrange("p i q -> p (i q)"), Xs[:, g, 0:511])], [bbig, bXs[g]], [pb[b]],
                   start=False, stop=True)
                act(Yf[:, g, :], ps[b][:, :], AF.Copy, [pb[b]], [bYf[g]])
            for i in range(8):
                q4, half = (i // 2), (i % 2)
                selx, rows = selrows(q4)
                b = 4 + bank()
                mm(ps[b][:, :], [(selx[rows, (half * 8 + g) * 128:(half * 8 + g + 1) * 128], Yf[rows, g, :]) for g in range(8)],
                   [bsel] + bYf, [pb[b]])
                stt(yT[:, i::8], uT[:, i::8], vecs[:, V_SD + bt:V_SD + bt + 1], ps[b][:, :], ALU.mult, ALU.add,
                    [pb[b], buT, bvec], [byT])
            for hf in range(2):
                hs = slice(hf * (S_LEN // 2), (hf + 1) * (S_LEN // 2))
                act(gt1[:], yT[:, hs], AF.Square, [byT], [bg1])
                ts(gt1[:], gt1[:], 0.044715, 1.0, ALU.mult, ALU.add, [bg1], [bg1])
                tt(gt1[:], gt1[:], yT[:, hs], ALU.mult, [bg1, byT], [bg1])
                act(gt1[:], gt1[:], AF.Sigmoid, [bg1], [bg1], scale=1.5957691216057308)
                tt(zT[:, hs], gt1[:], yT[:, hs], ALU.mult, [bg1, byT], [bzT])
            S.dma("sp", ZT[bt * 128:(bt + 1) * 128, :], zT[:], reads=[bzT])
        S.barrier()
        sb.reset(m)

    def out_tail(l, t0, rhsT, brhs, W2d, kcn, wrot, ystage_rot, sqrot, byt):
        for n in range(16):
            wt, bwt = wrot.next()
            alias = getattr(wrot, "alias", {}).get(id(bwt), [])
            src_ = W2d[:, n * 128:n * 128 + 128].rearrange("(kc p) n -> p kc n", p=128)
            S.dma("pool", wt[:, 0:kcn, 0:128], src_, writes=[bwt] + list(alias))
            ys, bys = ystage_rot.next()
            for sub in range(2):
                b = bank()
                mm(ps[b][:, :], [(wt[:, kc, 0:128], rhsT[:, kc, sub * 512:(sub + 1) * 512]) for kc in range(kcn)],
                   [bwt] + brhs + list(alias), [pb[b]])
                act(ys[:, sub * 512:(sub + 1) * 512], ps[b][:, :], AF.Copy, [pb[b]], [bys])
                sq, bq = sqrot.next()
                act(sq[:], ps[b][:, :], AF.Square, [pb[b]], [bq])
                mm(ps[6 + sub][:, :], [(onesb[:], sq[:])], [bq, bcst], [pb[6 + sub]], start=(n == 0), stop=(n == 15))
            S.dma("sp", YT[n * 128:(n + 1) * 128, t0:t0 + TT], ys[:], reads=[bys], writes=[byt[n]])

    def phase_D(l):
        m = sb.mark()
        xsrc = xT_in if l == 0 else XT
        OTt = sb.alloc([128, 16, TT], BF16)
        ZTt = sb.alloc([128, 8, TT], BF16)
        bin_ = Buf("din")
        mg = sb.alloc([128, 16, TT], BF16)
        bmg = [Buf(f"mg{n}") for n in range(16)]
        wrot = Rot(sb, 3, [128, 16, 128], BF16, "w")
        w8rot = Rot(sb, 4, [128, 8, 128], BF16, "w8")
        grot = Rot(sb, 4, [128, TT], BF16, "g")
        trot = Rot(sb, 6, [128, 512], F32, "t")
        yrot = Rot(sb, 2, [128, TT], F32, "y")
        xrot = Rot(sb, 2, [128, TT], F32, "x")
        sqrot = Rot(sb, 3, [128, 512], BF16, "sq")
        rstd = sb.alloc([128, TT], F32)
        brstd = Buf("rstd")
        byt = [Buf(f"yt{n}") for n in range(16)]
        pending = []

        def load_in(t0_):
            S.dma("sp", OTt[:], OT[:, t0_:t0_ + TT].rearrange("(kc p) t -> p kc t", p=128), writes=[bin_])
            S.dma("sp", ZTt[:], ZT[:, t0_:t0_ + TT].rearrange("(kc p) t -> p kc t", p=128), writes=[bin_])
        load_in(0)
        for tti in range(NT):
            t0 = tti * TT
            for n in range(16):
                wo, bwo = wrot.next()
                wload(wo, bwo, w_o[l], n * 128, 128, 16)
                w1, bw1 = w8rot.next()
                wload(w1, bw1, w_glu[l], n * 128, 128, 8)
                w2, bw2 = w8rot.next()
                wload(w2, bw2, w_glu[l], D + n * 128, 128, 8)
                ga, bga = grot.next()
                gb, bgb = grot.next()
                S.dma("sp", ga[:], GT[n * 128:(n + 1) * 128, t0:t0 + TT], writes=[bga])
                S.dma("sp", gb[:], GT[D + n * 128:D + (n + 1) * 128, t0:t0 + TT], writes=[bgb])
                for sub in range(2):
                    cs_ = slice(sub * 512, (sub + 1) * 512)
                    base = 3 * (bank_i[0] % 2)
                    bank_i[0] += 1
                    ba_, b1_, b2_ = base, base + 1, base + 2
                    mm(ps[ba_][:, :], [(wo[:, kc, :], OTt[:, kc, cs_]) for kc in range(16)], [bwo, bin_], [pb[ba_]])
                    mm(ps[b1_][:, :], [(w1[:, kc, :], ZTt[:, kc, cs_]) for kc in range(8)], [bw1, bin_], [pb[b1_]])
                    mm(ps[b2_][:, :], [(w2[:, kc, :], ZTt[:, kc, cs_]) for kc in range(8)], [bw2, bin_], [pb[b2_]])
                    sg_, bsg = trot.next()
                    act(sg_[:], ps[b2_][:, :], AF.Sigmoid, [pb[b2_], bvec], [bsg], bias=vecs[:, V_BGLU + 16 + n:V_BGLU + 17 + n])
                    so, bso = trot.next()
                    stt(so[:], ps[b1_][:, :], vecs[:, V_BGLU + n:V_BGLU + n + 1], sg_[:], ALU.add, ALU.mult, [pb[b1_], bsg, bvec], [bso])
                    ta, bta = trot.next()
                    tt(ta[:], ps[ba_][:, :], ga[:, cs_], ALU.mult, [pb[ba_], bga], [bta])
                    tt(so[:], so[:], gb[:, cs_], ALU.mult, [bso, bgb], [bso])
                    tt(mg[:, n, cs_], ta[:], so[:], ALU.add, [bta, bso], [bmg[n]])
                if pending:
                    pending.pop(0)()
            while pending:
                pending.pop(0)()
            if tti + 1 < NT:
                load_in(t0 + TT)
            out_tail(l, t0, mg, bmg, w_out[l], 16, wrot, yrot, sqrot, byt)
            rstd_from([6, 7], rstd, brstd, D)
            pending = postnorm_steps(t0, YT, byt, xsrc, XT, V_POSTMIX, rstd, brstd, yrot, xrot)
        while pending:
            pending.pop(0)()
        S.barrier()
        sb.reset(m)

    def phase_E(l, last):
        m = sb.mark()
        dst = outT if last else XT
        hT = sb.alloc([128, 16, TT], BF16)
        hT_off = sb.last_off
        bh = [Buf(f"h{k}") for k in range(16)]
        actT = sb.alloc([128, 44, TT], BF16)
        bact = [Buf(f"a{f}") for f in range(44)]
        wrot = Rot(sb, 4, [128, 16, 128], BF16, "w")
        wdrot = Rot.__new__(Rot)
        wdrot.items = [(sb.at([128, 44, 128], BF16, hT_off + i * 12288), Buf(f"wd{i}")) for i in range(2)]
        wdrot.i = 0
        wdrot.alias = {id(wdrot.items[i][1]): bh[6 * i:6 * i + 6] for i in range(2)}
        trot = Rot(sb, 3, [128, 512], F32, "t")
        yrot = Rot(sb, 2, [128, TT], F32, "y")
        xrot = Rot(sb, 2, [128, TT], F32, "x")
        sqrot = Rot(sb, 3, [128, 512], BF16, "sq")
        rstd = sb.alloc([128, TT], F32)
        brstd = Buf("rstd")
        byt = [Buf(f"yt{n}") for n in range(16)]
        rstd2 = sb.alloc([128, TT], F32)
        brstd2 = Buf("rstd2")
        pending = []
        for tti in range(NT):
            t0 = tti * TT
            make_hT(XT, t0, V_PREF, hT, bh, xrot, sqrot, rstd, brstd)
            for f in range(44):
                wg_, bwg = wrot.next()
                wload(wg_, bwg, w_fg[l], f * 128, 128, 16)
                wu_, bwu = wrot.next()
                wload(wu_, bwu, w_fu[l], f * 128, 128, 16)
                for sub in range(2):
                    cs_ = slice(sub * 512, (sub + 1) * 512)
                    base = 2 * (bank_i[0] % 3)
                    bank_i[0] += 1
                    bg_, bu_ = base, base + 1
                    mm(ps[bg_][:, :], [(wg_[:, kc, :], hT[:, kc, cs_]) for kc in range(16)], [bwg] + bh, [pb[bg_]])
                    mm(ps[bu_][:, :], [(wu_[:, kc, :], hT[:, kc, cs_]) for kc in range(16)], [bwu] + bh, [pb[bu_]])
                    sg_, bsg = trot.next()
                    act(sg_[:], ps[bg_][:, :], AF.Silu, [pb[bg_]], [bsg])
                    tt(actT[:, f, cs_], ps[bu_][:, :], sg_[:], ALU.mult, [pb[bu_], bsg], [bact[f]])
                if pending:
                    pending.pop(0)()
            while pending:
                pending.pop(0)()
            out_tail(l, t0, actT, bact, w_fd[l], 44, wdrot, yrot, sqrot, byt)
            rstd_from([6, 7], rstd2, brstd2, D)
            pending = postnorm_steps(t0, YT, byt, XT, dst, V_POSTF, rstd2, brstd2, yrot, xrot)
        while pending:
            pending.pop(0)()
        S.barrier()
        sb.reset(m)

    rope_tables()
    for l in range(n_layers):
        S.dma("sp", vecs[:], vecs_in[l], writes=[bvec])
        if "A" in phases:
            phase_A(l)
        if "B" in phases:
            phase_B(l)
        if "C" in phases:
            phase_C(l)
        if "D" in phases:
            phase_D(l)
        if "E" in phases:
            phase_E(l, last=(l == n_layers - 1))
        S.barrier()
    S.barrier()
    S.emit()
    return nc, S


def _chunkcols(v):
    v = np.asarray(v, dtype=np.float32)
    return np.ascontiguousarray(v.reshape(-1, 128).T)


def host_consts():
    c = np.zeros((128, NCONST), np.float32)
    r = np.arange(128)
    c[r, C_ID + r] = 1.0
    c[r, C_PSW + (r + 64) % 128] = 1.0
    jj = r // 16
    c[:, C_MASK:C_MASK + 128] = (jj[None, :] >= jj[:, None]).astype(np.float32)
    for half in range(2):
        for j in range(8):
            blk = (half * 8 + j) * 128
            for rr in range(128):
                hp, qp = (rr % 32) // 16, rr % 16
                if hp == half:
                    c[rr, C_SEL + blk + 16 * j + qp] = 1.0
    c[:, C_ONES:C_ONES + 128] = 1.0
    c[96:128, C_SEL2:C_SEL2 + 2048] = c[96:128, C_SEL:C_SEL + 2048]
    sm = C_SM
    c[:64, sm + 0] = 1.0
    c[64:, sm + 1] = 1.0
    c[:64, sm + 2] = -1.0
    c[64:, sm + 3] = -1.0
    sg = np.where((r % 64) < 32, -1.0, 1.0)
    c[:, sm + 4] = sg
    inv = (10000.0 ** (-(np.arange(0, 64, 2, dtype=np.float32)) / 64.0)).astype(np.float32)
    c[:, sm + 5] = inv[r % 32]
    return c


def host_prepare(inp):
    Ld = L_DEPTH
    vecs = np.zeros((Ld, 128, NVEC), np.float32)
    s5p = np.zeros((Ld, 128, NS5P), np.float32)
    for l in range(Ld):
        vecs[l, :, V_PRE:V_PRE + 16] = _chunkcols(inp["pre_mix_norm"][l])
        vecs[l, :, V_QN:V_QN + 4] = _chunkcols(inp["q_norm"][l])
        vecs[l, :, V_KVN:V_KVN + 2] = _chunkcols(inp["kv_norm"][l])
        vecs[l, :, V_BG:V_BG + 32] = _chunkcols(inp["b_gate"][l])
        vecs[l, :, V_BGLU:V_BGLU + 32] = _chunkcols(inp["b_glu"][l])
        vecs[l, :, V_POSTMIX:V_POSTMIX + 16] = _chunkcols(inp["post_mix_norm"][l])
        vecs[l, :, V_PREF:V_PREF + 16] = _chunkcols(inp["pre_ffn_norm"][l])
        vecs[l, :, V_POSTF:V_POSTF + 16] = _chunkcols(inp["post_ffn_norm"][l])
        vecs[l, :, V_SD:V_SD + 8] = _chunkcols(inp["ssm_d"][l])
        arT = np.asarray(inp["ssm_a_re"][l], np.float32).T
        aiT = np.asarray(inp["ssm_a_im"][l], np.float32).T
        s5p[l, :, P_AR:P_AR + 64] = np.concatenate([arT, arT], 0)
        s5p[l, :, P_AI:P_AI + 64] = np.concatenate([aiT, aiT], 0)
        s5p[l, :, P_LDT:P_LDT + 64] = np.broadcast_to(np.asarray(inp["ssm_log_dt"][l], np.float32)[None, :], (128, 64))
        for key, off, perm in (("ssm_b_re", P_BRE, (1, 0, 2)), ("ssm_b_im", P_BIM, (1, 0, 2)),
                               ("ssm_c_re", P_CRE, (2, 0, 1)), ("ssm_c_im", P_CIM, (2, 0, 1))):
            a = np.transpose(np.asarray(inp[key][l], np.float32), perm).reshape(64, 1024)
            s5p[l, :, off:off + 1024] = np.concatenate([a, a], 0)
    return vecs, s5p


_CACHE = {}
LAYERS_PER_LAUNCH = 4


def kernel(**inputs):
    inp = {k: np.asarray(v) for k, v in inputs.items()}
    x = inp["x"]
    B = x.shape[0]
    vecs, s5p = host_prepare(inp)
    consts = host_consts()
    npl = LAYERS_PER_LAUNCH
    if npl not in _CACHE:
        _CACHE[npl] = build(n_layers=npl)[0]
    nc = _CACHE[npl]
    wnames = ["w_in", "w_uq", "w_ukv", "w_o_mla", "w_glu", "w_out", "w_ffn_gate", "w_ffn_up", "w_ffn_down"]
    xT = [np.ascontiguousarray(x[b].T, dtype=np.float32) for b in range(B)]
    for l0 in range(0, L_DEPTH, npl):
        shared = {k: np.ascontiguousarray(inp[k][l0:l0 + npl], dtype=np.float32) for k in wnames}
        shared["vecs"] = np.ascontiguousarray(vecs[l0:l0 + npl])
        shared["s5p"] = np.ascontiguousarray(s5p[l0:l0 + npl])
        shared["consts"] = consts
        in_maps = []
        for b in range(B):
            mp = dict(shared)
            mp["xT"] = xT[b]
            mp["pos"] = np.ascontiguousarray(inp["positions"][b], dtype=np.int32)
            in_maps.append(mp)
        res = run_bass_kernel_spmd(nc, in_maps, core_ids=list(range(B)))
        xT = [np.ascontiguousarray(res.results[b]["outT"], dtype=np.float32) for b in range(B)]
    out = np.stack([np.ascontiguousarray(xT[b].T) for b in range(B)], axis=0)
    return out.astype(np.float32)
```

```python
import math
import numpy as np
import concourse.bass as bass
import concourse.mybir as mybir
from concourse.bass_utils import run_bass_kernel_spmd

F32 = mybir.dt.float32
BF16 = mybir.dt.bfloat16
I32 = mybir.dt.int32
AF = mybir.ActivationFunctionType
ALU = mybir.AluOpType

D = 2048
S_LEN = 4096
L_DEPTH = 4
TT = 1024
NT = S_LEN // TT
NH = 16
FF = 5632
IN_W = 5952
OFF_KV = 512
OFF_PE = 768
OFF_SSM = 832
OFF_GATE = 1856
EPS = 1e-6
SCALE = 192.0 ** -0.5
MAGIC = 12582912.0
TWO_PI = 2.0 * math.pi
C1 = 6.28125
C2 = TWO_PI - C1

V_PRE, V_QN, V_KVN, V_BG, V_BGLU, V_POSTMIX, V_PREF, V_POSTF, V_SD = 0, 16, 20, 22, 54, 86, 102, 118, 134
NVEC = 142
C_ID, C_PSW, C_MASK, C_SEL, C_ONES, C_SM = 0, 128, 256, 384, 384 + 2048, 384 + 2048 + 128
C_SEL2 = C_SM + 8
NCONST = C_SEL2 + 2048
P_AR, P_AI, P_LDT, P_BRE, P_BIM, P_CRE, P_CIM = 0, 64, 128, 192, 192 + 1024, 192 + 2048, 192 + 3072
NS5P = 192 + 4096


class Buf:
    __slots__ = ("name", "w", "r")

    def __init__(self, name=""):
        self.name = name
        self.w = {}
        self.r = {}


class Sched:
    ENGS = ("pe", "act", "dve", "pool", "sp")
    ROT = 30000
    RING = 20

    def __init__(self, nc):
        self.nc = nc
        self.ops = {e: [] for e in self.ENGS}
        self.known = {e: {} for e in self.ENGS}
        self.csem = {}
        self.ccnt = {}
        self._csem_idx = {}
        for e in self.ENGS:
            self._new_csem(e)
        self.ring = {}
        self.ring_val = {}
        self.ring_i = {}
        for q in ("sp", "pool"):
            self.ring[q] = [nc.alloc_semaphore(f"dq_{q}_{i}") for i in range(self.RING)]
            self.ring_val[q] = [0] * self.RING
            self.ring_i[q] = 0
        self.n_instr = {e: 0 for e in self.ENGS}

    def _new_csem(self, e):
        idx = self._csem_idx.get(e, 0)
        self._csem_idx[e] = idx + 1
        self.csem[e] = self.nc.alloc_semaphore(f"c_{e}_{idx}")
        self.ccnt[e] = 0

    def _collect(self, eng, reads, writes):
        deps = {}

        def add(t):
            key = id(t[1])
            if key not in deps or deps[key][2] < t[2]:
                deps[key] = t

        for b in reads:
            for t in b.w.values():
                add(t)
        for b in writes:
            for t in b.w.values():
                add(t)
            for t in b.r.values():
                if t[0] == eng and eng in ("pe", "act", "dve"):
                    continue
                add(t)
        waits = []
        kn = self.known[eng]
        for key, t in deps.items():
            if t[0] == eng and eng == "pe":
                continue
            if kn.get(key, 0) >= t[2]:
                continue
            kn[key] = t[2]
            waits.append((t[1], t[2]))
        return waits

    def _register(self, ticket, reads, writes):
        key = id(ticket[1])
        for b in writes:
            b.w = {key: ticket}
            b.r = {}
        for b in reads:
            b.r[key] = ticket

    def op(self, eng, fn, reads=(), writes=()):
        waits = self._collect(eng, reads, writes)
        if self.ccnt[eng] >= self.ROT:
            self._new_csem(eng)
        self.ccnt[eng] += 1
        sem = self.csem[eng]
        ticket = (eng, sem, self.ccnt[eng])
        self.ops[eng].append((waits, fn, sem, 1))
        self._register(ticket, reads, writes)
        return ticket

    def dma(self, q, out_ap, in_ap, reads=(), writes=(), **kw):
        i = self.ring_i[q]
        self.ring_i[q] = (i + 1) % self.RING
        sem = self.ring[q][i]
        prev = self.ring_val[q][i]
        waits = self._collect(q, reads, writes)
        kn = self.known[q]
        if prev > 0 and kn.get(id(sem), 0) < prev:
            kn[id(sem)] = prev
            waits.append((sem, prev))
        val = prev + 16
        self.ring_val[q][i] = val
        ticket = (q + "_dma", sem, val)

        def fn(e, out_ap=out_ap, in_ap=in_ap, kw=kw):
            return e.dma_start(out=out_ap, in_=in_ap, **kw)

        self.ops[q].append((waits, fn, sem, 16))
        self._register(ticket, reads, writes)
        return ticket

    def barrier(self):
        tickets = []
        for q in ("sp", "pool"):
            for i in range(self.RING):
                if self.ring_val[q][i] > 0:
                    tickets.append((q + "_dma", self.ring[q][i], self.ring_val[q][i]))
        for e in self.ENGS:
            if self.ccnt[e] > 0:
                tickets.append((e, self.csem[e], self.ccnt[e]))
        for e in self.ENGS:
            kn = self.known[e]
            waits = []
            for t in tickets:
                if t[0] == e and e == "pe":
                    continue
                key = id(t[1])
                if kn.get(key, 0) >= t[2]:
                    continue
                kn[key] = t[2]
                waits.append((t[1], t[2]))
            if waits:
                self.ops[e].append((waits, None, None, 0))

    def emit(self):
        nc = self.nc
        engmap = {"pe": "tensor", "act": "scalar", "dve": "vector", "pool": "gpsimd", "sp": "sync"}
        with nc.Block() as block:
            for e in self.ENGS:
                ops = self.ops[e]
                if not ops:
                    continue

                def body(eng, ops=ops, e=e):
                    n = 0
                    for (waits, fn, sem, inc) in ops:
                        expl = waits if fn is None else waits[:-1]
                        for (s, v) in expl:
                            eng.wait_ge(s, v)
                            n += 1
                        if fn is not None:
                            r = fn(eng)
                            first, last = r if isinstance(r, tuple) else (r, r)
                            if waits:
                                first._wait_ge(waits[-1][0], waits[-1][1])
                            last.then_inc(sem, inc)
                            n += 1
                    self.n_instr[e] = n

                getattr(block, engmap[e])(body)


class SBAlloc:
    BASE = 16512
    CAP = 196608

    def __init__(self, nc):
        self.nc = nc
        self.top = self.BASE
        self.n = 0

    def mark(self):
        return self.top

    def reset(self, m):
        self.top = m

    def alloc(self, shape, dtype):
        nb = 4 if dtype in (F32, I32) else 2
        n = 1
        for s in shape[1:]:
            n *= s
        off = (self.top + 63) // 64 * 64
        self.top = off + n * nb
        assert self.top <= self.CAP, f"SBUF overflow {self.top}"
        self.n += 1
        self.last_off = off
        return self.nc.alloc_sbuf_tensor_at(f"sb{self.n}", list(shape), dtype, offset=off)

    def at(self, shape, dtype, off):
        self.n += 1
        return self.nc.alloc_sbuf_tensor_at(f"sb{self.n}", list(shape), dtype, offset=off)


class Rot:
    def __init__(self, sb, n, shape, dtype, name="r"):
        self.items = [(sb.alloc(shape, dtype), Buf(f"{name}{i}")) for i in range(n)]
        self.i = 0

    def next(self):
        it = self.items[self.i]
        self.i = (self.i + 1) % len(self.items)
        return it


def build(n_layers=L_DEPTH, dump=False, phases="ABCDE"):
    nc = bass.Bass("TRN2", target_bir_lowering=False)
    S = Sched(nc)
    sb = SBAlloc(nc)

    def din(name, shape, dt=F32):
        return nc.dram_tensor(name, list(shape), dt, kind="ExternalInput").ap()

    def dscr(name, shape, dt):
        kind = "ExternalOutput" if dump else "Internal"
        return nc.dram_tensor(name, list(shape), dt, kind=kind).ap()

    xT_in = din("xT", [D, S_LEN])
    pos_in = din("pos", [S_LEN], I32)
    w_in = din("w_in", [n_layers, D, IN_W])
    w_uq = din("w_uq", [n_layers, 512, 3072])
    w_ukv = din("w_ukv", [n_layers, 256, 4096])
    w_o = din("w_o_mla", [n_layers, D, D])
    w_glu = din("w_glu", [n_layers, 1024, 4096])
    w_out = din("w_out", [n_layers, D, D])
    w_fg = din("w_ffn_gate", [n_layers, D, FF])
    w_fu = din("w_ffn_up", [n_layers, D, FF])
    w_fd = din("w_ffn_down", [n_layers, FF, D])
    vecs_in = din("vecs", [n_layers, 128, NVEC])
    s5p_in = din("s5p", [n_layers, 128, NS5P])
    consts_in = din("consts", [128, NCONST])
    outT = nc.dram_tensor("outT", [D, S_LEN], F32, kind="ExternalOutput").ap()

    XT = dscr("XT", [D, S_LEN], F32)
    YT = dscr("YT", [D, S_LEN], F32)
    QT = dscr("QT", [NH, 192, S_LEN], BF16)
    KT = dscr("KT", [NH, 128, S_LEN], BF16)
    KPE = dscr("KPE", [64, S_LEN], BF16)
    VS = dscr("VS", [S_LEN, D], BF16)
    UT = dscr("UT", [1024, S_LEN], BF16)
    GT = dscr("GT", [4096, S_LEN], BF16)
    OT = dscr("OT", [D, S_LEN], BF16)
    ZT = dscr("ZT", [1024, S_LEN], BF16)
    CS = dscr("CS", [2, 64, S_LEN], F32)

    ps = [nc.alloc_psum_tensor(f"ps{i}", [128, 512], F32) for i in range(8)]
    pb = [Buf(f"pb{i}") for i in range(8)]
    bank_i = [0]

    def bank(n=4):
        b = bank_i[0] % n
        bank_i[0] += 1
        return b

    def mm(out_ap, pairs, reads, writes, start=True, stop=True):
        def fn(e):
            n = len(pairs)
            ins = None
            first = None
            for i, (lh, rh) in enumerate(pairs):
                ins = e.matmul(out_ap, lhsT=lh, rhs=rh, start=(start and i == 0), stop=(stop and i == n - 1))
                if first is None:
                    first = ins
            return (first, ins)
        return S.op("pe", fn, reads, writes)

    def act(out, in_, func, reads, writes, **kw):
        return S.op("act", lambda e: e.activation(out=out, in_=in_, func=func, **kw), reads, writes)

    def tt(out, a, b, op, reads, writes, eng="dve"):
        return S.op(eng, lambda e: e.tensor_tensor(out=out, in0=a, in1=b, op=op), reads, writes)

    def ts(out, a, s1, s2, op0, op1, reads, writes, eng="dve"):
        if op1 is None:
            return S.op(eng, lambda e: e.tensor_scalar(out=out, in0=a, scalar1=s1, scalar2=None, op0=op0), reads, writes)
        return S.op(eng, lambda e: e.tensor_scalar(out=out, in0=a, scalar1=s1, scalar2=s2, op0=op0, op1=op1), reads, writes)

    def stt(out, a, sc, b, op0, op1, reads, writes):
        return S.op("dve", lambda e: e.scalar_tensor_tensor(out=out, in0=a, scalar=sc, in1=b, op0=op0, op1=op1), reads, writes)

    def recip(out, in_, reads, writes):
        return S.op("dve", lambda e: e.reciprocal(out=out, in_=in_), reads, writes)

    def wload(wt, bwt, W2d, c0, ncols, kcn, col_dst=0):
        src = W2d[:, c0:c0 + ncols].rearrange("(kc p) n -> p kc n", p=128)
        S.dma("pool", wt[:, 0:kcn, col_dst:col_dst + ncols], src, writes=[bwt])

    cst = sb.alloc([128, C_SEL], F32)
    csm = sb.alloc([128, 8], F32)
    onesb = sb.alloc([128, 128], BF16)
    bcst = Buf("cst")
    S.dma("sp", cst[:], consts_in[:, 0:C_SEL], writes=[bcst])
    S.dma("sp", csm[:], consts_in[:, C_SM:C_SM + 8], writes=[bcst])
    S.dma("pool", onesb[:], consts_in[:, C_ONES:C_ONES + 128], writes=[bcst])

    ident = cst[:, C_ID:C_ID + 128]
    pswap = cst[:, C_PSW:C_PSW + 128]
    maskT = cst[:, C_MASK:C_MASK + 128]
    SELRE, SELIM, NSELRE, NSELIM, SGNROPE, INVF = 0, 1, 2, 3, 4, 5
    vecs = sb.alloc([128, NVEC], F32)
    bvec = Buf("vecs")
    persist_mark = sb.mark()

    def rope_tables():
        m = sb.mark()
        posi = sb.alloc([64, S_LEN], I32)
        ang = sb.alloc([64, S_LEN], F32)
        t1 = sb.alloc([64, S_LEN], F32)
        t2 = sb.alloc([64, S_LEN], F32)
        bp, ba, b1, b2 = Buf(), Buf(), Buf(), Buf()
        S.dma("sp", posi[:], pos_in.partition_broadcast(64), writes=[bp])
        S.op("dve", lambda e: e.tensor_copy(out=ang[:], in_=posi[:]), [bp], [ba])
        ts(ang[:], ang[:], csm[0:64, INVF:INVF + 1], None, ALU.mult, None, [ba, bcst], [ba])

        def reduce_sin(shift, dst_idx, signed):
            if shift != 0.0:
                ts(t1[:], ang[:], shift, None, ALU.add, None, [ba], [b1])
                srcang, bsrc = t1, b1
            else:
                srcang, bsrc = ang, ba
            ts(t2[:], srcang[:], 1.0 / TWO_PI, MAGIC, ALU.mult, ALU.add, [bsrc], [b2])
            ts(t2[:], t2[:], MAGIC, None, ALU.subtract, None, [b2], [b2])
            if shift == 0.0:
                stt(t1[:], t2[:], -C1, ang[:], ALU.mult, ALU.add, [b2, ba], [b1])
            else:
                stt(t1[:], t2[:], -C1, t1[:], ALU.mult, ALU.add, [b2, b1], [b1])
            stt(t1[:], t2[:], -C2, t1[:], ALU.mult, ALU.add, [b2, b1], [b1])
            ts(t1[:], t1[:], math.pi, -math.pi, ALU.min, ALU.max, [b1], [b1])
            act(t2[:], t1[:], AF.Sin, [b1], [b2])
            if signed:
                ts(t2[:], t2[:], csm[0:64, SGNROPE:SGNROPE + 1], None, ALU.mult, None, [b2, bcst], [b2])
            S.dma("sp", CS[dst_idx], t2[:], reads=[b2])

        reduce_sin(math.pi / 2.0, 0, False)
        reduce_sin(0.0, 1, True)
        S.barrier()
        sb.reset(m)

    def rstd_from(banks, out_tile, bout, nfeat, extra_scale=1.0):
        es2 = extra_scale * extra_scale
        for sub, bk in enumerate(banks):
            act(out_tile[:, sub * 512:(sub + 1) * 512], ps[bk][:, :], AF.Sqrt, [pb[bk]], [bout],
                scale=1.0 / (nfeat * es2), bias=EPS / es2)
        recip(out_tile[:], out_tile[:], [bout], [bout])

    def make_hT(src, t0, vcol, hT, bh, xrot, sqrot, rstd, brstd):
        for kc in range(16):
            xc, bx = xrot.next()
            S.dma("sp", xc[:], src[kc * 128:(kc + 1) * 128, t0:t0 + TT], writes=[bx])
            for sub in range(2):
                sq, bq = sqrot.next()
                act(sq[:], xc[:, sub * 512:(sub + 1) * 512], AF.Square, [bx], [bq])
                mm(ps[6 + sub][:, :], [(onesb[:], sq[:])], [bq, bcst], [pb[6 + sub]], start=(kc == 0), stop=(kc == 15))
        rstd_from([6, 7], rstd, brstd, D)
        for kc in range(16):
            xc, bx = xrot.next()
            S.dma("sp", xc[:], src[kc * 128:(kc + 1) * 128, t0:t0 + TT], writes=[bx])
            stt(hT[:, kc, :], xc[:], vecs[:, vcol + kc:vcol + kc + 1], rstd[:], ALU.mult, ALU.mult,
                [bx, brstd, bvec], [bh[kc]])

    def postnorm_steps(t0, ysrc, byt, xsrc, dst, vcol, rstd, brstd, yrot, xrot):
        return [(lambda n=n: postnorm_one(n, t0, ysrc, byt, xsrc, dst, vcol, rstd, brstd, yrot, xrot)) for n in range(16)]

    def postnorm_one(n, t0, ysrc, byt, xsrc, dst, vcol, rstd, brstd, yrot, xrot):
        if True:
            yc, by = yrot.next()
            xc, bx = xrot.next()
            S.dma("sp", yc[:], ysrc[n * 128:(n + 1) * 128, t0:t0 + TT], reads=[byt[n]], writes=[by])
            S.dma("sp", xc[:], xsrc[n * 128:(n + 1) * 128, t0:t0 + TT], writes=[bx])
            stt(yc[:], yc[:], vecs[:, vcol + n:vcol + n + 1], rstd[:], ALU.mult, ALU.mult, [by, brstd, bvec], [by])
            tt(yc[:], yc[:], xc[:], ALU.add, [by, bx], [by])
            S.dma("sp", dst[n * 128:(n + 1) * 128, t0:t0 + TT], yc[:], reads=[by])

    def phase_A(l):
        m = sb.mark()
        src = xT_in if l == 0 else XT
        wuq = sb.alloc([128, 4, 3072], BF16)
        wuqr = sb.alloc([128, 4, 16, 64], BF16)
        wukv = sb.alloc([128, 2, 4096], BF16)
        bw = Buf("wres")
        for c in range(3):
            wload(wuq, bw, w_uq[l], c * 1024, 1024, 4, col_dst=c * 1024)
        for c in range(4):
            wload(wukv, bw, w_ukv[l], c * 1024, 1024, 2, col_dst=c * 1024)
        uq4 = w_uq[l].rearrange("(kc p) (h e) -> kc p h e", p=128, e=192)
        for kc in range(4):
            S.dma("pool", wuqr[:, kc, :, 0:32], uq4[kc][:, :, 160:192], writes=[bw])
            S.dma("pool", wuqr[:, kc, :, 32:64], uq4[kc][:, :, 128:160], writes=[bw])
        hT = sb.alloc([128, 16, TT], BF16)
        bh = [Buf(f"h{k}") for k in range(16)]
        xrot = Rot(sb, 3, [128, TT], F32, "x")
        sqrot = Rot(sb, 3, [128, 512], BF16, "sq")
        rstd = sb.alloc([128, TT], F32)
        brstd = Buf("rstd")
        rstdq = sb.alloc([128, TT], F32)
        brq = Buf("rstdq")
        rstdkv = sb.alloc([128, TT], F32)
        brkv = Buf("rstdkv")
        rkt = sb.alloc([128, 8], F32)
        brkt = Buf("rkt")
        cqw = sb.alloc([128, 4, TT], BF16)
        bcq = Buf("cqw")
        ckvw = sb.alloc([128, 2, TT], BF16)
        bckv = Buf("ckvw")
        sqkv = sb.alloc([128, 2, TT], BF16)
        bsqkv = Buf("sqkv")
        cst_ = sb.alloc([64, 2, TT], F32)
        bcs = Buf("cs")
        wrot = Rot(sb, 3, [128, 16, 128], BF16, "w")
        strot = Rot(sb, 4, [128, 512], BF16, "st")
        tmrot = Rot(sb, 4, [64, 512], F32, "tm")

        def proj(c0, M, evac, rot_cols=None):
            wt, bwt = wrot.next()
            if rot_cols is None:
                wload(wt, bwt, w_in[l], c0, M, 16)
            else:
                wload(wt, bwt, w_in[l], rot_cols[0], 32, 16, col_dst=0)
                wload(wt, bwt, w_in[l], rot_cols[1], 32, 16, col_dst=32)
            for sub in range(2):
                b = bank()
                mm(ps[b][0:M, :], [(wt[:, kc, 0:M], hT[:, kc, sub * 512:(sub + 1) * 512]) for kc in range(16)],
                   [bwt] + bh, [pb[b]])
                evac(b, sub)

        make_hT(src, 0, V_PRE, hT, bh, xrot, sqrot, rstd, brstd)
        for tti in range(NT):
            t0 = tti * TT
            S.dma("sp", cst_[:, 0, :], CS[0][:, t0:t0 + TT], writes=[bcs])
            S.dma("sp", cst_[:, 1, :], CS[1][:, t0:t0 + TT], writes=[bcs])

            for j in range(4):
                def ev(b, sub, j=j):
                    sq, bq = sqrot.next()
                    act(sq[:], ps[b][:, :], AF.Square, [pb[b]], [bq])
                    act(cqw[:, j, sub * 512:(sub + 1) * 512], ps[b][:, :], AF.Copy, [pb[b], bvec], [bcq],
                        scale=vecs[:, V_QN + j:V_QN + j + 1])
                    mm(ps[6 + sub][:, :], [(onesb[:], sq[:])], [bq, bcst], [pb[6 + sub]], start=(j == 0), stop=(j == 3))
                proj(j * 128, 128, ev)
            rstd_from([6, 7], rstdq, brq, 512, extra_scale=SCALE)
            for j in range(2):
                def ev(b, sub, j=j):
                    act(sqkv[:, j, sub * 512:(sub + 1) * 512], ps[b][:, :], AF.Square, [pb[b]], [bsqkv])
                    act(ckvw[:, j, sub * 512:(sub + 1) * 512], ps[b][:, :], AF.Copy, [pb[b], bvec], [bckv],
                        scale=vecs[:, V_KVN + j:V_KVN + j + 1])
                    mm(ps[4 + sub][:, :], [(onesb[:], sqkv[:, j, sub * 512:(sub + 1) * 512])], [bsqkv, bcst], [pb[4 + sub]],
                       start=(j == 0), stop=(j == 1))
                proj(OFF_KV + j * 128, 128, ev)
            rstd_from([4, 5], rstdkv, brkv, 256)
            b = bank()
            for blk in range(8):
                mm(ps[b][:, blk:blk + 1], [(sqkv[:, j, blk * 128:(blk + 1) * 128], onesb[:, 0:1]) for j in range(2)],
                   [bsqkv, bcst], [pb[b]])
            act(rkt[:], ps[b][:, 0:8], AF.Sqrt, [pb[b]], [brkt], scale=1.0 / 256.0, bias=EPS)
            recip(rkt[:], rkt[:], [brkt], [brkt])
            pe_hold = {}

            def ev_pe(b, sub):
                ta, bta = tmrot.next()
                tt(ta[:], ps[b][0:64, :], cst_[:, 0, sub * 512:(sub + 1) * 512], ALU.mult, [pb[b], bcs], [bta])
                pe_hold[sub] = (ta, bta)

            def ev_rot(b, sub):
                ta, bta = pe_hold[sub]
                tb, btb = tmrot.next()
                tt(tb[:], ps[b][0:64, :], cst_[:, 1, sub * 512:(sub + 1) * 512], ALU.mult, [pb[b], bcs], [btb])
                st, bst = strot.next()
                tt(st[0:64, :], ta[:], tb[:], ALU.add, [bta, btb], [bst])
                S.dma("sp", KPE[:, t0 + sub * 512:t0 + (sub + 1) * 512], st[0:64, :], reads=[bst])
            proj(OFF_PE, 64, ev_pe)
            proj(OFF_PE, 64, ev_rot, rot_cols=(OFF_PE + 32, OFF_PE))
            for j in range(8):
                def ev(b, sub, j=j):
                    st, bst = strot.next()
                    act(st[:], ps[b][:, :], AF.Copy, [pb[b]], [bst])
                    S.dma("sp", UT[j * 128:(j + 1) * 128, t0 + sub * 512:t0 + (sub + 1) * 512], st[:], reads=[bst])
                proj(OFF_SSM + j * 128, 128, ev)
            for j in range(32):
                def ev(b, sub, j=j):
                    st, bst = strot.next()
                    act(st[:], ps[b][:, :], AF.Sigmoid, [pb[b], bvec], [bst], bias=vecs[:, V_BG + j:V_BG + j + 1])
                    S.dma("sp", GT[j * 128:(j + 1) * 128, t0 + sub * 512:t0 + (sub + 1) * 512], st[:], reads=[bst])
                proj(OFF_GATE + j * 128, 128, ev)
            if tti + 1 < NT:
                make_hT(src, t0 + TT, V_PRE, hT, bh, xrot, sqrot, rstd, brstd)
            for h in range(NH):
                for sub in range(2):
                    cs_ = slice(sub * 512, (sub + 1) * 512)
                    tok = slice(t0 + sub * 512, t0 + (sub + 1) * 512)
                    b = bank()
                    mm(ps[b][:, :], [(wuq[:, kc, h * 192:h * 192 + 128], cqw[:, kc, cs_]) for kc in range(4)], [bw, bcq], [pb[b]])
                    st, bst = strot.next()
                    tt(st[:], ps[b][:, :], rstdq[:, cs_], ALU.mult, [pb[b], brq], [bst])
                    S.dma("sp", QT[h, 0:128, tok], st[:], reads=[bst])
                    b1 = bank()
                    mm(ps[b1][0:64, :], [(wuq[:, kc, h * 192 + 128:h * 192 + 192], cqw[:, kc, cs_]) for kc in range(4)], [bw, bcq], [pb[b1]])
                    b2 = bank()
                    mm(ps[b2][0:64, :], [(wuqr[:, kc, h, :], cqw[:, kc, cs_]) for kc in range(4)], [bw, bcq], [pb[b2]])
                    ta, bta = tmrot.next()
                    tb, btb = tmrot.next()
                    tt(ta[:], ps[b1][0:64, :], cst_[:, 0, cs_], ALU.mult, [pb[b1], bcs], [bta])
                    tt(tb[:], ps[b2][0:64, :], cst_[:, 1, cs_], ALU.mult, [pb[b2], bcs], [btb])
                    tt(ta[:], ta[:], tb[:], ALU.add, [bta, btb], [bta])
                    st2, bst2 = strot.next()
                    tt(st2[0:64, :], ta[:], rstdq[0:64, cs_], ALU.mult, [bta, brq], [bst2])
                    S.dma("sp", QT[h, 128:192, tok], st2[0:64, :], reads=[bst2])
            for h in range(NH):
                for sub in range(2):
                    cs_ = slice(sub * 512, (sub + 1) * 512)
                    tok = slice(t0 + sub * 512, t0 + (sub + 1) * 512)
                    b = bank()
                    mm(ps[b][:, :], [(wukv[:, kc, h * 256:h * 256 + 128], ckvw[:, kc, cs_]) for kc in range(2)], [bw, bckv], [pb[b]])
                    st, bst = strot.next()
                    tt(st[:], ps[b][:, :], rstdkv[:, cs_], ALU.mult, [pb[b], brkv], [bst])
                    S.dma("sp", KT[h, :, tok], st[:], reads=[bst])
            wv = wukv[:, :, :].rearrange("p k (h e) -> p k h e", e=256)
            for blk in range(8):
                for cg in range(4):
                    b = bank()
                    mm(ps[b][:, :].rearrange("p (h e) -> p h e", e=128),
                       [(ckvw[:, kc, blk * 128:(blk + 1) * 128], wv[:, kc, 4 * cg:4 * cg + 4, 128:256]) for kc in range(2)],
                       [bw, bckv], [pb[b]])
                    st, bst = strot.next()
                    act(st[:], ps[b][:, :], AF.Copy, [pb[b], brkt], [bst], scale=rkt[:, blk:blk + 1])
                    S.dma("sp", VS[t0 + blk * 128:t0 + (blk + 1) * 128, cg * 512:(cg + 1) * 512], st[:], reads=[bst])
        S.barrier()
        sb.reset(m)

    def phase_B(l):
        m = sb.mark()
        kpe = sb.alloc([64, S_LEN], BF16)
        bkpe = Buf("kpe")
        S.dma("sp", kpe[:], KPE[:, :], writes=[bkpe])
        hb = []
        for i in range(2):
            hb.append(dict(
                qn=sb.alloc([128, S_LEN], BF16), qp=sb.alloc([64, S_LEN], BF16), kn=sb.alloc([128, S_LEN], BF16),
                v=sb.alloc([128, 32, 128], BF16), o=sb.alloc([128, S_LEN], BF16),
                bin=Buf(f"hin{i}"), bo=Buf(f"ho{i}")))
        ptrot = Rot(sb, 6, [128, 512], BF16, "pt")
        rcrot = Rot(sb, 2, [128, 512], F32, "rc")
        st_i = [0]
        for h in range(NH):
            B = hb[h % 2]
            S.dma("sp", B["qn"][:], QT[h, 0:128, :], writes=[B["bin"]])
            S.dma("sp", B["qp"][:], QT[h, 128:192, :], writes=[B["bin"]])
            S.dma("sp", B["kn"][:], KT[h, :, :], writes=[B["bin"]])
            S.dma("sp", B["v"][:], VS[:, h * 128:(h + 1) * 128].rearrange("(b p) e -> p b e", p=128), writes=[B["bin"]])
            pairs = []
            for qt in range(8):
                nkb = 4 * qt + 4
                for kb in range(nkb):
                    pairs.append((qt, kb, nkb))
            held = {}

            def emit_qk(i):
                qt, kb, nkb = pairs[i]
                d = kb - 4 * qt
                c0 = 0 if d < 0 else d * 128
                qs = slice(qt * 512 + c0, (qt + 1) * 512)
                ks = slice(kb * 128, (kb + 1) * 128)
                sbk = (0, 1, 6, 7)[st_i[0] % 4]
                st_i[0] += 1
                mm(ps[sbk][:, c0:512], [(B["kn"][:, ks], B["qn"][:, qs]), (kpe[:, ks], B["qp"][:, qs])],
                   [B["bin"], bkpe], [pb[sbk]])
                pt, bpt = ptrot.next()
                act(pt[:, c0:512], ps[sbk][:, c0:512], AF.Exp, [pb[sbk]], [bpt])
                if d >= 0:
                    S.op("pool", lambda e, pt=pt, c0=c0: e.memset(pt[64:128, c0:c0 + 64], 0.0), [], [bpt])
                held[i] = (pt, bpt, c0)

            def emit_pv(i):
                qt, kb, nkb = pairs[i]
                pt, bpt, c0 = held.pop(i)
                ob = 2 + (qt % 2)
                lb = 4 + (qt % 2)
                mm(ps[ob][:, c0:512], [(B["v"][:, kb, :], pt[:, c0:512])], [B["bin"], bpt], [pb[ob]],
                   start=(kb == 0), stop=(kb == nkb - 1))
                mm(ps[lb][:, c0:512], [(onesb[:], pt[:, c0:512])], [bcst, bpt], [pb[lb]],
                   start=(kb == 0), stop=(kb == nkb - 1))
                if kb == nkb - 1:
                    rc, brc = rcrot.next()
                    recip(rc[:], ps[lb][:, :], [pb[lb]], [brc])
                    tt(B["o"][:, qt * 512:(qt + 1) * 512], ps[ob][:, :], rc[:], ALU.mult, [pb[ob], brc], [B["bo"]])

            PD = 2
            for i in range(len(pairs) + PD):
                if i < len(pairs):
                    emit_qk(i)
                if i >= PD:
                    emit_pv(i - PD)
            S.dma("sp", OT[h * 128:(h + 1) * 128, :], B["o"][:], reads=[B["bo"]])
        S.barrier()
        sb.reset(m)

    def phase_C(l):
        m = sb.mark()
        selb = sb.alloc([128, 2048], BF16)
        selb2 = sb.alloc([128, 2048], BF16)
        bsel = Buf("sel")
        for (t_, c_) in ((selb, C_SEL), (selb2, C_SEL2)):
            S.dma("pool", t_[:, 0:1024], consts_in[:, c_:c_ + 1024], writes=[bsel])
            S.dma("pool", t_[:, 1024:2048], consts_in[:, c_ + 1024:c_ + 2048], writes=[bsel])

        def selrows(q4):
            if q4 < 3:
                return selb, slice(32 * q4, 32 * q4 + 32)
            return selb2, slice(64, 128)
        prm = sb.alloc([128, 192], F32)
        cre = sb.alloc([128, 64, 16], F32)
        cim = sb.alloc([128, 64, 16], F32)
        bre = sb.alloc([128, 64, 16], F32)
        mk2_off = sb.last_off
        bim = sb.alloc([128, 64, 16], F32)
        bprm = Buf("prm")
        S.dma("sp", prm[:], s5p_in[l][:, 0:192], writes=[bprm])
        for t_, off in ((bre, P_BRE), (bim, P_BIM), (cre, P_CRE), (cim, P_CIM)):
            S.dma("sp", t_[:].rearrange("p g q -> p (g q)"), s5p_in[l][:, off:off + 1024], writes=[bprm])
        NW = 40
        wk = sb.alloc([128, NW, 64], F32)
        bwk = Buf("wk")
        W = lambda i: wk[:, i, :]
        R = [bwk, bprm, bcst]
        ar, ai, ldt = prm[:, P_AR:P_AR + 64], prm[:, P_AI:P_AI + 64], prm[:, P_LDT:P_LDT + 64]
        DT, ARD, AID, MAG, T1, T2, SINP, COSP, ABR, ABI, DEN, NR, FRE, FIM, IR, II = range(16)
        act(W(DT), ldt, AF.Exp, R, [bwk])
        tt(W(ARD), ar, W(DT), ALU.mult, R, [bwk])
        tt(W(AID), ai, W(DT), ALU.mult, R, [bwk])
        act(W(MAG), W(ARD), AF.Exp, R, [bwk])

        def red_sin(dst, shift):
            ts(W(T1), W(AID), shift, None, ALU.add, None, R, [bwk])
            ts(W(T2), W(T1), 1.0 / TWO_PI, MAGIC, ALU.mult, ALU.add, R, [bwk])
            ts(W(T2), W(T2), MAGIC, None, ALU.subtract, None, R, [bwk])
            stt(W(T1), W(T2), -C1, W(T1), ALU.mult, ALU.add, R, [bwk])
            stt(W(T1), W(T2), -C2, W(T1), ALU.mult, ALU.add, R, [bwk])
            ts(W(T1), W(T1), math.pi, -math.pi, ALU.min, ALU.max, R, [bwk])
            act(W(dst), W(T1), AF.Sin, R, [bwk])
        red_sin(SINP, 0.0)
        red_sin(COSP, math.pi / 2.0)
        tt(W(ABR), W(MAG), W(COSP), ALU.mult, R, [bwk])
        tt(W(ABI), W(MAG), W(SINP), ALU.mult, R, [bwk])
        tt(W(T1), ar, ar, ALU.mult, R, [bwk])
        tt(W(T2), ai, ai, ALU.mult, R, [bwk])
        tt(W(DEN), W(T1), W(T2), ALU.add, R, [bwk])
        recip(W(DEN), W(DEN), R, [bwk])
        ts(W(NR), W(ABR), -1.0, None, ALU.add, None, R, [bwk])
        tt(W(T1), W(NR), ar, ALU.mult, R, [bwk])
        tt(W(T2), W(ABI), ai, ALU.mult, R, [bwk])
        tt(W(T1), W(T1), W(T2), ALU.add, R, [bwk])
        tt(W(FRE), W(T1), W(DEN), ALU.mult, R, [bwk])
        tt(W(T1), W(ABI), ar, ALU.mult, R, [bwk])
        tt(W(T2), W(NR), ai, ALU.mult, R, [bwk])
        tt(W(T1), W(T1), W(T2), ALU.subtract, R, [bwk])
        tt(W(FIM), W(T1), W(DEN), ALU.mult, R, [bwk])
        tt(W(T1), W(ABR), W(ABR), ALU.mult, R, [bwk])
        tt(W(T2), W(ABI), W(ABI), ALU.mult, R, [bwk])
        tt(W(T1), W(T1), W(T2), ALU.add, R, [bwk])
        recip(W(T1), W(T1), R, [bwk])
        tt(W(IR), W(ABR), W(T1), ALU.mult, R, [bwk])
        tt(W(T2), W(ABI), W(T1), ALU.mult, R, [bwk])
        ts(W(II), W(T2), -1.0, None, ALU.mult, None, R, [bwk])
        pw = sb.alloc([128, 16, 2, 64], F32)
        bpw = Buf("pw")
        RP = [bwk, bpw, bcst]
        PR = lambda m_: pw[:, m_ + 7, 0, :]
        PI = lambda m_: pw[:, m_ + 7, 1, :]
        S.op("dve", lambda e: e.memset(PR(0), 1.0), RP, [bpw])
        S.op("dve", lambda e: e.memset(PI(0), 0.0), RP, [bpw])

        def cmul(dr, di, xr, xi, yr, yi):
            tt(W(T1), xr, yr, ALU.mult, RP, [bwk])
            tt(W(T2), xi, yi, ALU.mult, RP, [bwk])
            tt(W(16), xr, yi, ALU.mult, RP, [bwk])
            tt(W(17), xi, yr, ALU.mult, RP, [bwk])
            tt(dr, W(T1), W(T2), ALU.subtract, RP, [bpw])
            tt(di, W(16), W(17), ALU.add, RP, [bpw])
        for m_ in range(1, 9):
            cmul(PR(m_), PI(m_), PR(m_ - 1), PI(m_ - 1), W(ABR), W(ABI))
        for m_ in range(-1, -8, -1):
            cmul(PR(m_), PI(m_), PR(m_ + 1), PI(m_ + 1), W(IR), W(II))
        dp = sb.alloc([128, 9, 2, 64], F32)
        S.op("dve", lambda e: e.tensor_copy(out=dp[:, 0, 0, :], in_=PR(8)), RP, [bpw])
        S.op("dve", lambda e: e.tensor_copy(out=dp[:, 0, 1, :], in_=PI(8)), RP, [bpw])
        for k in range(1, 9):
            cmul(dp[:, k, 0, :], dp[:, k, 1, :], dp[:, k - 1, 0, :], dp[:, k - 1, 1, :], dp[:, k - 1, 0, :], dp[:, k - 1, 1, :])
        dps = sb.alloc([128, 9, 64], F32)
        sg = sb.alloc([128, 1], F32)
        tt(sg[:], csm[:, SELRE:SELRE + 1], csm[:, SELIM:SELIM + 1], ALU.subtract, RP, [bpw])
        for k in range(9):
            ts(dps[:, k, :], dp[:, k, 1, :], sg[:, 0:1], None, ALU.mult, None, RP, [bpw])
        E1 = sb.alloc([128, 9, 64], F32)
        E2 = sb.alloc([128, 9, 64], F32)
        for m_ in range(9):
            ts(W(T1), PR(m_), csm[:, SELRE:SELRE + 1], None, ALU.mult, None, RP, [bwk])
            stt(E1[:, m_, :], PI(m_), csm[:, NSELIM:NSELIM + 1], W(T1), ALU.mult, ALU.add, RP, [bpw])
            ts(W(T1), PI(m_), csm[:, NSELRE:NSELRE + 1], None, ALU.mult, None, RP, [bwk])
            stt(E2[:, m_, :], PR(m_), csm[:, NSELIM:NSELIM + 1], W(T1), ALU.mult, ALU.add, RP, [bpw])
        B1 = sb.alloc([128, 64, 16], F32)
        B2 = sb.alloc([128, 64, 16], F32)
        m_alias = sb.mark()
        bbr = sb.alloc([128, 64, 16], F32)
        bbi = sb.alloc([128, 64, 16], F32)
        tq = sb.alloc([128, 64, 16], F32)
        bbb = Buf("bb")
        RB = [bwk, bpw, bprm, bbb, bcst]
        fre_b = W(FRE).unsqueeze(2).broadcast_to([128, 64, 16])
        fim_b = W(FIM).unsqueeze(2).broadcast_to([128, 64, 16])
        tt(bbr[:], bre[:], fre_b, ALU.mult, RB, [bbb])
        tt(tq[:], bim[:], fim_b, ALU.mult, RB, [bbb])
        tt(bbr[:], bbr[:], tq[:], ALU.subtract, RB, [bbb])
        tt(bbi[:], bim[:], fre_b, ALU.mult, RB, [bbb])
        tt(tq[:], bre[:], fim_b, ALU.mult, RB, [bbb])
        tt(bbi[:], bbi[:], tq[:], ALU.add, RB, [bbb])
        f2 = lambda t_: t_[:].rearrange("p g q -> p (g q)")
        ts(f2(tq), f2(bbr), csm[:, SELRE:SELRE + 1], None, ALU.mult, None, RB, [bbb])
        stt(f2(B1), f2(bbi), csm[:, SELIM:SELIM + 1], f2(tq), ALU.mult, ALU.add, RB, [bbb])
        ts(f2(tq), f2(bbi), csm[:, NSELRE:NSELRE + 1], None, ALU.mult, None, RB, [bbb])
        stt(f2(B2), f2(bbr), csm[:, SELIM:SELIM + 1], f2(tq), ALU.mult, ALU.add, RB, [bbb])

        sb.reset(m_alias)
        LM = sb.alloc([128, 8, 8, 16], F32)
        WTt = sb.alloc([128, 8, 8, 16], F32)
        RR = sb.alloc([128, 8, 8, 16], F32)
        GGb = sb.alloc([128, 8, 8, 16], BF16)
        tq2 = sb.alloc([128, 8, 16], F32)
        tq3 = sb.alloc([128, 8, 16], F32)
        bbig = Buf("big")
        Tg = sb.alloc([128, 8, 128], BF16)
        Wg = sb.alloc([128, 8, 128], BF16)
        MkB = [sb.alloc([128, 8, 9, 128], BF16), sb.at([128, 8, 9, 128], BF16, mk2_off)]
        bmk = [Buf("mk0"), Buf("mk1")]
        tmrot2 = Rot(sb, 2, [128, 128], F32, "tmM")
        bmat = Buf("mat")

        def build_Mk(bt_, g, k):
            gg = bt_ * 8 + g
            tmpM, btm = tmrot2.next()
            ts(tmpM[:], ident, dp[:, k, 0, gg:gg + 1], None, ALU.mult, None, [bpw, bcst], [btm])
            stt(MkB[bt_ % 2][:, g, k, :], pswap, dps[:, k, gg:gg + 1], tmpM[:], ALU.mult, ALU.add, [bpw, bcst, btm], [bmk[bt_ % 2]])
        uT = sb.alloc([128, S_LEN], BF16)
        buT = Buf("uT")
        yT = sb.alloc([128, S_LEN], F32)
        byT = Buf("yT")
        zT = sb.alloc([128, S_LEN], BF16)
        bzT = Buf("zT")
        gt1 = sb.alloc([128, S_LEN // 2], F32)
        bg1 = Buf("g1")
        Uf = sb.alloc([128, 8, 512], BF16)
        Xs = sb.alloc([128, 8, 512], BF16)
        Yf = sb.alloc([128, 8, 512], BF16)
        bUf = [Buf(f"uf{g}") for g in range(8)]
        bXs = [Buf(f"xs{g}") for g in range(8)]
        bYf = [Buf(f"yf{g}") for g in range(8)]
        RG = [bwk, bpw, bbb, bprm, bcst, bbig]
        for bt in range(8):
            g0 = bt * 8
            S.dma("sp", uT[:], UT[bt * 128:(bt + 1) * 128, :], writes=[buT])
            if bt == 0:
                for g in range(8):
                    for k in range(9):
                        build_Mk(0, g, k)
            for g in range(8):
                q4, half = (g // 2), (g % 2)
                selx, rows = selrows(q4)
                b = 4 + bank()
                mm(ps[b][:, :], [(selx[rows, (half * 8 + j) * 128:(half * 8 + j + 1) * 128], uT[rows, j::8]) for j in range(8)],
                   [bsel, buT], [pb[b]])
                act(Uf[:, g, :], ps[b][:, :], AF.Copy, [pb[b]], [bUf[g]])

            def bc(ap2):
                return ap2.unsqueeze(2).broadcast_to([128, 8, 16])
            for j in range(8):
                for (dst, m_, X1, X2, pr_, pi_) in (
                    (LM, -j, B1, B2, PR, PI), (WTt, 7 - j, B1, B2, PR, PI)):
                    tt(tq2[:], X1[:, g0:g0 + 8, :], bc(pr_(m_)[:, g0:g0 + 8]), ALU.mult, RG, [bbig])
                    tt(dst[:, :, j, :], X2[:, g0:g0 + 8, :], bc(pi_(m_)[:, g0:g0 + 8]), ALU.mult, RG, [bbig])
                    tt(dst[:, :, j, :], dst[:, :, j, :], tq2[:], ALU.add, RG, [bbig])
                for (dst, m_) in ((RR, j), (GGb, j + 1)):
                    tt(tq2[:], cre[:, g0:g0 + 8, :], bc(E1[:, m_, g0:g0 + 8]), ALU.mult, RG, [bbig])
                    tt(tq3[:], cim[:, g0:g0 + 8, :], bc(E2[:, m_, g0:g0 + 8]), ALU.mult, RG, [bbig])
                    tt(dst[:, :, j, :], tq3[:], tq2[:], ALU.add, RG, [bbig])
            for g in range(8):
                gg = g0 + g
                b = bank()
                mm(ps[b][:, 0:128], [(LM[:, g, :, :].rearrange("p j q -> p (j q)"), RR[:, g, :, :].rearrange("p i q -> p (i q)"))],
                   [bbig], [pb[b]])
                tt(Tg[:, g, :], ps[b][:, 0:128], maskT, ALU.mult, [pb[b], bcst], [bmat])
                b = bank()
                S.op("pe", lambda e, b=b, g=g: e.transpose(ps[b][:, 0:128], WTt[:, g, :, :].rearrange("p j q -> p (j q)"), ident),
                     [bbig, bcst], [pb[b]])
                act(Wg[:, g, :], ps[b][:, 0:128], AF.Copy, [pb[b]], [bmat])
            for g in range(8):
                b = 4 + bank()
                mm(ps[b][:, :], [(Wg[:, g, :], Uf[:, g, :])], [bmat, bUf[g]], [pb[b]])
                act(Xs[:, g, :], ps[b][:, :], AF.Copy, [pb[b]], [bXs[g]])
            for k in range(9):
                sh = 1 << k
                for g in range(8):
                    mm(ps[g][:, sh:512], [(MkB[bt % 2][:, g, k, :], Xs[:, g, 0:512 - sh])], [bmk[bt % 2], bXs[g]], [pb[g]])
                for g in range(8):
                    tt(Xs[:, g, sh:512], Xs[:, g, sh:512], ps[g][:, sh:512], ALU.add, [pb[g], bXs[g]], [bXs[g]])
                if bt + 1 < 8:
                    for g in range(8):
                        build_Mk(bt + 1, g, k)
            for g in range(8):
                b = 4 + bank()
                mm(ps[b][:, :], [(Tg[:, g, :], Uf[:, g, :])], [bmat, bUf[g]], [pb[b]], start=True, stop=False)
                mm(ps[b][:, 1:512], [(GGb[:, g, :, :].rearrange("p i q -> p (i q)"), Xs[:, g, 0:511])], [bbig, bXs[g]], [pb[b]],
                   start=False, stop=True)
                act(Yf[:, g, :], ps[b][:, :], AF.Copy, [pb[b]], [bYf[g]])
            for i in range(8):
                q4, half = (i // 2), (i % 2)
                selx, rows = selrows(q4)
                b = 4 + bank()
                mm(ps[b][:, :], [(selx[rows, (half * 8 + g) * 128:(half * 8 + g + 1) * 128], Yf[rows, g, :]) for g in range(8)],
                   [bsel] + bYf, [pb[b]])
                stt(yT[:, i::8], uT[:, i::8], vecs[:, V_SD + bt:V_SD + bt + 1], ps[b][:, :], ALU.mult, ALU.add,
                    [pb[b], buT, bvec], [byT])
            for hf in range(2):
                hs = slice(hf * (S_LEN // 2), (hf + 1) * (S_LEN // 2))
                act(gt1[:], yT[:, hs], AF.Square, [byT], [bg1])
                ts(gt1[:], gt1[:], 0.044715, 1.0, ALU.mult, ALU.add, [bg1], [bg1])
                tt(gt1[:], gt1[:], yT[:, hs], ALU.mult, [bg1, byT], [bg1])
                act(gt1[:], gt1[:], AF.Sigmoid, [bg1], [bg1], scale=1.5957691216057308)
                tt(zT[:, hs], gt1[:], yT[:, hs], ALU.mult, [bg1, byT], [bzT])
            S.dma("sp", ZT[bt * 128:(bt + 1) * 128, :], zT[:], reads=[bzT])
        S.barrier()
        sb.reset(m)

    def out_tail(l, t0, rhsT, brhs, W2d, kcn, wrot, ystage_rot, sqrot, byt):
        for n in range(16):
            wt, bwt = wrot.next()
            alias = getattr(wrot, "alias", {}).get(id(bwt), [])
            src_ = W2d[:, n * 128:n * 128 + 128].rearrange("(kc p) n -> p kc n", p=128)
            S.dma("pool", wt[:, 0:kcn, 0:128], src_, writes=[bwt] + list(alias))
            ys, bys = ystage_rot.next()
            for sub in range(2):
                b = bank()
                mm(ps[b][:, :], [(wt[:, kc, 0:128], rhsT[:, kc, sub * 512:(sub + 1) * 512]) for kc in range(kcn)],
                   [bwt] + brhs + list(alias), [pb[b]])
                act(ys[:, sub * 512:(sub + 1) * 512], ps[b][:, :], AF.Copy, [pb[b]], [bys])
                sq, bq = sqrot.next()
                act(sq[:], ps[b][:, :], AF.Square, [pb[b]], [bq])
                mm(ps[6 + sub][:, :], [(onesb[:], sq[:])], [bq, bcst], [pb[6 + sub]], start=(n == 0), stop=(n == 15))
            S.dma("sp", YT[n * 128:(n + 1) * 128, t0:t0 + TT], ys[:], reads=[bys], writes=[byt[n]])

    def phase_D(l):
        m = sb.mark()
        xsrc = xT_in if l == 0 else XT
        OTt = sb.alloc([128, 16, TT], BF16)
        ZTt = sb.alloc([128, 8, TT], BF16)
        bin_ = Buf("din")
        mg = sb.alloc([128, 16, TT], BF16)
        bmg = [Buf(f"mg{n}") for n in range(16)]
        wrot = Rot(sb, 3, [128, 16, 128], BF16, "w")
        w8rot = Rot(sb, 4, [128, 8, 128], BF16, "w8")
        grot = Rot(sb, 4, [128, TT], BF16, "g")
        trot = Rot(sb, 6, [128, 512], F32, "t")
        yrot = Rot(sb, 2, [128, TT], F32, "y")
        xrot = Rot(sb, 2, [128, TT], F32, "x")
        sqrot = Rot(sb, 3, [128, 512], BF16, "sq")
        rstd = sb.alloc([128, TT], F32)
        brstd = Buf("rstd")
        byt = [Buf(f"yt{n}") for n in range(16)]
        pending = []

        def load_in(t0_):
            S.dma("sp", OTt[:], OT[:, t0_:t0_ + TT].rearrange("(kc p) t -> p kc t", p=128), writes=[bin_])
            S.dma("sp", ZTt[:], ZT[:, t0_:t0_ + TT].rearrange("(kc p) t -> p kc t", p=128), writes=[bin_])
        load_in(0)
        for tti in range(NT):
            t0 = tti * TT
            for n in range(16):
                wo, bwo = wrot.next()
                wload(wo, bwo, w_o[l], n * 128, 128, 16)
                w1, bw1 = w8rot.next()
                wload(w1, bw1, w_glu[l], n * 128, 128, 8)
                w2, bw2 = w8rot.next()
                wload(w2, bw2, w_glu[l], D + n * 128, 128, 8)
                ga, bga = grot.next()
                gb, bgb = grot.next()
                S.dma("sp", ga[:], GT[n * 128:(n + 1) * 128, t0:t0 + TT], writes=[bga])
                S.dma("sp", gb[:], GT[D + n * 128:D + (n + 1) * 128, t0:t0 + TT], writes=[bgb])
                for sub in range(2):
                    cs_ = slice(sub * 512, (sub + 1) * 512)
                    base = 3 * (bank_i[0] % 2)
                    bank_i[0] += 1
                    ba_, b1_, b2_ = base, base + 1, base + 2
                    mm(ps[ba_][:, :], [(wo[:, kc, :], OTt[:, kc, cs_]) for kc in range(16)], [bwo, bin_], [pb[ba_]])
                    mm(ps[b1_][:, :], [(w1[:, kc, :], ZTt[:, kc, cs_]) for kc in range(8)], [bw1, bin_], [pb[b1_]])
                    mm(ps[b2_][:, :], [(w2[:, kc, :], ZTt[:, kc, cs_]) for kc in range(8)], [bw2, bin_], [pb[b2_]])
                    sg_, bsg = trot.next()
                    act(sg_[:], ps[b2_][:, :], AF.Sigmoid, [pb[b2_], bvec], [bsg], bias=vecs[:, V_BGLU + 16 + n:V_BGLU + 17 + n])
                    so, bso = trot.next()
                    stt(so[:], ps[b1_][:, :], vecs[:, V_BGLU + n:V_BGLU + n + 1], sg_[:], ALU.add, ALU.mult, [pb[b1_], bsg, bvec], [bso])
                    ta, bta = trot.next()
                    tt(ta[:], ps[ba_][:, :], ga[:, cs_], ALU.mult, [pb[ba_], bga], [bta])
                    tt(so[:], so[:], gb[:, cs_], ALU.mult, [bso, bgb], [bso])
                    tt(mg[:, n, cs_], ta[:], so[:], ALU.add, [bta, bso], [bmg[n]])
                if pending:
                    pending.pop(0)()
            while pending:
                pending.pop(0)()
            if tti + 1 < NT:
                load_in(t0 + TT)
            out_tail(l, t0, mg, bmg, w_out[l], 16, wrot, yrot, sqrot, byt)
            rstd_from([6, 7], rstd, brstd, D)
            pending = postnorm_steps(t0, YT, byt, xsrc, XT, V_POSTMIX, rstd, brstd, yrot, xrot)
        while pending:
            pending.pop(0)()
        S.barrier()
        sb.reset(m)

    def phase_E(l, last):
        m = sb.mark()
        dst = outT if last else XT
        hT = sb.alloc([128, 16, TT], BF16)
        hT_off = sb.last_off
        bh = [Buf(f"h{k}") for k in range(16)]
        actT = sb.alloc([128, 44, TT], BF16)
        bact = [Buf(f"a{f}") for f in range(44)]
        wrot = Rot(sb, 4, [128, 16, 128], BF16, "w")
        wdrot = Rot.__new__(Rot)
        wdrot.items = [(sb.at([128, 44, 128], BF16, hT_off + i * 12288), Buf(f"wd{i}")) for i in range(2)]
        wdrot.i = 0
        wdrot.alias = {id(wdrot.items[i][1]): bh[6 * i:6 * i + 6] for i in range(2)}
        trot = Rot(sb, 3, [128, 512], F32, "t")
        yrot = Rot(sb, 2, [128, TT], F32, "y")
        xrot = Rot(sb, 2, [128, TT], F32, "x")
        sqrot = Rot(sb, 3, [128, 512], BF16, "sq")
        rstd = sb.alloc([128, TT], F32)
        brstd = Buf("rstd")
        byt = [Buf(f"yt{n}") for n in range(16)]
        rstd2 = sb.alloc([128, TT], F32)
        brstd2 = Buf("rstd2")
        pending = []
        for tti in range(NT):
            t0 = tti * TT
            make_hT(XT, t0, V_PREF, hT, bh, xrot, sqrot, rstd, brstd)
            for f in range(44):
                wg_, bwg = wrot.next()
                wload(wg_, bwg, w_fg[l], f * 128, 128, 16)
                wu_, bwu = wrot.next()
                wload(wu_, bwu, w_fu[l], f * 128, 128, 16)
                for sub in range(2):
                    cs_ = slice(sub * 512, (sub + 1) * 512)
                    base = 2 * (bank_i[0] % 3)
                    bank_i[0] += 1
                    bg_, bu_ = base, base + 1
                    mm(ps[bg_][:, :], [(wg_[:, kc, :], hT[:, kc, cs_]) for kc in range(16)], [bwg] + bh, [pb[bg_]])
                    mm(ps[bu_][:, :], [(wu_[:, kc, :], hT[:, kc, cs_]) for kc in range(16)], [bwu] + bh, [pb[bu_]])
                    sg_, bsg = trot.next()
                    act(sg_[:], ps[bg_][:, :], AF.Silu, [pb[bg_]], [bsg])
                    tt(actT[:, f, cs_], ps[bu_][:, :], sg_[:], ALU.mult, [pb[bu_], bsg], [bact[f]])
                if pending:
                    pending.pop(0)()
            while pending:
                pending.pop(0)()
            out_tail(l, t0, actT, bact, w_fd[l], 44, wdrot, yrot, sqrot, byt)
            rstd_from([6, 7], rstd2, brstd2, D)
            pending = postnorm_steps(t0, YT, byt, XT, dst, V_POSTF, rstd2, brstd2, yrot, xrot)
        while pending:
            pending.pop(0)()
        S.barrier()
        sb.reset(m)

    rope_tables()
    for l in range(n_layers):
        S.dma("sp", vecs[:], vecs_in[l], writes=[bvec])
        if "A" in phases:
            phase_A(l)
        if "B" in phases:
            phase_B(l)
        if "C" in phases:
            phase_C(l)
        if "D" in phases:
            phase_D(l)
        if "E" in phases:
            phase_E(l, last=(l == n_layers - 1))
    S.barrier()
    S.emit()
    return nc, S


def _chunkcols(v):
    v = np.asarray(v, dtype=np.float32)
    return np.ascontiguousarray(v.reshape(-1, 128).T)


def host_consts():
    c = np.zeros((128, NCONST), np.float32)
    r = np.arange(128)
    c[r, C_ID + r] = 1.0
    c[r, C_PSW + (r + 64) % 128] = 1.0
    jj = r // 16
    c[:, C_MASK:C_MASK + 128] = (jj[None, :] >= jj[:, None]).astype(np.float32)
    for half in range(2):
        for j in range(8):
            blk = (half * 8 + j) * 128
            for rr in range(128):
                hp, qp = (rr % 32) // 16, rr % 16
                if hp == half:
                    c[rr, C_SEL + blk + 16 * j + qp] = 1.0
    c[:, C_ONES:C_ONES + 128] = 1.0
    c[96:128, C_SEL2:C_SEL2 + 2048] = c[96:128, C_SEL:C_SEL + 2048]
    sm = C_SM
    c[:64, sm + 0] = 1.0
    c[64:, sm + 1] = 1.0
    c[:64, sm + 2] = -1.0
    c[64:, sm + 3] = -1.0
    sg = np.where((r % 64) < 32, -1.0, 1.0)
    c[:, sm + 4] = sg
    inv = (10000.0 ** (-(np.arange(0, 64, 2, dtype=np.float32)) / 64.0)).astype(np.float32)
    c[:, sm + 5] = inv[r % 32]
    return c


def host_prepare(inp):
    Ld = L_DEPTH
    vecs = np.zeros((Ld, 128, NVEC), np.float32)
    s5p = np.zeros((Ld, 128, NS5P), np.float32)
    for l in range(Ld):
        vecs[l, :, V_PRE:V_PRE + 16] = _chunkcols(inp["pre_mix_norm"][l])
        vecs[l, :, V_QN:V_QN + 4] = _chunkcols(inp["q_norm"][l])
        vecs[l, :, V_KVN:V_KVN + 2] = _chunkcols(inp["kv_norm"][l])
        vecs[l, :, V_BG:V_BG + 32] = _chunkcols(inp["b_gate"][l])
        vecs[l, :, V_BGLU:V_BGLU + 32] = _chunkcols(inp["b_glu"][l])
        vecs[l, :, V_POSTMIX:V_POSTMIX + 16] = _chunkcols(inp["post_mix_norm"][l])
        vecs[l, :, V_PREF:V_PREF + 16] = _chunkcols(inp["pre_ffn_norm"][l])
        vecs[l, :, V_POSTF:V_POSTF + 16] = _chunkcols(inp["post_ffn_norm"][l])
        vecs[l, :, V_SD:V_SD + 8] = _chunkcols(inp["ssm_d"][l])
        arT = np.asarray(inp["ssm_a_re"][l], np.float32).T
        aiT = np.asarray(inp["ssm_a_im"][l], np.float32).T
        s5p[l, :, P_AR:P_AR + 64] = np.concatenate([arT, arT], 0)
        s5p[l, :, P_AI:P_AI + 64] = np.concatenate([aiT, aiT], 0)
        s5p[l, :, P_LDT:P_LDT + 64] = np.broadcast_to(np.asarray(inp["ssm_log_dt"][l], np.float32)[None, :], (128, 64))
        for key, off, perm in (("ssm_b_re", P_BRE, (1, 0, 2)), ("ssm_b_im", P_BIM, (1, 0, 2)),
                               ("ssm_c_re", P_CRE, (2, 0, 1)), ("ssm_c_im", P_CIM, (2, 0, 1))):
            a = np.transpose(np.asarray(inp[key][l], np.float32), perm).reshape(64, 1024)
            s5p[l, :, off:off + 1024] = np.concatenate([a, a], 0)
    return vecs, s5p


_CACHE = {}
LAYERS_PER_LAUNCH = 4


def kernel(**inputs):
    inp = {k: np.asarray(v) for k, v in inputs.items()}
    x = inp["x"]
    B = x.shape[0]
    vecs, s5p = host_prepare(inp)
    consts = host_consts()
    npl = LAYERS_PER_LAUNCH
    if npl not in _CACHE:
        _CACHE[npl] = build(n_layers=npl)[0]
    nc = _CACHE[npl]
    wnames = ["w_in", "w_uq", "w_ukv", "w_o_mla", "w_glu", "w_out", "w_ffn_gate", "w_ffn_up", "w_ffn_down"]
    xT = [np.ascontiguousarray(x[b].T, dtype=np.float32) for b in range(B)]
    for l0 in range(0, L_DEPTH, npl):
        shared = {k: np.ascontiguousarray(inp[k][l0:l0 + npl], dtype=np.float32) for k in wnames}
        shared["vecs"] = np.ascontiguousarray(vecs[l0:l0 + npl])
        shared["s5p"] = np.ascontiguousarray(s5p[l0:l0 + npl])
        shared["consts"] = consts
        in_maps = []
        for b in range(B):
            mp = dict(shared)
            mp["xT"] = xT[b]
            mp["pos"] = np.ascontiguousarray(inp["positions"][b], dtype=np.int32)
            in_maps.append(mp)
        res = run_bass_kernel_spmd(nc, in_maps, core_ids=list(range(B)))
        xT = [np.ascontiguousarray(res.results[b]["outT"], dtype=np.float32) for b in range(B)]
    out = np.stack([np.ascontiguousarray(xT[b].T) for b in range(B)], axis=0)
    return out.astype(np.float32)
```
